# Optimizing a Trainium2 kernel written in Bass

```python
import jax, jax.numpy as jnp
from jax import lax
import numpy as np

D_MODEL = 1024
BATCH = 8
SEQ = 4096
DEPTH = 4

GRID_W = 64
CTX_LEN = 256
HEAD_DIM = 64
ATTN_HEADS = 8
KV_HEADS = 2
GQA_GROUP = ATTN_HEADS // KV_HEADS
ATTN_WIDTH = ATTN_HEADS * HEAD_DIM
KV_WIDTH = KV_HEADS * HEAD_DIM
MLP_HEADS = 8
MLP_WIDTH = MLP_HEADS * HEAD_DIM
CHUNK = 128
Q_BLOCK = 128
MIX_WIDTH = ATTN_WIDTH + MLP_WIDTH
IN_WIDTH = ATTN_WIDTH + 2 * KV_WIDTH + 2 * MLP_WIDTH
FFN_HIDDEN = -(-8 * D_MODEL // (3 * 256)) * 256
N_MOD = 6
ROPE_THETA = 10000.0
ROPE_AXIS_DIM = HEAD_DIM // 2
EPS = 1e-6

kernel_name = "hybrid_gqa_gmlp_diffusion_trunk"


def rms_norm(x, g):
    xf = x.astype(jnp.float32)
    y = xf * lax.rsqrt(jnp.mean(xf * xf, axis=-1, keepdims=True) + EPS)
    return (y * g.astype(jnp.float32)).astype(x.dtype)


def layer_norm(x, g, b):
    xf = x.astype(jnp.float32)
    mu = jnp.mean(xf, axis=-1, keepdims=True)
    var = jnp.mean(jnp.square(xf - mu), axis=-1, keepdims=True)
    y = (xf - mu) * lax.rsqrt(var + EPS)
    return (y * g.astype(jnp.float32) + b.astype(jnp.float32)).astype(x.dtype)


def modulate(h, shift, scale):
    return h * (1 + scale) + shift


def axial_rope_tables(n):
    rows = n // GRID_W
    pos_row = jnp.broadcast_to(jnp.arange(rows, dtype=jnp.float32)[:, None], (rows, GRID_W)).reshape(-1)
    pos_col = jnp.broadcast_to(jnp.arange(GRID_W, dtype=jnp.float32)[None, :], (rows, GRID_W)).reshape(-1)
    inv = ROPE_THETA ** (-jnp.arange(0, ROPE_AXIS_DIM, 2, dtype=jnp.float32) / ROPE_AXIS_DIM)
    ang = jnp.concatenate([pos_row[:, None] * inv, pos_col[:, None] * inv], axis=-1)
    return jnp.cos(ang), jnp.sin(ang)


def apply_rope(x, cos, sin):
    b, n, h, d = x.shape
    half = ROPE_AXIS_DIM // 2
    xr = x.reshape(b, n, h, 2, 2, half)
    c = cos.reshape(n, 2, half)[None, :, None].astype(x.dtype)
    s = sin.reshape(n, 2, half)[None, :, None].astype(x.dtype)
    x1, x2 = xr[..., 0, :], xr[..., 1, :]
    out = jnp.stack([x1 * c - x2 * s, x1 * s + x2 * c], axis=-2)
    return out.reshape(b, n, h, d)


def split_proj(p):
    b, n = p.shape[:2]
    q, k, v, z = jnp.split(p, [ATTN_WIDTH, ATTN_WIDTH + KV_WIDTH, ATTN_WIDTH + 2 * KV_WIDTH], axis=-1)
    return (q.reshape(b, n, ATTN_HEADS, HEAD_DIM), k.reshape(b, n, KV_HEADS, HEAD_DIM),
            v.reshape(b, n, KV_HEADS, HEAD_DIM), z)


def latent_attention(q, k_all, v_all):
    b, n = q.shape[:2]
    nblk = n // Q_BLOCK
    scale = 1.0 / np.sqrt(HEAD_DIM)
    qb = q.reshape(b, nblk, Q_BLOCK, KV_HEADS, GQA_GROUP, HEAD_DIM).transpose(1, 0, 3, 4, 2, 5)
    kt = k_all.transpose(0, 2, 1, 3)
    vt = v_all.transpose(0, 2, 1, 3)

    def one_block(qblk):
        s = jnp.einsum("bkgqd,bksd->bkgqs", qblk, kt, preferred_element_type=jnp.float32) * scale
        p = jax.nn.softmax(s, axis=-1).astype(vt.dtype)
        return jnp.einsum("bkgqs,bksd->bkgqd", p, vt)

    ob = lax.map(one_block, qb)
    return ob.transpose(1, 0, 4, 2, 3, 5).reshape(b, n, ATTN_WIDTH)


def context_attention(q, k, v):
    b, L = q.shape[:2]
    scale = 1.0 / np.sqrt(HEAD_DIM)
    qg = q.reshape(b, L, KV_HEADS, GQA_GROUP, HEAD_DIM)
    s = jnp.einsum("blkgd,bmkd->bkglm", qg, k, preferred_element_type=jnp.float32) * scale
    p = jax.nn.softmax(s, axis=-1).astype(v.dtype)
    o = jnp.einsum("bkglm,bmkd->blkgd", p, v)
    return o.reshape(b, L, ATTN_WIDTH)


def chunk_spatial_gating(z, g_sg, b_sg, w_s, b_s):
    b, n = z.shape[:2]
    z = jax.nn.gelu(z)
    u, v = jnp.split(z, 2, axis=-1)
    v = layer_norm(v, g_sg, b_sg)
    v = v.reshape(b, n // CHUNK, CHUNK, MLP_HEADS, HEAD_DIM)
    s = jnp.einsum("hpq,bcqhd->bcphd", w_s, v) + b_s.T[None, None, :, :, None]
    return u * s.reshape(b, n, MLP_WIDTH)


def swiglu(h, w_ffn_in, w_ffn_out):
    gate, up = jnp.split(h @ w_ffn_in, 2, axis=-1)
    return (jax.nn.silu(gate) * up) @ w_ffn_out


def setup_inputs(seed: int = 0) -> dict:
    key = jax.random.key(seed)
    ks = jax.random.split(key, 20)
    f32 = jnp.float32
    nrm = lambda k, shape, s: jax.random.normal(k, shape, f32) * s
    return {
        "x": nrm(ks[0], (BATCH, SEQ, D_MODEL), 1.0),
        "c": nrm(ks[1], (BATCH, D_MODEL), 1.0),
        "ctx": nrm(ks[2], (BATCH, CTX_LEN, D_MODEL), 1.0),
        "c_ctx": nrm(ks[3], (D_MODEL,), 1.0),
        "w_mod": nrm(ks[4], (DEPTH, D_MODEL, N_MOD * D_MODEL), 0.3 * D_MODEL ** -0.5),
        "b_mod": nrm(ks[5], (DEPTH, N_MOD * D_MODEL), 0.02),
        "g_pre_mix": 1.0 + nrm(ks[6], (DEPTH, D_MODEL), 0.02),
        "g_post_mix": 1.0 + nrm(ks[7], (DEPTH, D_MODEL), 0.02),
        "g_pre_ffn": 1.0 + nrm(ks[8], (DEPTH, D_MODEL), 0.02),
        "g_post_ffn": 1.0 + nrm(ks[9], (DEPTH, D_MODEL), 0.02),
        "w_in": nrm(ks[10], (DEPTH, D_MODEL, IN_WIDTH), D_MODEL ** -0.5),
        "g_q": 1.0 + nrm(ks[11], (DEPTH, HEAD_DIM), 0.02),
        "g_k": 1.0 + nrm(ks[12], (DEPTH, HEAD_DIM), 0.02),
        "g_sg": 1.0 + nrm(ks[13], (DEPTH, MLP_WIDTH), 0.02),
        "b_sg": nrm(ks[14], (DEPTH, MLP_WIDTH), 0.02),
        "w_s": nrm(ks[15], (DEPTH, MLP_HEADS, CHUNK, CHUNK), CHUNK ** -0.5),
        "b_s": 1.0 + nrm(ks[16], (DEPTH, MLP_HEADS, CHUNK), 0.02),
        "w_out": nrm(ks[17], (DEPTH, MIX_WIDTH, D_MODEL), MIX_WIDTH ** -0.5),
        "w_ffn_in": nrm(ks[18], (DEPTH, D_MODEL, 2 * FFN_HIDDEN), D_MODEL ** -0.5),
        "w_ffn_out": nrm(ks[19], (DEPTH, FFN_HIDDEN, D_MODEL), FFN_HIDDEN ** -0.5),
    }


def reference(x, c, ctx, c_ctx, w_mod, b_mod, g_pre_mix, g_post_mix, g_pre_ffn, g_post_ffn,
              w_in, g_q, g_k, g_sg, b_sg, w_s, b_s, w_out, w_ffn_in, w_ffn_out):
    b, n = x.shape[:2]
    cos, sin = axial_rope_tables(n)
    silu_c = jax.nn.silu(c)
    silu_cc = jax.nn.silu(c_ctx)
    xc = ctx
    for l in range(DEPTH):
        last = l == DEPTH - 1
        m_x = (silu_c @ w_mod[l] + b_mod[l]).reshape(b, N_MOD, D_MODEL)
        mx = [m_x[:, i, None, :] for i in range(N_MOD)]
        m_c = (silu_cc @ w_mod[l] + b_mod[l]).reshape(N_MOD, D_MODEL)
        mc = [m_c[i] for i in range(N_MOD)]

        hx = modulate(rms_norm(x, g_pre_mix[l]), mx[0], mx[1])
        hc = modulate(rms_norm(xc, g_pre_mix[l]), mc[0], mc[1])
        qx, kx, vx, zx = split_proj(hx @ w_in[l])
        qx = apply_rope(rms_norm(qx, g_q[l]), cos, sin)
        kx = apply_rope(rms_norm(kx, g_k[l]), cos, sin)
        if last:
            kv_c = hc @ w_in[l][:, ATTN_WIDTH:ATTN_WIDTH + 2 * KV_WIDTH]
            kc, vc = jnp.split(kv_c, 2, axis=-1)
            kc = kc.reshape(b, -1, KV_HEADS, HEAD_DIM)
            vc = vc.reshape(b, -1, KV_HEADS, HEAD_DIM)
        else:
            qc, kc, vc, zc = split_proj(hc @ w_in[l])
            qc = rms_norm(qc, g_q[l])
        kc = rms_norm(kc, g_k[l])
        k_all = jnp.concatenate([kc, kx], axis=1)
        v_all = jnp.concatenate([vc, vx], axis=1)
        attn_x = latent_attention(qx, k_all, v_all)
        mlp_x = chunk_spatial_gating(zx, g_sg[l], b_sg[l], w_s[l], b_s[l])
        out_x = jnp.concatenate([attn_x, mlp_x], axis=-1) @ w_out[l]
        x = x + mx[2] * rms_norm(out_x, g_post_mix[l])
        if not last:
            attn_c = context_attention(qc, kc, vc)
            mlp_c = chunk_spatial_gating(zc, g_sg[l], b_sg[l], w_s[l], b_s[l])
            out_c = jnp.concatenate([attn_c, mlp_c], axis=-1) @ w_out[l]
            xc = xc + mc[2] * rms_norm(out_c, g_post_mix[l])

        fx = swiglu(modulate(rms_norm(x, g_pre_ffn[l]), mx[3], mx[4]), w_ffn_in[l], w_ffn_out[l])
        x = x + mx[5] * rms_norm(fx, g_post_ffn[l])
        if not last:
            fc = swiglu(modulate(rms_norm(xc, g_pre_ffn[l]), mc[3], mc[4]), w_ffn_in[l], w_ffn_out[l])
            xc = xc + mc[5] * rms_norm(fc, g_post_ffn[l])
    return x
```

```python
import os
import numpy as np
import concourse.bass as bass
import concourse.mybir as mybir
from concourse.bass_utils import run_bass_kernel_spmd

F32 = mybir.dt.float32
BF16 = mybir.dt.bfloat16
AF = mybir.ActivationFunctionType
ALU = mybir.AluOpType
AX = mybir.AxisListType

D = 1024
SEQ = 4096
CTX = 256
NTOK = SEQ + CTX
NT_TILES = NTOK // 128
DEPTH = 4
HID = 2816
NJ = HID // 128
INW = 1792
EPS = 1e-6
QPERM = [0, 4, 1, 5, 2, 6, 3, 7]


class Res:
    __slots__ = ("w", "r", "name", "sem", "excl")

    def __init__(self, name=""):
        self.w = None
        self.r = []
        self.name = name
        self.sem = None
        self.excl = False


class Buf(Res):
    __slots__ = ("ap",)

    def __init__(self, ap, name=""):
        Res.__init__(self, name)
        self.ap = ap


class Prog:
    ENGS = ("pe", "act", "dve", "pool", "sp")

    def __init__(self, nc):
        self.nc = nc
        self.streams = {e: [] for e in self.ENGS}
        self.sems = {}
        self.cnt = {}
        self.cur = {}
        self.epoch = 0
        self.dma_keys = {}
        self._new_epoch()
        self.waited = {e: {} for e in self.ENGS}
        self.ndma = 0
        self.nops = 0

    def _new_epoch(self):
        for e in self.ENGS:
            key = "%s#%d" % (e, self.epoch)
            self.sems[key] = self.nc.alloc_semaphore("s_%s_%d" % (e, self.epoch))
            self.cnt[key] = 0
            self.cur[e] = key
        self.epoch += 1

    def _need(self, eng, deps):
        best = {}
        for (k, v) in deps:
            if v > best.get(k, 0):
                best[k] = v
        out = []
        w = self.waited[eng]
        for k, v in best.items():
            if eng == "pe" and k.startswith("pe#"):
                continue
            if w.get(k, 0) < v:
                w[k] = v
                out.append((k, v))
        return out

    @staticmethod
    def _collect(reads, writes):
        deps = []
        for r in reads:
            if r.w is not None:
                deps.append(r.w)
        for wr in writes:
            if wr.w is not None:
                deps.append(wr.w)
            deps.extend(wr.r)
        return deps

    def _mark(self, me, reads, writes):
        for r in reads:
            r.r.append(me)
            if len(r.r) > 64:
                best = {}
                for (k, v) in r.r:
                    if v > best.get(k, 0):
                        best[k] = v
                r.r = list(best.items())
        for wr in writes:
            wr.w = me
            wr.r = []

    def op(self, eng, method, *args, reads=(), writes=(), **kw):
        ex = [r for r in reads if r.excl and r not in writes]
        if ex:
            writes = list(writes) + ex
        waits = self._need(eng, self._collect(reads, writes))
        key = self.cur[eng]
        self.cnt[key] += 1
        me = (key, self.cnt[key])
        self.streams[eng].append((waits, (method, args, kw), (key, 1)))
        self._mark(me, reads, writes)
        self.nops += 1
        return me

    def dma_sem_for(self, res):
        key = "dma:" + res.name
        if key not in self.sems:
            self.ndma += 1
            self.sems[key] = self.nc.alloc_semaphore("d_%d" % self.ndma)
            self.cnt[key] = 0
        return key

    def dma(self, queue, semres, out, in_, reads=(), writes=()):
        semkey = self.dma_sem_for(semres)
        waits = self._need(queue, self._collect(reads, writes))
        self.cnt[semkey] += 16
        me = (semkey, self.cnt[semkey])
        self.streams[queue].append((waits, ("dma_start", (), dict(out=out, in_=in_)), (semkey, 16)))
        self._mark(me, reads, writes)
        self.nops += 1
        return me

    def barrier(self, new_epoch=True):
        allv = [(k, v) for k, v in self.cnt.items() if v > 0]
        for e in self.ENGS:
            waits = self._need(e, allv)
            if waits:
                self.streams[e].append((waits, None, None))
        if new_epoch:
            self._new_epoch()

    def emit(self):
        nc = self.nc
        sems = self.sems
        streams = self.streams

        waited = {}
        for ename in self.ENGS:
            for (waits, fn, inc) in streams[ename]:
                for (k, v) in waits:
                    waited.setdefault(k, set()).add(v)
        remap = {}
        for k, vals in waited.items():
            if k.startswith("dma:"):
                continue
            remap[k] = {v: i + 1 for i, v in enumerate(sorted(vals))}
        seen = {}

        def run(e, ename):
            for (waits, fn, inc) in streams[ename]:
                for (k, v) in waits:
                    if k in remap:
                        e.wait_ge(sems[k], remap[k][v])
                    else:
                        e.wait_ge(sems[k], v)
                if fn is not None:
                    ins = getattr(e, fn[0])(*fn[1], **fn[2])
                    k = inc[0]
                    if k.startswith("dma:"):
                        ins.then_inc(sems[k], inc[1])
                    else:
                        seen[k] = seen.get(k, 0) + 1
                        if seen[k] in remap.get(k, ()):
                            ins.then_inc(sems[k], 1)

        with nc.Block() as block:
            @block.tensor
            def _(e):
                run(e, "pe")

            @block.scalar
            def _(e):
                run(e, "act")

            @block.vector
            def _(e):
                run(e, "dve")

            @block.gpsimd
            def _(e):
                run(e, "pool")

            @block.sync
            def _(e):
                run(e, "sp")


class Arena:
    def __init__(self, ap_bf16):
        self.ap = ap_bf16
        self.off = 0
        self.size = ap_bf16.shape[1]

    def reset(self):
        self.off = 0

    def take(self, shape, dtype, name=""):
        n = 1
        for s in shape[1:]:
            n *= s
        nel = n * (2 if dtype == F32 else 1)
        if self.off % 2:
            self.off += 1
        assert self.off + nel <= self.size, ("arena overflow", name, self.off, nel, self.size)
        v = self.ap[:, self.off:self.off + nel]
        self.off += nel
        if dtype == F32:
            v = v.bitcast(F32)
        if len(shape) == 3:
            v = v.rearrange("p (a b) -> p a b", a=shape[1])
        elif len(shape) == 4:
            v = v.rearrange("p (a b c) -> p a b c", a=shape[1], b=shape[2])
        return Buf(v, name)


def build(n_layers=DEPTH, stop=99):
    nc = bass.Bass("TRN2", target_bir_lowering=False)
    p = Prog(nc)
    O = p.op
    DM = p.dma
    L = n_layers

    def din(name, shape):
        return nc.dram_tensor(name, list(shape), F32, kind="ExternalInput").ap()

    x_in = din("x", [SEQ, D])
    ctx_in = din("ctx", [CTX, D])
    c_in = din("c", [128, 8])
    cc_in = din("c_ctx", [128, 8])
    w_mod = din("w_mod", [L, 12, 128, 8 * 512])
    b_mod = din("b_mod", [L, 6 * D])
    g_pre_mix = din("g_pre_mix", [L, D])
    g_post_mix = din("g_post_mix", [L, D])
    g_pre_ffn = din("g_pre_ffn", [L, D])
    g_post_ffn = din("g_post_ffn", [L, D])
    w_in = din("w_in", [L, 128, 8 * INW])
    g_q = din("g_q", [L, 64])
    g_k = din("g_k", [L, 64])
    g_sg = din("g_sg", [L, 512])
    b_sg = din("b_sg", [L, 512])
    w_s = din("w_s", [L, 128, 8 * 128])
    b_s = din("b_s", [L, 128, 8])
    w_out = din("w_out", [L, 128, 8 * D])
    w_gu = din("w_gu", [L, NJ, 128, 8 * 256])
    w_o = din("w_o", [L, 128, NJ * D])
    ident_in = din("ident", [128, 128])
    cos_in = din("rope_cos", [128, 32 * 32])
    sin_in = din("rope_sin", [128, 32 * 32])
    y_out = nc.dram_tensor("y", [SEQ, D], F32, kind="ExternalOutput").ap()
    xs = nc.dram_tensor("xs", [NTOK, D], F32, kind="Internal").ap()
    mod_d = nc.dram_tensor("mod_d", [L * 2, 6 * D], F32, kind="Internal").ap()

    xs_res = [Res("xs%d" % t) for t in range(NT_TILES)]
    y_res = [Res("y%d" % t) for t in range(32)]
    modd_res = Res("mod_d")

    def sb(name, shape, dtype):
        return Buf(nc.alloc_sbuf_tensor(name, list(shape), dtype).ap(), name)

    idb = sb("idb", [128, 128], BF16)
    xa = [sb("xa%d" % i, [128, D], F32) for i in range(2)]
    xr = sb("xr0", [128, D], F32)
    sq_junk = sb("sq_junk", [128, D], BF16)
    hb = [sb("hb%d" % i, [128, D], BF16) for i in range(2)]
    hT = sb("hT", [128, 8, 512], BF16)
    modx = [sb("modx%d" % i, [128, D], F32) for i in range(3)]
    stat = [sb("stat%d" % i, [128, 16], F32) for i in range(4)]
    arena = Arena(nc.alloc_sbuf_tensor("arena", [128, 79 * 1024], BF16).ap())

    psT = nc.alloc_psum_tensor("psT", [128, 1024], BF16).ap()
    psA = nc.alloc_psum_tensor("psA", [128, 2048], F32).ap()
    psBs = [nc.alloc_psum_tensor("psB%d" % i, [128, 512], F32).ap() for i in range(3)]
    bT = Buf(psT, "bT")
    bA = [Buf(psA[:, i * 512:(i + 1) * 512], "bA%d" % i) for i in range(4)]
    bB = [Buf(psBs[i], "bB%d" % i) for i in range(3)]
    bC = bA[3]
    for b_ in [bT] + bA + bB:
        b_.excl = True

    cnt = {"xa": 0, "hb": 0, "stat": 0}

    def next_xa():
        b = xa[cnt["xa"] % 2]
        cnt["xa"] += 1
        return b

    def next_stat():
        b = stat[cnt["stat"] % 4]
        cnt["stat"] += 1
        return b

    def load_tile(dst, src_ap, src_res):
        DM("sp", dst, dst.ap, src_ap, reads=[src_res] if src_res else [], writes=[dst])

    def rsqrt_small(st, lo, hi, scale, eps):
        v = st.ap[:, lo:hi]
        O("dve", "tensor_scalar", v, v, scale, eps, op0=ALU.mult, op1=ALU.add, reads=[st], writes=[st])
        O("act", "activation", out=v, in_=v, func=AF.Sqrt, reads=[st], writes=[st])
        O("dve", "reciprocal", v, v, reads=[st], writes=[st])

    def prenorm_to_hT(xbuf, G1, SH, ncol0):
        st = next_stat()
        h = hb[cnt["hb"] % 2]
        cnt["hb"] += 1
        O("act", "activation", out=sq_junk.ap, in_=xbuf.ap, func=AF.Square, accum_out=st.ap[:, 0:1],
          reads=[xbuf], writes=[sq_junk, st])
        rsqrt_small(st, 0, 1, 1.0 / D, EPS)
        O("dve", "scalar_tensor_tensor", out=xbuf.ap, in0=xbuf.ap, scalar=st.ap[:, 0:1], in1=G1.ap,
          op0=ALU.mult, op1=ALU.mult, reads=[xbuf, st, G1], writes=[xbuf])
        O("pool", "tensor_tensor", h.ap, xbuf.ap, SH.ap, op=ALU.add, reads=[xbuf, SH], writes=[h])
        for k in range(8):
            O("pe", "transpose", bT.ap[:, k * 128:(k + 1) * 128], h.ap[:, k * 128:(k + 1) * 128], idb.ap,
              reads=[h, idb], writes=[bT])
        O("act", "copy", hT.ap[:, :, ncol0:ncol0 + 128], bT.ap.rearrange("p (k c) -> p k c", k=8),
          reads=[bT], writes=[hT])

    def load_mods(M, l, r, idx, gpre, gpost):
        row = l * 2 + r
        tmp = xr
        DM("sp", M[0], M[0].ap, mod_d[row, (idx + 1) * D:(idx + 2) * D].partition_broadcast(128), reads=[modd_res], writes=[M[0]])
        DM("sp", tmp, tmp.ap, gpre[l, :].partition_broadcast(128), writes=[tmp])
        O("dve", "scalar_tensor_tensor", out=M[0].ap, in0=M[0].ap, scalar=1.0, in1=tmp.ap, op0=ALU.add, op1=ALU.mult,
          reads=[M[0], tmp], writes=[M[0]])
        DM("sp", M[1], M[1].ap, mod_d[row, idx * D:(idx + 1) * D].partition_broadcast(128), reads=[modd_res], writes=[M[1]])
        DM("sp", M[2], M[2].ap, mod_d[row, (idx + 2) * D:(idx + 3) * D].partition_broadcast(128), reads=[modd_res], writes=[M[2]])
        DM("sp", tmp, tmp.ap, gpost[l, :].partition_broadcast(128), writes=[tmp])
        O("dve", "tensor_tensor", M[2].ap, M[2].ap, tmp.ap, op=ALU.mult, reads=[M[2], tmp], writes=[M[2]])

    def postnorm_store(pw_banks, pw_ap, GG, src_ap, src_res, dst_ap, dst_res):
        st = next_stat()
        xb = xr
        load_tile(xb, src_ap, src_res)
        O("act", "activation", out=sq_junk.ap, in_=pw_ap, func=AF.Square, accum_out=st.ap[:, 0:1],
          reads=pw_banks, writes=[sq_junk, st])
        rsqrt_small(st, 0, 1, 1.0 / D, EPS)
        o = next_xa()
        O("dve", "scalar_tensor_tensor", out=o.ap, in0=pw_ap, scalar=st.ap[:, 0:1], in1=GG.ap, op0=ALU.mult, op1=ALU.mult,
          reads=pw_banks + [st, GG], writes=[o])
        O("pool", "tensor_tensor", o.ap, o.ap, xb.ap, op=ALU.add, reads=[o, xb], writes=[o])
        DM("sp", o, dst_ap, o.ap, reads=[o], writes=[dst_res])

    def tile_src(l, t):
        if l == 0:
            if t < 2:
                return ctx_in[t * 128:(t + 1) * 128, :], None
            return x_in[(t - 2) * 128:(t - 1) * 128, :], None
        return xs[t * 128:(t + 1) * 128, :], xs_res[t]

    idf = xa[0]
    DM("sp", idf, idf.ap[:, 0:128], ident_in, writes=[idf])
    O("dve", "tensor_copy", idb.ap, idf.ap[:, 0:128], reads=[idf], writes=[idb])

    arena.reset()
    cst = arena.take([128, 8, 2], F32, "cst")
    craw = [arena.take([128, 8], F32, "craw%d" % i) for i in range(2)]
    ctmp = arena.take([128, 8], F32, "ctmp")
    stage = [arena.take([128, 8, 512], F32, "stage%d" % i) for i in range(2)]
    bmt = [arena.take([128, 512], F32, "bmt%d" % i) for i in range(2)]
    mrow = [arena.take([128, 512], F32, "mrow%d" % i) for i in range(2)]
    DM("sp", craw[0], craw[0].ap, c_in, writes=[craw[0]])
    DM("sp", craw[1], craw[1].ap, cc_in, writes=[craw[1]])
    for r in range(2):
        O("act", "activation", out=ctmp.ap, in_=craw[r].ap, func=AF.Tanh, scale=0.5, reads=[craw[r]], writes=[ctmp])
        O("dve", "scalar_tensor_tensor", out=ctmp.ap, in0=ctmp.ap, scalar=1.0, in1=craw[r].ap, op0=ALU.add, op1=ALU.mult,
          reads=[ctmp, craw[r]], writes=[ctmp])
        O("dve", "tensor_scalar", cst.ap[:, :, r], ctmp.ap, 0.5, None, op0=ALU.mult, reads=[ctmp], writes=[cst])
    it = 0
    for l in range(L):
        for j in range(12):
            sg = stage[it % 2]
            bm = bmt[it % 2]
            mr = mrow[it % 2]
            pb = bA[it % 2]
            DM("sp", sg, sg.ap, w_mod[l, j].rearrange("p (k n) -> p k n", k=8), writes=[sg])
            DM("sp", bm, bm.ap[0:2, :], b_mod[l, j * 512:(j + 1) * 512].partition_broadcast(2), writes=[bm])
            for k in range(8):
                O("pe", "matmul", pb.ap[0:2, :], lhsT=cst.ap[:, k, :], rhs=sg.ap[:, k, :], start=(k == 0), stop=(k == 7),
                  reads=[cst, sg], writes=[pb])
            O("dve", "tensor_tensor", mr.ap[0:2, :], pb.ap[0:2, :], bm.ap[0:2, :], op=ALU.add, reads=[pb, bm], writes=[mr])
            DM("sp", mr, mod_d[2 * l:2 * l + 2, j * 512:(j + 1) * 512], mr.ap[0:2, :], reads=[mr], writes=[modd_res])
            it += 1
    p.barrier()

    def finish():
        p.barrier(new_epoch=False)
        print("ops:", p.nops, "sems:", len(p.sems))
        p.emit()
        return nc

    if stop <= 1:
        return finish()

    for l in range(L):
        last = (l == L - 1)
        arena.reset()
        w_in_sb = arena.take([128, 8, INW], BF16, "w_in_sb")
        w_out_sb = arena.take([128, 8, D], BF16, "w_out_sb")
        wsT = arena.take([128, 8, 128], BF16, "wsT")
        bsB = arena.take([128, 8, 64], F32, "bsB")
        bsT = arena.take([128, 8], F32, "bsT")
        KT = arena.take([128, NTOK], BF16, "KT")
        Vaug = arena.take([128, NT_TILES, 2, 128], BF16, "Vaug")
        QT = arena.take([128, 4, 512], BF16, "QT")
        mixT = arena.take([128, 8, 512], BF16, "mixT")
        PT = [arena.take([128, 512], BF16, "PT%d" % i) for i in range(4)]
        rd = [arena.take([128, 512], F32, "rd%d" % i) for i in range(2)]
        gA = arena.take([128, 1024], F32, "gA")
        gB = arena.take([128, 1024], F32, "gB")
        ub = arena.take([128, 512], BF16, "ub")
        vln = arena.take([128, 512], BF16, "vln")
        mlpb = arena.take([128, 512], BF16, "mlpb")
        qf = arena.take([128, 512], F32, "qf")
        qt1 = arena.take([128, 256], F32, "qt1")
        qt2 = arena.take([128, 256], F32, "qt2")
        qb = arena.take([128, 512], BF16, "qb")
        kb = arena.take([128, 128], BF16, "kb")
        cosT = arena.take([128, 32, 32], F32, "cosT")
        sinT = arena.take([128, 32, 32], F32, "sinT")
        gqB = arena.take([128, 64], F32, "gqB")
        gkB = arena.take([128, 64], F32, "gkB")
        gsgB = arena.take([128, 512], F32, "gsgB")
        bsgB = arena.take([128, 512], F32, "bsgB")
        modc = [arena.take([128, D], F32, "modc%d" % i) for i in range(3)]

        DM("pool", w_in_sb, w_in_sb.ap, w_in[l].rearrange("p (k n) -> p k n", k=8), writes=[w_in_sb])
        DM("pool", w_out_sb, w_out_sb.ap, w_out[l].rearrange("p (k n) -> p k n", k=8), writes=[w_out_sb])
        DM("pool", wsT, wsT.ap, w_s[l].rearrange("p (h q) -> p h q", h=8), writes=[wsT])
        O("pool", "tensor_scalar", wsT.ap, wsT.ap, 0.5, None, op0=ALU.mult, reads=[wsT], writes=[wsT])
        if stop == 1.1:
            return finish()
        DM("sp", bsT, bsT.ap, b_s[l], writes=[bsT])
        O("pool", "tensor_scalar", bsB.ap, bsT.ap.unsqueeze(2).to_broadcast([128, 8, 64]), 0.5, None, op0=ALU.mult,
          reads=[bsT], writes=[bsB])
        DM("sp", cosT, cosT.ap, cos_in.rearrange("p (a b) -> p a b", a=32), writes=[cosT])
        DM("sp", sinT, sinT.ap, sin_in.rearrange("p (a b) -> p a b", a=32), writes=[sinT])
        DM("sp", gqB, gqB.ap, g_q[l, :].partition_broadcast(128), writes=[gqB])
        DM("sp", gkB, gkB.ap, g_k[l, :].partition_broadcast(128), writes=[gkB])
        DM("sp", gsgB, gsgB.ap, g_sg[l, :].partition_broadcast(128), writes=[gsgB])
        DM("sp", bsgB, bsgB.ap, b_sg[l, :].partition_broadcast(128), writes=[bsgB])
        if stop == 1.2:
            return finish()
        O("pool", "memset", Vaug.ap[:, :, 0, 64:128], 1.0, writes=[Vaug])
        O("pool", "memset", Vaug.ap[:, :, 1, 0:64], 1.0, writes=[Vaug])
        if stop == 1.3:
            return finish()
        load_mods(modc, l, 1, 0, g_pre_mix, g_post_mix)
        if stop == 1.4:
            return finish()
        load_mods(modx, l, 0, 0, g_pre_mix, g_post_mix)

        if stop <= 2:
            return finish()

        def rope(src_ap, nh, lt, dst_ap, rbufs, wbufs):
            s5 = src_ap.rearrange("p (h a b f) -> p h a b f", h=nh, a=2, b=2)
            d5 = dst_ap.rearrange("p (h a b f) -> p h a b f", h=nh, a=2, b=2)
            x1 = s5[:, :, :, 0, :]
            x2 = s5[:, :, :, 1, :]
            cb = cosT.ap[:, lt, :].rearrange("p (a f) -> p a f", a=2).unsqueeze(1).to_broadcast([128, nh, 2, 16])
            sbb = sinT.ap[:, lt, :].rearrange("p (a f) -> p a f", a=2).unsqueeze(1).to_broadcast([128, nh, 2, 16])
            n = nh * 32
            a1 = qt1.ap[:, 0:n].rearrange("p (h a f) -> p h a f", h=nh, a=2)
            a2 = qt2.ap[:, 0:n].rearrange("p (h a f) -> p h a f", h=nh, a=2)
            O("dve", "tensor_tensor", a1, x1, cb, op=ALU.mult, reads=rbufs + [cosT], writes=[qt1])
            O("dve", "tensor_tensor", a2, x2, sbb, op=ALU.mult, reads=rbufs + [sinT], writes=[qt2])
            O("dve", "tensor_tensor", d5[:, :, :, 0, :], a1, a2, op=ALU.subtract, reads=[qt1, qt2], writes=wbufs)
            O("dve", "tensor_tensor", a1, x1, sbb, op=ALU.mult, reads=rbufs + [sinT, qt1], writes=[qt1])
            O("dve", "tensor_tensor", a2, x2, cb, op=ALU.mult, reads=rbufs + [cosT, qt2], writes=[qt2])
            O("dve", "tensor_tensor", d5[:, :, :, 1, :], a1, a2, op=ALU.add, reads=[qt1, qt2], writes=wbufs)

        def head_norm(src_ap, src_bufs, nh, gB_, dstf):
            st = next_stat()
            n = nh * 64
            d = dstf.ap[:, 0:n]
            d3 = d.rearrange("p (h f) -> p h f", h=nh)
            O("act", "activation", out=d, in_=src_ap, func=AF.Square, reads=src_bufs, writes=[dstf])
            O("dve", "tensor_reduce", st.ap[:, 0:nh], d3, axis=AX.X, op=ALU.add, reads=[dstf], writes=[st])
            rsqrt_small(st, 0, nh, 1.0 / 64, EPS)
            O("dve", "tensor_tensor", d3, src_ap.rearrange("p (h f) -> p h f", h=nh),
              st.ap[:, 0:nh].unsqueeze(2).to_broadcast([128, nh, 64]), op=ALU.mult,
              reads=src_bufs + [st, dstf], writes=[dstf])
            O("dve", "tensor_tensor", d3, d3, gB_.ap.unsqueeze(1).to_broadcast([128, nh, 64]), op=ALU.mult,
              reads=[dstf, gB_], writes=[dstf])

        for t in range(NT_TILES):
            isctx = t < 2
            M = modc if isctx else modx
            xb = next_xa()
            sap, sres = tile_src(l, t)
            load_tile(xb, sap, sres)
            prenorm_to_hT(xb, M[0], M[1], 0)
            if stop == 2.1:
                return finish()
            for k in range(8):
                O("pe", "matmul", bC.ap[:, 0:256], lhsT=hT.ap[:, k, 0:128], rhs=w_in_sb.ap[:, k, 512:768],
                  start=(k == 0), stop=(k == 7), reads=[hT, w_in_sb], writes=[bC])
            if stop == 2.2:
                return finish()
            head_norm(bC.ap[:, 0:128], [bC], 2, gkB, qf)
            if stop == 2.3:
                return finish()
            if isctx:
                O("dve", "tensor_copy", kb.ap, qf.ap[:, 0:128], reads=[qf], writes=[kb])
            else:
                rope(qf.ap[:, 0:128], 2, t - 2, kb.ap, [qf], [kb])
            O("act", "copy", Vaug.ap[:, t, 0, 0:64], bC.ap[:, 128:192], reads=[bC], writes=[Vaug])
            O("act", "copy", Vaug.ap[:, t, 1, 64:128], bC.ap[:, 192:256], reads=[bC], writes=[Vaug])
            O("pe", "transpose", bT.ap[:, 0:128], kb.ap, idb.ap, reads=[kb, idb], writes=[bT])
            O("act", "copy", KT.ap[:, t * 128:(t + 1) * 128], bT.ap[:, 0:128], reads=[bT], writes=[KT])
            if stop == 2.4 and t == 0:
                return finish()
            if stop == 2.5 and t == int(os.environ.get("KTILE", "2")):
                return finish()

        if stop <= 3:
            return finish()
        groups = []
        if not last:
            groups.append((True, [0, 1]))
        for g in range(8):
            groups.append((False, [2 + 4 * g + i for i in range(4)]))
        for (isctx, tiles) in groups:
            M = modc if isctx else modx
            NTk = 128 * len(tiles)
            nk = 2 if isctx else NT_TILES
            for i, t in enumerate(tiles):
                xb = next_xa()
                sap, sres = tile_src(l, t)
                load_tile(xb, sap, sres)
                prenorm_to_hT(xb, M[0], M[1], i * 128)
                c0 = i * 128
                for (bank, col0) in ((bA[0], 0), (bA[1], 768), (bA[2], 1280)):
                    for k in range(8):
                        O("pe", "matmul", bank.ap, lhsT=hT.ap[:, k, c0:c0 + 128], rhs=w_in_sb.ap[:, k, col0:col0 + 512],
                          start=(k == 0), stop=(k == 7), reads=[hT, w_in_sb], writes=[bank])
                head_norm(bA[0].ap, [bA[0]], 8, gqB, qf)
                if isctx:
                    O("dve", "tensor_copy", qb.ap, qf.ap, reads=[qf], writes=[qb])
                else:
                    rope(qf.ap, 8, t - 2, qb.ap, [qf], [qb])
                for j in range(4):
                    O("pe", "transpose", bT.ap[:, j * 128:(j + 1) * 128], qb.ap[:, j * 128:(j + 1) * 128], idb.ap,
                      reads=[qb, idb], writes=[bT])
                O("act", "copy", QT.ap[:, :, c0:c0 + 128], bT.ap[:, 0:512].rearrange("p (j c) -> p j c", j=4),
                  reads=[bT], writes=[QT])
                z = psA[:, 512:1536]
                zb = [bA[1], bA[2]]
                O("act", "activation", out=gA.ap, in_=z, func=AF.Square, reads=zb, writes=[gA])
                O("dve", "tensor_scalar", gA.ap, gA.ap, 0.044715, 1.0, op0=ALU.mult, op1=ALU.add, reads=[gA], writes=[gA])
                O("dve", "tensor_tensor", gA.ap, gA.ap, z, op=ALU.mult, reads=[gA] + zb, writes=[gA])
                O("act", "activation", out=gB.ap, in_=gA.ap, func=AF.Tanh, scale=0.7978845608028654, reads=[gA], writes=[gB])
                O("dve", "scalar_tensor_tensor", out=ub.ap, in0=gB.ap[:, 0:512], scalar=1.0, in1=z[:, 0:512], op0=ALU.add, op1=ALU.mult,
                  reads=[gB, bA[1]], writes=[ub])
                vp = gA.ap[:, 512:1024]
                O("dve", "scalar_tensor_tensor", out=vp, in0=gB.ap[:, 512:1024], scalar=1.0, in1=z[:, 512:1024], op0=ALU.add, op1=ALU.mult,
                  reads=[gB, bA[2], gA], writes=[gA])
                st = next_stat()
                O("dve", "bn_stats", st.ap[:, 0:6], vp, reads=[gA], writes=[st])
                O("dve", "bn_aggr", st.ap[:, 8:10], st.ap[:, 0:6], reads=[st], writes=[st])
                rsqrt_small(st, 9, 10, 1.0, 4 * EPS)
                O("dve", "tensor_scalar", vp, vp, st.ap[:, 8:9], st.ap[:, 9:10], op0=ALU.subtract, op1=ALU.mult,
                  reads=[gA, st], writes=[gA])
                O("pool", "tensor_tensor", vp, vp, gsgB.ap, op=ALU.mult, reads=[gA, gsgB], writes=[gA])
                O("pool", "tensor_tensor", vln.ap, vp, bsgB.ap, op=ALU.add, reads=[gA, bsgB], writes=[vln])
                for h in range(8):
                    O("pe", "matmul", bB[0].ap[:, h * 64:(h + 1) * 64], lhsT=wsT.ap[:, h, :], rhs=vln.ap[:, h * 64:(h + 1) * 64],
                      start=True, stop=True, reads=[wsT, vln], writes=[bB[0]])
                O("dve", "tensor_tensor", gB.ap[:, 0:512], bB[0].ap, bsB.ap.rearrange("p h f -> p (h f)"), op=ALU.add,
                  reads=[bB[0], bsB], writes=[gB])
                O("pool", "tensor_tensor", mlpb.ap, gB.ap[:, 0:512], ub.ap, op=ALU.mult, reads=[gB, ub], writes=[mlpb])
                for j in range(4):
                    O("pe", "transpose", bT.ap[:, 512 + j * 128:512 + (j + 1) * 128], mlpb.ap[:, j * 128:(j + 1) * 128], idb.ap,
                      reads=[mlpb, idb], writes=[bT])
                O("act", "copy", mixT.ap[:, 4:8, c0:c0 + 128], bT.ap[:, 512:1024].rearrange("p (j c) -> p j c", j=4),
                  reads=[bT], writes=[mixT])

            sc_banks = [bB[0], bB[1], bB[2]]
            o_banks = [bC, bA[0]]
            hi = 0
            for j in range(4):
                for hh in range(2):
                    ob = o_banks[hi % 2]
                    rdb = rd[hi % 2]
                    hi += 1
                    r0 = hh * 64

                    def qk(kt):
                        sb_ = sc_banks[kt % 3]
                        O("pe", "matmul", sb_.ap[:, 0:NTk], lhsT=KT.ap[r0:r0 + 64, kt * 128:(kt + 1) * 128],
                          rhs=QT.ap[r0:r0 + 64, j, 0:NTk], start=True, stop=True, reads=[KT, QT], writes=[sb_])

                    def ex(kt):
                        sb_ = sc_banks[kt % 3]
                        pt = PT[kt % 4]
                        O("act", "activation", out=pt.ap[:, 0:NTk], in_=sb_.ap[:, 0:NTk], func=AF.Exp, scale=0.125,
                          reads=[sb_], writes=[pt])

                    def pv(kt):
                        pt = PT[kt % 4]
                        O("pe", "matmul", ob.ap[:, 0:NTk], lhsT=Vaug.ap[:, kt, hh, :], rhs=pt.ap[:, 0:NTk],
                          start=(kt == 0), stop=(kt == nk - 1), reads=[Vaug, pt], writes=[ob])

                    LA = 2
                    for kt in range(min(LA, nk)):
                        qk(kt)
                    for kt in range(nk):
                        ex(kt)
                        if kt + LA < nk:
                            qk(kt + LA)
                        pv(kt)
                    d0 = 64 - r0
                    O("dve", "reciprocal", rdb.ap[r0:r0 + 64, 0:NTk], ob.ap[d0:d0 + 64, 0:NTk], reads=[ob], writes=[rdb])
                    O("dve", "tensor_tensor", mixT.ap[r0:r0 + 64, j, 0:NTk], ob.ap[r0:r0 + 64, 0:NTk], rdb.ap[r0:r0 + 64, 0:NTk],
                      op=ALU.mult, reads=[ob, rdb], writes=[mixT])

            for i, t in enumerate(tiles):
                c0 = i * 128
                pw = [bA[2], bA[3]]
                for half in range(2):
                    for c in range(8):
                        O("pe", "matmul", pw[half].ap, lhsT=mixT.ap[:, c, c0:c0 + 128], rhs=w_out_sb.ap[:, c, half * 512:(half + 1) * 512],
                          start=(c == 0), stop=(c == 7), reads=[mixT, w_out_sb], writes=[pw[half]])
                sap, sres = tile_src(l, t)
                postnorm_store(pw, psA[:, 1024:2048], M[2], sap, sres, xs[t * 128:(t + 1) * 128, :], xs_res[t])

        if stop <= 4:
            return finish()
        p.barrier(new_epoch=False)
        arena.reset()
        wgu = [arena.take([128, 8, 256], BF16, "wgu%d" % j) for j in range(NJ)]
        wo = [arena.take([128, D], BF16, "wo%d" % j) for j in range(NJ)]
        actT = arena.take([128, NJ, 512], BF16, "actT")
        sgt = [arena.take([128, 512], F32, "sgt%d" % i) for i in range(2)]
        for j in range(NJ):
            DM("pool", wgu[j], wgu[j].ap, w_gu[l, j].rearrange("p (k n) -> p k n", k=8), writes=[wgu[j]])
        for j in range(NJ):
            DM("pool", wo[j], wo[j].ap, w_o[l][:, j * D:(j + 1) * D], writes=[wo[j]])
        cgroups = []
        if not last:
            cgroups.append((True, [0, 1]))
        for g in range(8):
            cgroups.append((False, [2 + 4 * g + i for i in range(4)]))
        cur_stream = None
        for (isctx, tiles) in cgroups:
            r = 1 if isctx else 0
            if cur_stream != r:
                load_mods(modx, l, r, 3, g_pre_ffn, g_post_ffn)
                cur_stream = r
            NTk = 128 * len(tiles)
            for i, t in enumerate(tiles):
                xb = next_xa()
                load_tile(xb, xs[t * 128:(t + 1) * 128, :], xs_res[t])
                prenorm_to_hT(xb, modx[0], modx[1], i * 128)
            gub = [(bA[0], bA[1]), (bB[0], bB[1])]
            for j in range(NJ):
                pg, pu = gub[j % 2]
                for k in range(8):
                    O("pe", "matmul", pg.ap[:, 0:NTk], lhsT=wgu[j].ap[:, k, 0:128], rhs=hT.ap[:, k, 0:NTk],
                      start=(k == 0), stop=(k == 7), reads=[wgu[j], hT], writes=[pg])
                for k in range(8):
                    O("pe", "matmul", pu.ap[:, 0:NTk], lhsT=wgu[j].ap[:, k, 128:256], rhs=hT.ap[:, k, 0:NTk],
                      start=(k == 0), stop=(k == 7), reads=[wgu[j], hT], writes=[pu])
                sg_ = sgt[j % 2]
                O("act", "activation", out=sg_.ap[:, 0:NTk], in_=pg.ap[:, 0:NTk], func=AF.Silu, reads=[pg], writes=[sg_])
                O("dve", "tensor_tensor", actT.ap[:, j, 0:NTk], sg_.ap[:, 0:NTk], pu.ap[:, 0:NTk], op=ALU.mult,
                  reads=[sg_, pu], writes=[actT])
            for i, t in enumerate(tiles):
                c0 = i * 128
                pw = [bA[2], bA[3]]
                for half in range(2):
                    for j in range(NJ):
                        O("pe", "matmul", pw[half].ap, lhsT=actT.ap[:, j, c0:c0 + 128], rhs=wo[j].ap[:, half * 512:(half + 1) * 512],
                          start=(j == 0), stop=(j == NJ - 1), reads=[actT, wo[j]], writes=[pw[half]])
                if last and not isctx:
                    dst_ap, dst_res = y_out[(t - 2) * 128:(t - 1) * 128, :], y_res[t - 2]
                else:
                    dst_ap, dst_res = xs[t * 128:(t + 1) * 128, :], xs_res[t]
                postnorm_store(pw, psA[:, 1024:2048], modx[2], xs[t * 128:(t + 1) * 128, :], xs_res[t], dst_ap, dst_res)
        p.barrier()

    return finish()


def _rope_tables():
    n = SEQ
    rows = n // 64
    pos_row = np.repeat(np.arange(rows, dtype=np.float32), 64)
    pos_col = np.tile(np.arange(64, dtype=np.float32), rows)
    inv = (10000.0 ** (-np.arange(0, 32, 2, dtype=np.float32) / 32)).astype(np.float32)
    ang = np.concatenate([pos_row[:, None] * inv, pos_col[:, None] * inv], axis=-1).astype(np.float32)
    cos = np.cos(ang).astype(np.float32)
    sin = np.sin(ang).astype(np.float32)
    cos = np.ascontiguousarray(cos.reshape(32, 128, 32).transpose(1, 0, 2)).reshape(128, 32 * 32)
    sin = np.ascontiguousarray(sin.reshape(32, 128, 32).transpose(1, 0, 2)).reshape(128, 32 * 32)
    return cos, sin


def prep_shared(inputs, L):
    f = lambda a: np.ascontiguousarray(np.asarray(a, dtype=np.float32))
    sh = {}
    wm = f(inputs["w_mod"])[:L]
    sh["w_mod"] = f(wm.reshape(L, 8, 128, 12, 512).transpose(0, 3, 2, 1, 4)).reshape(L, 12, 128, 8 * 512)
    sh["b_mod"] = f(inputs["b_mod"])[:L]
    for k in ("g_pre_mix", "g_post_mix", "g_pre_ffn", "g_post_ffn", "g_q", "g_k", "g_sg", "b_sg"):
        sh[k] = f(inputs[k])[:L]
    wi = f(inputs["w_in"])[:L]
    qcols = np.concatenate([np.arange(h * 64, (h + 1) * 64) for h in QPERM])
    cols = np.concatenate([qcols, np.arange(512, INW)])
    wi = wi[:, :, cols]
    sh["w_in"] = f(wi.reshape(L, 8, 128, INW).transpose(0, 2, 1, 3)).reshape(L, 128, 8 * INW)
    ws = f(inputs["w_s"])[:L]
    sh["w_s"] = f(ws.transpose(0, 3, 1, 2)).reshape(L, 128, 8 * 128)
    sh["b_s"] = f(f(inputs["b_s"])[:L].transpose(0, 2, 1))
    wo_ = f(inputs["w_out"])[:L]
    rows = np.concatenate([qcols, np.arange(512, 1024)])
    wo_ = wo_[:, rows, :]
    sh["w_out"] = f(wo_.reshape(L, 8, 128, D).transpose(0, 2, 1, 3)).reshape(L, 128, 8 * D)
    wf = f(inputs["w_ffn_in"])[:L]
    gate = wf[:, :, :HID].reshape(L, 8, 128, NJ, 128)
    up = wf[:, :, HID:].reshape(L, 8, 128, NJ, 128)
    gu = np.concatenate([gate, up], axis=-1)
    sh["w_gu"] = f(gu.transpose(0, 3, 2, 1, 4)).reshape(L, NJ, 128, 8 * 256)
    wfo = f(inputs["w_ffn_out"])[:L]
    sh["w_o"] = f(wfo.reshape(L, NJ, 128, D).transpose(0, 2, 1, 3)).reshape(L, 128, NJ * D)
    sh["ident"] = np.eye(128, dtype=np.float32)
    cos, sin = _rope_tables()
    sh["rope_cos"] = cos
    sh["rope_sin"] = sin
    sh["c_ctx"] = f(f(inputs["c_ctx"]).reshape(8, 128).T)
    return sh


def prep_core(inputs, b):
    f = lambda a: np.ascontiguousarray(np.asarray(a, dtype=np.float32))
    return {
        "x": f(inputs["x"][b]),
        "ctx": f(inputs["ctx"][b]),
        "c": f(f(inputs["c"][b]).reshape(8, 128).T),
    }


_NC_CACHE = {}


def run(inputs, n_layers=DEPTH, cores=None, stop=99):
    if cores is None:
        cores = list(range(8))
    if (n_layers, stop) not in _NC_CACHE:
        _NC_CACHE[(n_layers, stop)] = build(n_layers, stop)
    nc = _NC_CACHE[(n_layers, stop)]
    sh = prep_shared(inputs, n_layers)
    in_maps = []
    for b in cores:
        m = dict(sh)
        m.update(prep_core(inputs, b))
        in_maps.append(m)
    res = run_bass_kernel_spmd(nc, in_maps, core_ids=list(range(len(cores))))
    return np.stack([np.asarray(r["y"], dtype=np.float32) for r in res.results], axis=0)


def kernel(**inputs):
    return run(inputs, DEPTH)
```

```python
import os
import numpy as np
import concourse.bass as bass
import concourse.mybir as mybir
from concourse.bass_utils import run_bass_kernel_spmd

F32 = mybir.dt.float32
BF16 = mybir.dt.bfloat16
AF = mybir.ActivationFunctionType
ALU = mybir.AluOpType
AX = mybir.AxisListType

D = 1024
SEQ = 4096
CTX = 256
NTOK = SEQ + CTX
NT_TILES = NTOK // 128
DEPTH = 4
HID = 2816
NJ = HID // 128
INW = 1792
EPS = 1e-6
QPERM = [0, 4, 1, 5, 2, 6, 3, 7]


class Res:
    __slots__ = ("w", "r", "name", "sem", "excl")

    def __init__(self, name=""):
        self.w = None
        self.r = []
        self.name = name
        self.sem = None
        self.excl = False


class Buf(Res):
    __slots__ = ("ap",)

    def __init__(self, ap, name=""):
        Res.__init__(self, name)
        self.ap = ap


class Prog:
    ENGS = ("pe", "act", "dve", "pool", "sp")

    def __init__(self, nc):
        self.nc = nc
        self.streams = {e: [] for e in self.ENGS}
        self.sems = {}
        self.cnt = {}
        self.cur = {}
        self.epoch = 0
        self.dma_keys = {}
        self._new_epoch()
        self.waited = {e: {} for e in self.ENGS}
        self.ndma = 0
        self.nops = 0

    def _new_epoch(self):
        for e in self.ENGS:
            key = "%s#%d" % (e, self.epoch)
            self.sems[key] = self.nc.alloc_semaphore("s_%s_%d" % (e, self.epoch))
            self.cnt[key] = 0
            self.cur[e] = key
        self.epoch += 1

    def _need(self, eng, deps):
        best = {}
        for (k, v) in deps:
            if v > best.get(k, 0):
                best[k] = v
        out = []
        w = self.waited[eng]
        for k, v in best.items():
            if eng == "pe" and k.startswith("pe#"):
                continue
            if w.get(k, 0) < v:
                w[k] = v
                out.append((k, v))
        return out

    @staticmethod
    def _collect(reads, writes):
        deps = []
        for r in reads:
            if r.w is not None:
                deps.append(r.w)
        for wr in writes:
            if wr.w is not None:
                deps.append(wr.w)
            deps.extend(wr.r)
        return deps

    def _mark(self, me, reads, writes):
        for r in reads:
            r.r.append(me)
            if len(r.r) > 64:
                best = {}
                for (k, v) in r.r:
                    if v > best.get(k, 0):
                        best[k] = v
                r.r = list(best.items())
        for wr in writes:
            wr.w = me
            wr.r = []

    def op(self, eng, method, *args, reads=(), writes=(), **kw):
        ex = [r for r in reads if r.excl and r not in writes]
        if ex:
            writes = list(writes) + ex
        waits = self._need(eng, self._collect(reads, writes))
        key = self.cur[eng]
        self.cnt[key] += 1
        me = (key, self.cnt[key])
        self.streams[eng].append((waits, (method, args, kw), (key, 1)))
        self._mark(me, reads, writes)
        self.nops += 1
        return me

    def dma_sem_for(self, res):
        key = "dma:" + res.name
        if key not in self.sems:
            self.ndma += 1
            self.sems[key] = self.nc.alloc_semaphore("d_%d" % self.ndma)
            self.cnt[key] = 0
        return key

    def dma(self, queue, semres, out, in_, reads=(), writes=()):
        semkey = self.dma_sem_for(semres)
        waits = self._need(queue, self._collect(reads, writes))
        self.cnt[semkey] += 16
        me = (semkey, self.cnt[semkey])
        self.streams[queue].append((waits, ("dma_start", (), dict(out=out, in_=in_)), (semkey, 16)))
        self._mark(me, reads, writes)
        self.nops += 1
        return me

    def barrier(self, new_epoch=True):
        allv = [(k, v) for k, v in self.cnt.items() if v > 0]
        for e in self.ENGS:
            waits = self._need(e, allv)
            if waits:
                self.streams[e].append((waits, None, None))
        if new_epoch:
            self._new_epoch()

    def emit(self):
        nc = self.nc
        sems = self.sems
        streams = self.streams

        waited = {}
        for ename in self.ENGS:
            for (waits, fn, inc) in streams[ename]:
                for (k, v) in waits:
                    waited.setdefault(k, set()).add(v)
        remap = {}
        for k, vals in waited.items():
            if k.startswith("dma:"):
                continue
            remap[k] = {v: i + 1 for i, v in enumerate(sorted(vals))}
        seen = {}

        def run(e, ename):
            for (waits, fn, inc) in streams[ename]:
                for (k, v) in waits:
                    if k in remap:
                        e.wait_ge(sems[k], remap[k][v])
                    else:
                        e.wait_ge(sems[k], v)
                if fn is not None:
                    ins = getattr(e, fn[0])(*fn[1], **fn[2])
                    k = inc[0]
                    if k.startswith("dma:"):
                        ins.then_inc(sems[k], inc[1])
                    else:
                        seen[k] = seen.get(k, 0) + 1
                        if seen[k] in remap.get(k, ()):
                            ins.then_inc(sems[k], 1)

        with nc.Block() as block:
            @block.tensor
            def _(e):
                run(e, "pe")

            @block.scalar
            def _(e):
                run(e, "act")

            @block.vector
            def _(e):
                run(e, "dve")

            @block.gpsimd
            def _(e):
                run(e, "pool")

            @block.sync
            def _(e):
                run(e, "sp")


class Arena:
    def __init__(self, ap_bf16):
        self.ap = ap_bf16
        self.off = 0
        self.size = ap_bf16.shape[1]

    def reset(self):
        self.off = 0

    def take(self, shape, dtype, name=""):
        n = 1
        for s in shape[1:]:
            n *= s
        nel = n * (2 if dtype == F32 else 1)
        if self.off % 2:
            self.off += 1
        assert self.off + nel <= self.size, ("arena overflow", name, self.off, nel, self.size)
        v = self.ap[:, self.off:self.off + nel]
        self.off += nel
        if dtype == F32:
            v = v.bitcast(F32)
        if len(shape) == 3:
            v = v.rearrange("p (a b) -> p a b", a=shape[1])
        elif len(shape) == 4:
            v = v.rearrange("p (a b c) -> p a b c", a=shape[1], b=shape[2])
        return Buf(v, name)


def build(n_layers=DEPTH, stop=99):
    nc = bass.Bass("TRN2", target_bir_lowering=False)
    p = Prog(nc)
    O = p.op
    DM = p.dma
    L = n_layers

    def din(name, shape):
        return nc.dram_tensor(name, list(shape), F32, kind="ExternalInput").ap()

    x_in = din("x", [SEQ, D])
    ctx_in = din("ctx", [CTX, D])
    c_in = din("c", [128, 8])
    cc_in = din("c_ctx", [128, 8])
    w_mod = din("w_mod", [L, 12, 128, 8 * 512])
    b_mod = din("b_mod", [L, 6 * D])
    g_pre_mix = din("g_pre_mix", [L, D])
    g_post_mix = din("g_post_mix", [L, D])
    g_pre_ffn = din("g_pre_ffn", [L, D])
    g_post_ffn = din("g_post_ffn", [L, D])
    w_in = din("w_in", [L, 128, 8 * INW])
    g_q = din("g_q", [L, 64])
    g_k = din("g_k", [L, 64])
    g_sg = din("g_sg", [L, 512])
    b_sg = din("b_sg", [L, 512])
    w_s = din("w_s", [L, 128, 8 * 128])
    b_s = din("b_s", [L, 128, 8])
    w_out = din("w_out", [L, 128, 8 * D])
    w_gu = din("w_gu", [L, NJ, 128, 8 * 256])
    w_o = din("w_o", [L, 128, NJ * D])
    ident_in = din("ident", [128, 128])
    cos_in = din("rope_cos", [128, 32 * 32])
    sin_in = din("rope_sin", [128, 32 * 32])
    y_out = nc.dram_tensor("y", [SEQ, D], F32, kind="ExternalOutput").ap()
    xs = nc.dram_tensor("xs", [NTOK, D], F32, kind="Internal").ap()
    mod_d = nc.dram_tensor("mod_d", [L * 2, 6 * D], F32, kind="Internal").ap()

    xs_res = [Res("xs%d" % t) for t in range(NT_TILES)]
    y_res = [Res("y%d" % t) for t in range(32)]
    modd_res = Res("mod_d")

    def sb(name, shape, dtype):
        return Buf(nc.alloc_sbuf_tensor(name, list(shape), dtype).ap(), name)

    idb = sb("idb", [128, 128], BF16)
    xa = [sb("xa%d" % i, [128, D], F32) for i in range(2)]
    xr = sb("xr0", [128, D], F32)
    sq_junk = sb("sq_junk", [128, D], BF16)
    hb = [sb("hb%d" % i, [128, D], BF16) for i in range(2)]
    hT = sb("hT", [128, 8, 512], BF16)
    hT2 = sb("hT2", [128, 8, 512], BF16)
    modx = [sb("modx%d" % i, [128, D], F32) for i in range(3)]
    stat = [sb("stat%d" % i, [128, 16], F32) for i in range(4)]
    arena = Arena(nc.alloc_sbuf_tensor("arena", [128, 79 * 1024], BF16).ap())

    psT = nc.alloc_psum_tensor("psT", [128, 1024], BF16).ap()
    psA = nc.alloc_psum_tensor("psA", [128, 2048], F32).ap()
    psBs = [nc.alloc_psum_tensor("psB%d" % i, [128, 512], F32).ap() for i in range(3)]
    bT = Buf(psT, "bT")
    bA = [Buf(psA[:, i * 512:(i + 1) * 512], "bA%d" % i) for i in range(4)]
    bB = [Buf(psBs[i], "bB%d" % i) for i in range(3)]
    bC = bA[3]
    for b_ in [bT] + bA + bB:
        b_.excl = True

    cnt = {"xa": 0, "hb": 0, "stat": 0}

    def next_xa():
        b = xa[cnt["xa"] % 2]
        cnt["xa"] += 1
        return b

    def next_stat():
        b = stat[cnt["stat"] % 4]
        cnt["stat"] += 1
        return b

    def load_tile(dst, src_ap, src_res):
        DM("sp", dst, dst.ap, src_ap, reads=[src_res] if src_res else [], writes=[dst])

    def rsqrt_small(st, lo, hi, scale, eps):
        v = st.ap[:, lo:hi]
        O("dve", "tensor_scalar", v, v, scale, eps, op0=ALU.mult, op1=ALU.add, reads=[st], writes=[st])
        O("act", "activation", out=v, in_=v, func=AF.Sqrt, reads=[st], writes=[st])
        O("dve", "reciprocal", v, v, reads=[st], writes=[st])

    def prenorm_to_hT(xbuf, G1, SH, ncol0, hT=hT):
        st = next_stat()
        h = hb[cnt["hb"] % 2]
        cnt["hb"] += 1
        O("act", "activation", out=sq_junk.ap, in_=xbuf.ap, func=AF.Square, accum_out=st.ap[:, 0:1],
          reads=[xbuf], writes=[sq_junk, st])
        rsqrt_small(st, 0, 1, 1.0 / D, EPS)
        O("dve", "scalar_tensor_tensor", out=xbuf.ap, in0=xbuf.ap, scalar=st.ap[:, 0:1], in1=G1.ap,
          op0=ALU.mult, op1=ALU.mult, reads=[xbuf, st, G1], writes=[xbuf])
        O("pool", "tensor_tensor", h.ap, xbuf.ap, SH.ap, op=ALU.add, reads=[xbuf, SH], writes=[h])
        for k in range(8):
            O("pe", "transpose", bT.ap[:, k * 128:(k + 1) * 128], h.ap[:, k * 128:(k + 1) * 128], idb.ap,
              reads=[h, idb], writes=[bT])
        O("act", "copy", hT.ap[:, :, ncol0:ncol0 + 128], bT.ap.rearrange("p (k c) -> p k c", k=8),
          reads=[bT], writes=[hT])

    def load_mods(M, l, r, idx, gpre, gpost):
        row = l * 2 + r
        tmp = xr
        DM("sp", M[0], M[0].ap, mod_d[row, (idx + 1) * D:(idx + 2) * D].partition_broadcast(128), reads=[modd_res], writes=[M[0]])
        DM("sp", tmp, tmp.ap, gpre[l, :].partition_broadcast(128), writes=[tmp])
        O("dve", "scalar_tensor_tensor", out=M[0].ap, in0=M[0].ap, scalar=1.0, in1=tmp.ap, op0=ALU.add, op1=ALU.mult,
          reads=[M[0], tmp], writes=[M[0]])
        DM("sp", M[1], M[1].ap, mod_d[row, idx * D:(idx + 1) * D].partition_broadcast(128), reads=[modd_res], writes=[M[1]])
        DM("sp", M[2], M[2].ap, mod_d[row, (idx + 2) * D:(idx + 3) * D].partition_broadcast(128), reads=[modd_res], writes=[M[2]])
        DM("sp", tmp, tmp.ap, gpost[l, :].partition_broadcast(128), writes=[tmp])
        O("dve", "tensor_tensor", M[2].ap, M[2].ap, tmp.ap, op=ALU.mult, reads=[M[2], tmp], writes=[M[2]])

    def postnorm_store(pw_banks, pw_ap, GG, src_ap, src_res, dst_ap, dst_res):
        st = next_stat()
        xb = xr
        load_tile(xb, src_ap, src_res)
        O("act", "activation", out=sq_junk.ap, in_=pw_ap, func=AF.Square, accum_out=st.ap[:, 0:1],
          reads=pw_banks, writes=[sq_junk, st])
        rsqrt_small(st, 0, 1, 1.0 / D, EPS)
        o = next_xa()
        O("dve", "scalar_tensor_tensor", out=o.ap, in0=pw_ap, scalar=st.ap[:, 0:1], in1=GG.ap, op0=ALU.mult, op1=ALU.mult,
          reads=pw_banks + [st, GG], writes=[o])
        O("pool", "tensor_tensor", o.ap, o.ap, xb.ap, op=ALU.add, reads=[o, xb], writes=[o])
        DM("sp", o, dst_ap, o.ap, reads=[o], writes=[dst_res])

    def tile_src(l, t):
        if l == 0:
            if t < 2:
                return ctx_in[t * 128:(t + 1) * 128, :], None
            return x_in[(t - 2) * 128:(t - 1) * 128, :], None
        return xs[t * 128:(t + 1) * 128, :], xs_res[t]

    idf = xa[0]
    DM("sp", idf, idf.ap[:, 0:128], ident_in, writes=[idf])
    O("dve", "tensor_copy", idb.ap, idf.ap[:, 0:128], reads=[idf], writes=[idb])

    arena.reset()
    cst = arena.take([128, 8, 2], F32, "cst")
    craw = [arena.take([128, 8], F32, "craw%d" % i) for i in range(2)]
    ctmp = arena.take([128, 8], F32, "ctmp")
    stage = [arena.take([128, 8, 512], F32, "stage%d" % i) for i in range(2)]
    bmt = [arena.take([128, 512], F32, "bmt%d" % i) for i in range(2)]
    mrow = [arena.take([128, 512], F32, "mrow%d" % i) for i in range(2)]
    DM("sp", craw[0], craw[0].ap, c_in, writes=[craw[0]])
    DM("sp", craw[1], craw[1].ap, cc_in, writes=[craw[1]])
    for r in range(2):
        O("act", "activation", out=ctmp.ap, in_=craw[r].ap, func=AF.Tanh, scale=0.5, reads=[craw[r]], writes=[ctmp])
        O("dve", "scalar_tensor_tensor", out=ctmp.ap, in0=ctmp.ap, scalar=1.0, in1=craw[r].ap, op0=ALU.add, op1=ALU.mult,
          reads=[ctmp, craw[r]], writes=[ctmp])
        O("dve", "tensor_scalar", cst.ap[:, :, r], ctmp.ap, 0.5, None, op0=ALU.mult, reads=[ctmp], writes=[cst])
    it = 0
    for l in range(L):
        for j in range(12):
            sg = stage[it % 2]
            bm = bmt[it % 2]
            mr = mrow[it % 2]
            pb = bA[it % 2]
            DM("sp", sg, sg.ap, w_mod[l, j].rearrange("p (k n) -> p k n", k=8), writes=[sg])
            DM("sp", bm, bm.ap[0:2, :], b_mod[l, j * 512:(j + 1) * 512].partition_broadcast(2), writes=[bm])
            for k in range(8):
                O("pe", "matmul", pb.ap[0:2, :], lhsT=cst.ap[:, k, :], rhs=sg.ap[:, k, :], start=(k == 0), stop=(k == 7),
                  reads=[cst, sg], writes=[pb])
            O("dve", "tensor_tensor", mr.ap[0:2, :], pb.ap[0:2, :], bm.ap[0:2, :], op=ALU.add, reads=[pb, bm], writes=[mr])
            DM("sp", mr, mod_d[2 * l:2 * l + 2, j * 512:(j + 1) * 512], mr.ap[0:2, :], reads=[mr], writes=[modd_res])
            it += 1
    p.barrier()

    def finish():
        p.barrier(new_epoch=False)
        print("ops:", p.nops, "sems:", len(p.sems))
        p.emit()
        return nc

    if stop <= 1:
        return finish()

    for l in range(L):
        last = (l == L - 1)
        arena.reset()
        w_in_sb = arena.take([128, 8, INW], BF16, "w_in_sb")
        w_out_sb = arena.take([128, 8, D], BF16, "w_out_sb")
        wsT = arena.take([128, 8, 128], BF16, "wsT")
        bsB = arena.take([128, 8, 64], F32, "bsB")
        bsT = arena.take([128, 8], F32, "bsT")
        KT = arena.take([128, NTOK], BF16, "KT")
        Vaug = arena.take([128, NT_TILES, 2, 128], BF16, "Vaug")
        QT = arena.take([128, 8, 512], BF16, "QT")
        mixT = arena.take([128, 8, 512], BF16, "mixT")
        PT = [arena.take([128, 512], BF16, "PT%d" % i) for i in range(4)]
        rd = [arena.take([128, 512], F32, "rd%d" % i) for i in range(2)]
        gA = arena.take([128, 1024], F32, "gA")
        gB = arena.take([128, 1024], F32, "gB")
        ub = arena.take([128, 512], BF16, "ub")
        vln = arena.take([128, 512], BF16, "vln")
        mlpb = arena.take([128, 512], BF16, "mlpb")
        qf = arena.take([128, 512], F32, "qf")
        qt1 = arena.take([128, 256], F32, "qt1")
        qt2 = arena.take([128, 256], F32, "qt2")
        qb = arena.take([128, 512], BF16, "qb")
        kb = arena.take([128, 128], BF16, "kb")
        cosT = arena.take([128, 32, 32], F32, "cosT")
        sinT = arena.take([128, 32, 32], F32, "sinT")
        gqB = arena.take([128, 64], F32, "gqB")
        gkB = arena.take([128, 64], F32, "gkB")
        gsgB = arena.take([128, 512], F32, "gsgB")
        bsgB = arena.take([128, 512], F32, "bsgB")
        modc = [arena.take([128, D], F32, "modc%d" % i) for i in range(3)]

        DM("pool", w_in_sb, w_in_sb.ap, w_in[l].rearrange("p (k n) -> p k n", k=8), writes=[w_in_sb])
        DM("pool", w_out_sb, w_out_sb.ap, w_out[l].rearrange("p (k n) -> p k n", k=8), writes=[w_out_sb])
        DM("pool", wsT, wsT.ap, w_s[l].rearrange("p (h q) -> p h q", h=8), writes=[wsT])
        O("pool", "tensor_scalar", wsT.ap, wsT.ap, 0.5, None, op0=ALU.mult, reads=[wsT], writes=[wsT])
        if stop == 1.1:
            return finish()
        DM("sp", bsT, bsT.ap, b_s[l], writes=[bsT])
        O("pool", "tensor_scalar", bsB.ap, bsT.ap.unsqueeze(2).to_broadcast([128, 8, 64]), 0.5, None, op0=ALU.mult,
          reads=[bsT], writes=[bsB])
        DM("sp", cosT, cosT.ap, cos_in.rearrange("p (a b) -> p a b", a=32), writes=[cosT])
        DM("sp", sinT, sinT.ap, sin_in.rearrange("p (a b) -> p a b", a=32), writes=[sinT])
        DM("sp", gqB, gqB.ap, g_q[l, :].partition_broadcast(128), writes=[gqB])
        DM("sp", gkB, gkB.ap, g_k[l, :].partition_broadcast(128), writes=[gkB])
        DM("sp", gsgB, gsgB.ap, g_sg[l, :].partition_broadcast(128), writes=[gsgB])
        DM("sp", bsgB, bsgB.ap, b_sg[l, :].partition_broadcast(128), writes=[bsgB])
        if stop == 1.2:
            return finish()
        O("pool", "memset", QT.ap[64:128, 0:8:2, :], 0.0, writes=[QT])
        O("pool", "memset", QT.ap[0:64, 1:8:2, :], 0.0, writes=[QT])
        O("pool", "memset", Vaug.ap[:, :, 0, 64:128], 1.0, writes=[Vaug])
        O("pool", "memset", Vaug.ap[:, :, 1, 0:64], 1.0, writes=[Vaug])
        if stop == 1.3:
            return finish()
        load_mods(modc, l, 1, 0, g_pre_mix, g_post_mix)
        if stop == 1.4:
            return finish()
        load_mods(modx, l, 0, 0, g_pre_mix, g_post_mix)

        if stop <= 2:
            return finish()

        def rope(src_ap, nh, lt, dst_ap, rbufs, wbufs):
            s5 = src_ap.rearrange("p (h a b f) -> p h a b f", h=nh, a=2, b=2)
            d5 = dst_ap.rearrange("p (h a b f) -> p h a b f", h=nh, a=2, b=2)
            x1 = s5[:, :, :, 0, :]
            x2 = s5[:, :, :, 1, :]
            cb = cosT.ap[:, lt, :].rearrange("p (a f) -> p a f", a=2).unsqueeze(1).to_broadcast([128, nh, 2, 16])
            sbb = sinT.ap[:, lt, :].rearrange("p (a f) -> p a f", a=2).unsqueeze(1).to_broadcast([128, nh, 2, 16])
            n = nh * 32
            a1 = qt1.ap[:, 0:n].rearrange("p (h a f) -> p h a f", h=nh, a=2)
            a2 = qt2.ap[:, 0:n].rearrange("p (h a f) -> p h a f", h=nh, a=2)
            O("dve", "tensor_tensor", a1, x1, cb, op=ALU.mult, reads=rbufs + [cosT], writes=[qt1])
            O("dve", "tensor_tensor", a2, x2, sbb, op=ALU.mult, reads=rbufs + [sinT], writes=[qt2])
            O("dve", "tensor_tensor", d5[:, :, :, 0, :], a1, a2, op=ALU.subtract, reads=[qt1, qt2], writes=wbufs)
            O("dve", "tensor_tensor", a1, x1, sbb, op=ALU.mult, reads=rbufs + [sinT, qt1], writes=[qt1])
            O("dve", "tensor_tensor", a2, x2, cb, op=ALU.mult, reads=rbufs + [cosT, qt2], writes=[qt2])
            O("dve", "tensor_tensor", d5[:, :, :, 1, :], a1, a2, op=ALU.add, reads=[qt1, qt2], writes=wbufs)

        def head_norm(src_ap, src_bufs, nh, gB_, dstf):
            st = next_stat()
            n = nh * 64
            d = dstf.ap[:, 0:n]
            d3 = d.rearrange("p (h f) -> p h f", h=nh)
            O("act", "activation", out=d, in_=src_ap, func=AF.Square, reads=src_bufs, writes=[dstf])
            O("dve", "tensor_reduce", st.ap[:, 0:nh], d3, axis=AX.X, op=ALU.add, reads=[dstf], writes=[st])
            rsqrt_small(st, 0, nh, 1.0 / 64, EPS)
            O("dve", "tensor_tensor", d3, src_ap.rearrange("p (h f) -> p h f", h=nh),
              st.ap[:, 0:nh].unsqueeze(2).to_broadcast([128, nh, 64]), op=ALU.mult,
              reads=src_bufs + [st, dstf], writes=[dstf])
            O("dve", "tensor_tensor", d3, d3, gB_.ap.unsqueeze(1).to_broadcast([128, nh, 64]), op=ALU.mult,
              reads=[dstf, gB_], writes=[dstf])

        kvbanks = [bA[3], bA[2]]

        def front_A(t):
            isctx = t < 2
            M = modc if isctx else modx
            xb = next_xa()
            sap, sres = tile_src(l, t)
            load_tile(xb, sap, sres)
            prenorm_to_hT(xb, M[0], M[1], 0)
            bk = kvbanks[t % 2]
            for k in range(8):
                O("pe", "matmul", bk.ap[:, 0:256], lhsT=hT.ap[:, k, 0:128], rhs=w_in_sb.ap[:, k, 512:768],
                  start=(k == 0), stop=(k == 7), reads=[hT, w_in_sb], writes=[bk])

        def back_A(t):
            isctx = t < 2
            bk = kvbanks[t % 2]
            head_norm(bk.ap[:, 0:128], [bk], 2, gkB, qf)
            if isctx:
                O("dve", "tensor_copy", kb.ap, qf.ap[:, 0:128], reads=[qf], writes=[kb])
            else:
                rope(qf.ap[:, 0:128], 2, t - 2, kb.ap, [qf], [kb])
            O("act", "copy", Vaug.ap[:, t, 0, 0:64], bk.ap[:, 128:192], reads=[bk], writes=[Vaug])
            O("act", "copy", Vaug.ap[:, t, 1, 64:128], bk.ap[:, 192:256], reads=[bk], writes=[Vaug])
            O("pe", "transpose", bB[0].ap.bitcast(BF16)[:, 0:128], kb.ap, idb.ap, reads=[kb, idb], writes=[bB[0]])
            O("act", "copy", KT.ap[:, t * 128:(t + 1) * 128], bB[0].ap.bitcast(BF16)[:, 0:128], reads=[bB[0]], writes=[KT])

        front_A(0)
        for t in range(NT_TILES):
            if t + 1 < NT_TILES:
                front_A(t + 1)
            back_A(t)
        if stop <= 3:
            return finish()
        groups = []
        if not last:
            groups.append((True, [0, 1]))
        for g in range(8):
            groups.append((False, [2 + 4 * g + i for i in range(4)]))
        for (isctx, tiles) in groups:
            M = modc if isctx else modx
            NTk = 128 * len(tiles)
            nk = 2 if isctx else NT_TILES
            for i, t in enumerate(tiles):
                xb = next_xa()
                sap, sres = tile_src(l, t)
                load_tile(xb, sap, sres)
                prenorm_to_hT(xb, M[0], M[1], i * 128)
                c0 = i * 128
                for (bank, col0) in ((bA[0], 0), (bA[1], 768), (bA[2], 1280)):
                    for k in range(8):
                        O("pe", "matmul", bank.ap, lhsT=hT.ap[:, k, c0:c0 + 128], rhs=w_in_sb.ap[:, k, col0:col0 + 512],
                          start=(k == 0), stop=(k == 7), reads=[hT, w_in_sb], writes=[bank])
                head_norm(bA[0].ap, [bA[0]], 8, gqB, qf)
                if isctx:
                    O("dve", "tensor_copy", qb.ap, qf.ap, reads=[qf], writes=[qb])
                else:
                    rope(qf.ap, 8, t - 2, qb.ap, [qf], [qb])
                for j in range(4):
                    O("pe", "transpose", bT.ap[:, j * 128:(j + 1) * 128], qb.ap[:, j * 128:(j + 1) * 128], idb.ap,
                      reads=[qb, idb], writes=[bT])
                O("act", "copy", QT.ap[0:64, 0:8:2, c0:c0 + 128], bT.ap[0:64, 0:512].rearrange("p (j c) -> p j c", j=4),
                  reads=[bT], writes=[QT])
                O("dve", "tensor_copy", QT.ap[64:128, 1:8:2, c0:c0 + 128], bT.ap[64:128, 0:512].rearrange("p (j c) -> p j c", j=4),
                  reads=[bT], writes=[QT])
                z = psA[:, 512:1536]
                zb = [bA[1], bA[2]]
                O("act", "activation", out=gA.ap, in_=z, func=AF.Square, reads=zb, writes=[gA])
                O("dve", "tensor_scalar", gA.ap, gA.ap, 0.044715, 1.0, op0=ALU.mult, op1=ALU.add, reads=[gA], writes=[gA])
                O("dve", "tensor_tensor", gA.ap, gA.ap, z, op=ALU.mult, reads=[gA] + zb, writes=[gA])
                O("act", "activation", out=gB.ap, in_=gA.ap, func=AF.Tanh, scale=0.7978845608028654, reads=[gA], writes=[gB])
                O("dve", "scalar_tensor_tensor", out=ub.ap, in0=gB.ap[:, 0:512], scalar=1.0, in1=z[:, 0:512], op0=ALU.add, op1=ALU.mult,
                  reads=[gB, bA[1]], writes=[ub])
                vp = gA.ap[:, 512:1024]
                O("dve", "scalar_tensor_tensor", out=vp, in0=gB.ap[:, 512:1024], scalar=1.0, in1=z[:, 512:1024], op0=ALU.add, op1=ALU.mult,
                  reads=[gB, bA[2], gA], writes=[gA])
                st = next_stat()
                O("dve", "bn_stats", st.ap[:, 0:6], vp, reads=[gA], writes=[st])
                O("dve", "bn_aggr", st.ap[:, 8:10], st.ap[:, 0:6], reads=[st], writes=[st])
                rsqrt_small(st, 9, 10, 1.0, 4 * EPS)
                O("dve", "tensor_scalar", vp, vp, st.ap[:, 8:9], st.ap[:, 9:10], op0=ALU.subtract, op1=ALU.mult,
                  reads=[gA, st], writes=[gA])
                O("pool", "tensor_tensor", vp, vp, gsgB.ap, op=ALU.mult, reads=[gA, gsgB], writes=[gA])
                O("pool", "tensor_tensor", vln.ap, vp, bsgB.ap, op=ALU.add, reads=[gA, bsgB], writes=[vln])
                for h in range(8):
                    O("pe", "matmul", bB[0].ap[:, h * 64:(h + 1) * 64], lhsT=wsT.ap[:, h, :], rhs=vln.ap[:, h * 64:(h + 1) * 64],
                      start=True, stop=True, reads=[wsT, vln], writes=[bB[0]])
                O("dve", "tensor_tensor", gB.ap[:, 0:512], bB[0].ap, bsB.ap.rearrange("p h f -> p (h f)"), op=ALU.add,
                  reads=[bB[0], bsB], writes=[gB])
                O("pool", "tensor_tensor", mlpb.ap, gB.ap[:, 0:512], ub.ap, op=ALU.mult, reads=[gB, ub], writes=[mlpb])
                for j in range(4):
                    O("pe", "transpose", bT.ap[:, 512 + j * 128:512 + (j + 1) * 128], mlpb.ap[:, j * 128:(j + 1) * 128], idb.ap,
                      reads=[mlpb, idb], writes=[bT])
                O("act", "copy", mixT.ap[:, 4:8, c0:c0 + 128], bT.ap[:, 512:1024].rearrange("p (j c) -> p j c", j=4),
                  reads=[bT], writes=[mixT])

            sc_banks = [bB[0], bB[1], bB[2]]
            o_banks = [bC, bA[0]]
            hi = 0
            for j in range(4):
                for hh in range(2):
                    ob = o_banks[hi % 2]
                    rdb = rd[hi % 2]
                    hi += 1
                    r0 = hh * 64

                    def qk(kt):
                        sb_ = sc_banks[kt % 3]
                        O("pe", "matmul", sb_.ap[:, 0:NTk], lhsT=KT.ap[:, kt * 128:(kt + 1) * 128],
                          rhs=QT.ap[:, 2 * j + hh, 0:NTk], start=True, stop=True, reads=[KT, QT], writes=[sb_])

                    def ex(kt):
                        sb_ = sc_banks[kt % 3]
                        pt = PT[kt % 4]
                        O("act", "activation", out=pt.ap[:, 0:NTk], in_=sb_.ap[:, 0:NTk], func=AF.Exp, scale=0.125,
                          reads=[sb_], writes=[pt])

                    def pv(kt):
                        pt = PT[kt % 4]
                        O("pe", "matmul", ob.ap[:, 0:NTk], lhsT=Vaug.ap[:, kt, hh, :], rhs=pt.ap[:, 0:NTk],
                          start=(kt == 0), stop=(kt == nk - 1), reads=[Vaug, pt], writes=[ob])

                    LA = 2
                    for kt in range(min(LA, nk)):
                        qk(kt)
                    for kt in range(nk):
                        ex(kt)
                        if kt + LA < nk:
                            qk(kt + LA)
                        pv(kt)
                    d0 = 64 - r0
                    O("dve", "reciprocal", rdb.ap[r0:r0 + 64, 0:NTk], ob.ap[d0:d0 + 64, 0:NTk], reads=[ob], writes=[rdb])
                    O("dve", "tensor_tensor", mixT.ap[r0:r0 + 64, j, 0:NTk], ob.ap[r0:r0 + 64, 0:NTk], rdb.ap[r0:r0 + 64, 0:NTk],
                      op=ALU.mult, reads=[ob, rdb], writes=[mixT])

            for i, t in enumerate(tiles):
                c0 = i * 128
                pw = [bA[2], bA[3]]
                for half in range(2):
                    for c in range(8):
                        O("pe", "matmul", pw[half].ap, lhsT=mixT.ap[:, c, c0:c0 + 128], rhs=w_out_sb.ap[:, c, half * 512:(half + 1) * 512],
                          start=(c == 0), stop=(c == 7), reads=[mixT, w_out_sb], writes=[pw[half]])
                sap, sres = tile_src(l, t)
                postnorm_store(pw, psA[:, 1024:2048], M[2], sap, sres, xs[t * 128:(t + 1) * 128, :], xs_res[t])

        if stop <= 4:
            return finish()
        p.barrier(new_epoch=False)
        arena.reset()
        wgu = [arena.take([128, 8, 256], BF16, "wgu%d" % j) for j in range(NJ)]
        wo = [arena.take([128, D], BF16, "wo%d" % j) for j in range(NJ)]
        actT = arena.take([128, NJ, 512], BF16, "actT")
        sgt = [arena.take([128, 512], F32, "sgt%d" % i) for i in range(2)]
        for j in range(NJ):
            DM("pool", wgu[j], wgu[j].ap, w_gu[l, j].rearrange("p (k n) -> p k n", k=8), writes=[wgu[j]])
        for j in range(NJ):
            DM("pool", wo[j], wo[j].ap, w_o[l][:, j * D:(j + 1) * D], writes=[wo[j]])
        cgroups = []
        if not last:
            cgroups.append((True, [0, 1]))
        for g in range(8):
            cgroups.append((False, [2 + 4 * g + i for i in range(4)]))
        cur_stream = [None]
        hTs = [hT, hT2]

        def c_prenorm(gi):
            isctx, tiles = cgroups[gi]
            r = 1 if isctx else 0
            if cur_stream[0] != r:
                load_mods(modx, l, r, 3, g_pre_ffn, g_post_ffn)
                cur_stream[0] = r
            for i, t in enumerate(tiles):
                xb = next_xa()
                load_tile(xb, xs[t * 128:(t + 1) * 128, :], xs_res[t])
                prenorm_to_hT(xb, modx[0], modx[1], i * 128, hT=hTs[gi % 2])

        done_pre = set()
        for gi, (isctx, tiles) in enumerate(cgroups):
            if gi not in done_pre:
                c_prenorm(gi)
                done_pre.add(gi)
            hTc = hTs[gi % 2]
            NTk = 128 * len(tiles)
            gub = [(bA[0], bA[1]), (bB[0], bB[1])]
            for j in range(NJ):
                pg, pu = gub[j % 2]
                for k in range(8):
                    O("pe", "matmul", pg.ap[:, 0:NTk], lhsT=wgu[j].ap[:, k, 0:128], rhs=hTc.ap[:, k, 0:NTk],
                      start=(k == 0), stop=(k == 7), reads=[wgu[j], hTc], writes=[pg])
                for k in range(8):
                    O("pe", "matmul", pu.ap[:, 0:NTk], lhsT=wgu[j].ap[:, k, 128:256], rhs=hTc.ap[:, k, 0:NTk],
                      start=(k == 0), stop=(k == 7), reads=[wgu[j], hTc], writes=[pu])
                sg_ = sgt[j % 2]
                O("act", "activation", out=sg_.ap[:, 0:NTk], in_=pg.ap[:, 0:NTk], func=AF.Silu, reads=[pg], writes=[sg_])
                O("dve", "tensor_tensor", actT.ap[:, j, 0:NTk], sg_.ap[:, 0:NTk], pu.ap[:, 0:NTk], op=ALU.mult,
                  reads=[sg_, pu], writes=[actT])
            if gi + 1 < len(cgroups) and cgroups[gi + 1][0] == isctx:
                c_prenorm(gi + 1)
                done_pre.add(gi + 1)
            for i, t in enumerate(tiles):
                c0 = i * 128
                pw = [bA[2], bA[3]]
                for half in range(2):
                    for j in range(NJ):
                        O("pe", "matmul", pw[half].ap, lhsT=actT.ap[:, j, c0:c0 + 128], rhs=wo[j].ap[:, half * 512:(half + 1) * 512],
                          start=(j == 0), stop=(j == NJ - 1), reads=[actT, wo[j]], writes=[pw[half]])
                if last and not isctx:
                    dst_ap, dst_res = y_out[(t - 2) * 128:(t - 1) * 128, :], y_res[t - 2]
                else:
                    dst_ap, dst_res = xs[t * 128:(t + 1) * 128, :], xs_res[t]
                postnorm_store(pw, psA[:, 1024:2048], modx[2], xs[t * 128:(t + 1) * 128, :], xs_res[t], dst_ap, dst_res)
        p.barrier()

    return finish()


def _rope_tables():
    n = SEQ
    rows = n // 64
    pos_row = np.repeat(np.arange(rows, dtype=np.float32), 64)
    pos_col = np.tile(np.arange(64, dtype=np.float32), rows)
    inv = (10000.0 ** (-np.arange(0, 32, 2, dtype=np.float32) / 32)).astype(np.float32)
    ang = np.concatenate([pos_row[:, None] * inv, pos_col[:, None] * inv], axis=-1).astype(np.float32)
    cos = np.cos(ang).astype(np.float32)
    sin = np.sin(ang).astype(np.float32)
    cos = np.ascontiguousarray(cos.reshape(32, 128, 32).transpose(1, 0, 2)).reshape(128, 32 * 32)
    sin = np.ascontiguousarray(sin.reshape(32, 128, 32).transpose(1, 0, 2)).reshape(128, 32 * 32)
    return cos, sin


def prep_shared(inputs, L):
    f = lambda a: np.ascontiguousarray(np.asarray(a, dtype=np.float32))
    sh = {}
    wm = f(inputs["w_mod"])[:L]
    sh["w_mod"] = f(wm.reshape(L, 8, 128, 12, 512).transpose(0, 3, 2, 1, 4)).reshape(L, 12, 128, 8 * 512)
    sh["b_mod"] = f(inputs["b_mod"])[:L]
    for k in ("g_pre_mix", "g_post_mix", "g_pre_ffn", "g_post_ffn", "g_q", "g_k", "g_sg", "b_sg"):
        sh[k] = f(inputs[k])[:L]
    wi = f(inputs["w_in"])[:L]
    qcols = np.concatenate([np.arange(h * 64, (h + 1) * 64) for h in QPERM])
    cols = np.concatenate([qcols, np.arange(512, INW)])
    wi = wi[:, :, cols]
    sh["w_in"] = f(wi.reshape(L, 8, 128, INW).transpose(0, 2, 1, 3)).reshape(L, 128, 8 * INW)
    ws = f(inputs["w_s"])[:L]
    sh["w_s"] = f(ws.transpose(0, 3, 1, 2)).reshape(L, 128, 8 * 128)
    sh["b_s"] = f(f(inputs["b_s"])[:L].transpose(0, 2, 1))
    wo_ = f(inputs["w_out"])[:L]
    rows = np.concatenate([qcols, np.arange(512, 1024)])
    wo_ = wo_[:, rows, :]
    sh["w_out"] = f(wo_.reshape(L, 8, 128, D).transpose(0, 2, 1, 3)).reshape(L, 128, 8 * D)
    wf = f(inputs["w_ffn_in"])[:L]
    gate = wf[:, :, :HID].reshape(L, 8, 128, NJ, 128)
    up = wf[:, :, HID:].reshape(L, 8, 128, NJ, 128)
    gu = np.concatenate([gate, up], axis=-1)
    sh["w_gu"] = f(gu.transpose(0, 3, 2, 1, 4)).reshape(L, NJ, 128, 8 * 256)
    wfo = f(inputs["w_ffn_out"])[:L]
    sh["w_o"] = f(wfo.reshape(L, NJ, 128, D).transpose(0, 2, 1, 3)).reshape(L, 128, NJ * D)
    sh["ident"] = np.eye(128, dtype=np.float32)
    cos, sin = _rope_tables()
    sh["rope_cos"] = cos
    sh["rope_sin"] = sin
    sh["c_ctx"] = f(f(inputs["c_ctx"]).reshape(8, 128).T)
    return sh


def prep_core(inputs, b):
    f = lambda a: np.ascontiguousarray(np.asarray(a, dtype=np.float32))
    return {
        "x": f(inputs["x"][b]),
        "ctx": f(inputs["ctx"][b]),
        "c": f(f(inputs["c"][b]).reshape(8, 128).T),
    }


_NC_CACHE = {}


def run(inputs, n_layers=DEPTH, cores=None, stop=99):
    if cores is None:
        cores = list(range(8))
    if (n_layers, stop) not in _NC_CACHE:
        _NC_CACHE[(n_layers, stop)] = build(n_layers, stop)
    nc = _NC_CACHE[(n_layers, stop)]
    sh = prep_shared(inputs, n_layers)
    in_maps = []
    for b in cores:
        m = dict(sh)
        m.update(prep_core(inputs, b))
        in_maps.append(m)
    res = run_bass_kernel_spmd(nc, in_maps, core_ids=list(range(len(cores))))
    return np.stack([np.asarray(r["y"], dtype=np.float32) for r in res.results], axis=0)


def kernel(**inputs):
    return run(inputs, DEPTH)
```

```python
import os
import numpy as np
import concourse.bass as bass
import concourse.mybir as mybir
from concourse.bass_utils import run_bass_kernel_spmd

F32 = mybir.dt.float32
BF16 = mybir.dt.bfloat16
I32 = mybir.dt.int32
AF = mybir.ActivationFunctionType
ALU = mybir.AluOpType
AX = mybir.AxisListType

D = 1024
SEQ = 4096
CTX = 256
NTOK = SEQ + CTX
NT_TILES = NTOK // 128
DEPTH = 4
HID = 2816
NJ = HID // 128
INW = 1792
EPS = 1e-6
QPERM = [0, 4, 1, 5, 2, 6, 3, 7]


class Res:
    __slots__ = ("w", "r", "name", "sem", "excl")

    def __init__(self, name=""):
        self.w = None
        self.r = []
        self.name = name
        self.sem = None
        self.excl = False


class Buf(Res):
    __slots__ = ("ap",)

    def __init__(self, ap, name=""):
        Res.__init__(self, name)
        self.ap = ap


class Prog:
    ENGS = ("pe", "act", "dve", "pool", "sp")

    def __init__(self, nc):
        self.nc = nc
        self.streams = {e: [] for e in self.ENGS}
        self.sems = {}
        self.cnt = {}
        self.cur = {}
        self.epoch = 0
        self.dma_keys = {}
        self._new_epoch()
        self.waited = {e: {} for e in self.ENGS}
        self.ndma = 0
        self.nops = 0

    def _new_epoch(self):
        for e in self.ENGS:
            key = "%s#%d" % (e, self.epoch)
            self.sems[key] = self.nc.alloc_semaphore("s_%s_%d" % (e, self.epoch))
            self.cnt[key] = 0
            self.cur[e] = key
        self.epoch += 1

    def _need(self, eng, deps):
        best = {}
        for (k, v) in deps:
            if v > best.get(k, 0):
                best[k] = v
        out = []
        w = self.waited[eng]
        for k, v in best.items():
            if eng == "pe" and k.startswith("pe#"):
                continue
            if w.get(k, 0) < v:
                w[k] = v
                out.append((k, v))
        return out

    @staticmethod
    def _collect(reads, writes):
        deps = []
        for r in reads:
            if r.w is not None:
                deps.append(r.w)
        for wr in writes:
            if wr.w is not None:
                deps.append(wr.w)
            deps.extend(wr.r)
        return deps

    def _mark(self, me, reads, writes):
        for r in reads:
            r.r.append(me)
            if len(r.r) > 64:
                best = {}
                for (k, v) in r.r:
                    if v > best.get(k, 0):
                        best[k] = v
                r.r = list(best.items())
        for wr in writes:
            wr.w = me
            wr.r = []

    def op(self, eng, method, *args, reads=(), writes=(), **kw):
        ex = [r for r in reads if r.excl and r not in writes]
        if ex:
            writes = list(writes) + ex
        waits = self._need(eng, self._collect(reads, writes))
        key = self.cur[eng]
        self.cnt[key] += 1
        me = (key, self.cnt[key])
        self.streams[eng].append((waits, (method, args, kw), (key, 1)))
        self._mark(me, reads, writes)
        self.nops += 1
        return me

    def dma_sem_for(self, res):
        key = "dma:" + res.name
        if key not in self.sems:
            self.ndma += 1
            self.sems[key] = self.nc.alloc_semaphore("d_%d" % self.ndma)
            self.cnt[key] = 0
        return key

    def dma(self, queue, semres, out, in_, reads=(), writes=()):
        semkey = self.dma_sem_for(semres)
        waits = self._need(queue, self._collect(reads, writes))
        self.cnt[semkey] += 16
        me = (semkey, self.cnt[semkey])
        self.streams[queue].append((waits, ("dma_start", (), dict(out=out, in_=in_)), (semkey, 16)))
        self._mark(me, reads, writes)
        self.nops += 1
        return me

    def barrier(self, new_epoch=True):
        allv = [(k, v) for k, v in self.cnt.items() if v > 0]
        for e in self.ENGS:
            waits = self._need(e, allv)
            if waits:
                self.streams[e].append((waits, None, None))
        if new_epoch:
            self._new_epoch()

    def emit(self):
        nc = self.nc
        sems = self.sems
        streams = self.streams

        waited = {}
        for ename in self.ENGS:
            for (waits, fn, inc) in streams[ename]:
                for (k, v) in waits:
                    waited.setdefault(k, set()).add(v)
        remap = {}
        for k, vals in waited.items():
            if k.startswith("dma:"):
                continue
            remap[k] = {v: i + 1 for i, v in enumerate(sorted(vals))}
        seen = {}

        def run(e, ename):
            for (waits, fn, inc) in streams[ename]:
                for (k, v) in waits:
                    if k in remap:
                        e.wait_ge(sems[k], remap[k][v])
                    else:
                        e.wait_ge(sems[k], v)
                if fn is not None:
                    ins = getattr(e, fn[0])(*fn[1], **fn[2])
                    k = inc[0]
                    if k.startswith("dma:"):
                        ins.then_inc(sems[k], inc[1])
                    else:
                        seen[k] = seen.get(k, 0) + 1
                        if seen[k] in remap.get(k, ()):
                            ins.then_inc(sems[k], 1)

        with nc.Block() as block:
            @block.tensor
            def _(e):
                run(e, "pe")

            @block.scalar
            def _(e):
                run(e, "act")

            @block.vector
            def _(e):
                run(e, "dve")

            @block.gpsimd
            def _(e):
                run(e, "pool")

            @block.sync
            def _(e):
                run(e, "sp")


class Arena:
    def __init__(self, ap_bf16):
        self.ap = ap_bf16
        self.off = 0
        self.size = ap_bf16.shape[1]

    def reset(self):
        self.off = 0

    def take(self, shape, dtype, name=""):
        n = 1
        for s in shape[1:]:
            n *= s
        nel = n * (2 if dtype == F32 else 1)
        if self.off % 2:
            self.off += 1
        assert self.off + nel <= self.size, ("arena overflow", name, self.off, nel, self.size)
        v = self.ap[:, self.off:self.off + nel]
        self.off += nel
        if dtype == F32:
            v = v.bitcast(F32)
        if len(shape) == 3:
            v = v.rearrange("p (a b) -> p a b", a=shape[1])
        elif len(shape) == 4:
            v = v.rearrange("p (a b c) -> p a b c", a=shape[1], b=shape[2])
        return Buf(v, name)


def build(n_layers=DEPTH, stop=99):
    nc = bass.Bass("TRN2", target_bir_lowering=False)
    p = Prog(nc)
    O = p.op
    DM = p.dma
    L = n_layers

    def din(name, shape):
        return nc.dram_tensor(name, list(shape), F32, kind="ExternalInput").ap()

    x_in = din("x", [SEQ, D])
    ctx_in = din("ctx", [CTX, D])
    c_in = din("c", [128, 8])
    cc_in = din("c_ctx", [128, 8])
    w_mod = din("w_mod", [L, 12, 128, 8 * 512])
    b_mod = din("b_mod", [L, 6 * D])
    g_pre_mix = din("g_pre_mix", [L, D])
    g_post_mix = din("g_post_mix", [L, D])
    g_pre_ffn = din("g_pre_ffn", [L, D])
    g_post_ffn = din("g_post_ffn", [L, D])
    w_in = din("w_in", [L, 128, 8 * INW])
    g_q = din("g_q", [L, 64])
    g_k = din("g_k", [L, 64])
    g_sg = din("g_sg", [L, 512])
    b_sg = din("b_sg", [L, 512])
    w_s = din("w_s", [L, 128, 8 * 128])
    b_s = din("b_s", [L, 128, 8])
    w_out = din("w_out", [L, 128, 8 * D])
    w_gu = din("w_gu", [L, NJ, 128, 8 * 256])
    w_o = din("w_o", [L, 128, NJ * D])
    ident_in = din("ident", [128, 128])
    cos_in = din("rope_cos", [128, 32 * 32])
    sin_in = din("rope_sin", [128, 32 * 32])
    y_out = nc.dram_tensor("y", [SEQ, D], F32, kind="ExternalOutput").ap()
    xs = nc.dram_tensor("xs", [NTOK, D], F32, kind="Internal").ap()
    mod_d = nc.dram_tensor("mod_d", [L * 2, 6 * D], F32, kind="Internal").ap()

    xs_res = [Res("xs%d" % t) for t in range(NT_TILES)]
    y_res = [Res("y%d" % t) for t in range(32)]
    modd_res = Res("mod_d")

    def sb(name, shape, dtype):
        return Buf(nc.alloc_sbuf_tensor(name, list(shape), dtype).ap(), name)

    idb = sb("idb", [128, 128], BF16)
    xa = [sb("xa%d" % i, [128, D], F32) for i in range(2)]
    xr = sb("xr0", [128, D], F32)
    sq_junk = sb("sq_junk", [128, D], BF16)
    hb = [sb("hb%d" % i, [128, D], BF16) for i in range(2)]
    hT = sb("hT", [128, 8, 512], BF16)
    hT2 = sb("hT2", [128, 8, 512], BF16)
    modx = [sb("modx%d" % i, [128, D], F32) for i in range(3)]
    stat = [sb("stat%d" % i, [128, 16], F32) for i in range(4)]
    rs_y = [sb("rsy%d" % i, [128, 16], F32) for i in range(4)]
    rs_t = [sb("rst%d" % i, [128, 16], F32) for i in range(4)]
    one_i = sb("one_i", [128, 1], I32)
    magic_i = sb("magic_i", [128, 16], I32)
    arena = Arena(nc.alloc_sbuf_tensor("arena", [128, 79 * 1024], BF16).ap())

    psT = nc.alloc_psum_tensor("psT", [128, 1024], BF16).ap()
    psA = nc.alloc_psum_tensor("psA", [128, 2048], F32).ap()
    psBs = [nc.alloc_psum_tensor("psB%d" % i, [128, 512], F32).ap() for i in range(3)]
    bT = Buf(psT, "bT")
    bA = [Buf(psA[:, i * 512:(i + 1) * 512], "bA%d" % i) for i in range(4)]
    bB = [Buf(psBs[i], "bB%d" % i) for i in range(3)]
    bC = bA[3]
    for b_ in [bT] + bA + bB:
        b_.excl = True

    cnt = {"xa": 0, "hb": 0, "stat": 0}

    def next_xa():
        b = xa[cnt["xa"] % 2]
        cnt["xa"] += 1
        return b

    def next_stat():
        b = stat[cnt["stat"] % 4]
        cnt["stat"] += 1
        return b

    def load_tile(dst, src_ap, src_res):
        DM("sp", dst, dst.ap, src_ap, reads=[src_res] if src_res else [], writes=[dst])

    def rsqrt_small(st, lo, hi, scale, eps, mode="act"):
        n = hi - lo
        v = st.ap[:, lo:hi]
        O("dve", "tensor_scalar", v, v, scale, eps, op0=ALU.mult, op1=ALU.add, reads=[st], writes=[st])
        if mode == "act":
            O("act", "activation", out=v, in_=v, func=AF.Sqrt, reads=[st], writes=[st])
            O("dve", "reciprocal", v, v, reads=[st], writes=[st])
            return
        k = stat.index(st)
        yb, tb = rs_y[k], rs_t[k]
        y = yb.ap[:, 0:n]
        t1 = tb.ap[:, 0:n]
        yi = y.bitcast(I32)
        O("dve", "tensor_scalar", yi, v.bitcast(I32), one_i.ap[:, 0:1], None, op0=ALU.arith_shift_right,
          reads=[st, one_i], writes=[yb])
        O("dve", "tensor_tensor", yi, magic_i.ap[:, 0:n], yi, op=ALU.subtract, reads=[yb, magic_i], writes=[yb])
        NIT = 3
        for it_ in range(NIT):
            O("dve", "tensor_tensor", t1, v, y, op=ALU.mult, reads=[st, yb], writes=[tb])
            O("dve", "tensor_tensor", t1, t1, y, op=ALU.mult, reads=[tb, yb], writes=[tb])
            O("dve", "tensor_scalar", t1, t1, -0.5, 1.5, op0=ALU.mult, op1=ALU.add, reads=[tb], writes=[tb])
            if it_ < NIT - 1:
                O("dve", "tensor_tensor", y, y, t1, op=ALU.mult, reads=[yb, tb], writes=[yb])
            else:
                O("dve", "tensor_tensor", v, y, t1, op=ALU.mult, reads=[yb, tb], writes=[st])

    def prenorm_to_hT(xbuf, G1, SH, ncol0, hT=hT):
        st = next_stat()
        h = hb[cnt["hb"] % 2]
        cnt["hb"] += 1
        O("act", "activation", out=sq_junk.ap, in_=xbuf.ap, func=AF.Square, accum_out=st.ap[:, 0:1],
          reads=[xbuf], writes=[sq_junk, st])
        rsqrt_small(st, 0, 1, 1.0 / D, EPS)
        O("dve", "scalar_tensor_tensor", out=xbuf.ap, in0=xbuf.ap, scalar=st.ap[:, 0:1], in1=G1.ap,
          op0=ALU.mult, op1=ALU.mult, reads=[xbuf, st, G1], writes=[xbuf])
        O("dve", "tensor_tensor", h.ap, xbuf.ap, SH.ap, op=ALU.add, reads=[xbuf, SH], writes=[h])
        for k in range(8):
            O("pe", "transpose", bT.ap[:, k * 128:(k + 1) * 128], h.ap[:, k * 128:(k + 1) * 128], idb.ap,
              reads=[h, idb], writes=[bT])
        O("act", "copy", hT.ap[:, :, ncol0:ncol0 + 128], bT.ap.rearrange("p (k c) -> p k c", k=8),
          reads=[bT], writes=[hT])

    def load_mods(M, l, r, idx, gpre, gpost):
        row = l * 2 + r
        tmp = xr
        DM("sp", M[0], M[0].ap, mod_d[row, (idx + 1) * D:(idx + 2) * D].partition_broadcast(128), reads=[modd_res], writes=[M[0]])
        DM("sp", tmp, tmp.ap, gpre[l, :].partition_broadcast(128), writes=[tmp])
        O("dve", "scalar_tensor_tensor", out=M[0].ap, in0=M[0].ap, scalar=1.0, in1=tmp.ap, op0=ALU.add, op1=ALU.mult,
          reads=[M[0], tmp], writes=[M[0]])
        DM("sp", M[1], M[1].ap, mod_d[row, idx * D:(idx + 1) * D].partition_broadcast(128), reads=[modd_res], writes=[M[1]])
        DM("sp", M[2], M[2].ap, mod_d[row, (idx + 2) * D:(idx + 3) * D].partition_broadcast(128), reads=[modd_res], writes=[M[2]])
        DM("sp", tmp, tmp.ap, gpost[l, :].partition_broadcast(128), writes=[tmp])
        O("dve", "tensor_tensor", M[2].ap, M[2].ap, tmp.ap, op=ALU.mult, reads=[M[2], tmp], writes=[M[2]])

    def postnorm_store(pw_banks, pw_ap, GG, src_ap, src_res, dst_ap, dst_res):
        st = next_stat()
        xb = xr
        load_tile(xb, src_ap, src_res)
        O("act", "activation", out=sq_junk.ap, in_=pw_ap, func=AF.Square, accum_out=st.ap[:, 0:1],
          reads=pw_banks, writes=[sq_junk, st])
        rsqrt_small(st, 0, 1, 1.0 / D, EPS)
        o = next_xa()
        O("dve", "scalar_tensor_tensor", out=o.ap, in0=pw_ap, scalar=st.ap[:, 0:1], in1=GG.ap, op0=ALU.mult, op1=ALU.mult,
          reads=pw_banks + [st, GG], writes=[o])
        O("pool", "tensor_tensor", o.ap, o.ap, xb.ap, op=ALU.add, reads=[o, xb], writes=[o])
        DM("sp", o, dst_ap, o.ap, reads=[o], writes=[dst_res])

    def tile_src(l, t):
        if l == 0:
            if t < 2:
                return ctx_in[t * 128:(t + 1) * 128, :], None
            return x_in[(t - 2) * 128:(t - 1) * 128, :], None
        return xs[t * 128:(t + 1) * 128, :], xs_res[t]

    O("dve", "memset", one_i.ap, 1, writes=[one_i])
    O("dve", "memset", magic_i.ap, 0x5f3759df, writes=[magic_i])
    idf = xa[0]
    DM("sp", idf, idf.ap[:, 0:128], ident_in, writes=[idf])
    O("dve", "tensor_copy", idb.ap, idf.ap[:, 0:128], reads=[idf], writes=[idb])

    arena.reset()
    cst = arena.take([128, 8, 2], F32, "cst")
    craw = [arena.take([128, 8], F32, "craw%d" % i) for i in range(2)]
    ctmp = arena.take([128, 8], F32, "ctmp")
    stage = [arena.take([128, 8, 512], F32, "stage%d" % i) for i in range(2)]
    bmt = [arena.take([128, 512], F32, "bmt%d" % i) for i in range(2)]
    mrow = [arena.take([128, 512], F32, "mrow%d" % i) for i in range(2)]
    DM("sp", craw[0], craw[0].ap, c_in, writes=[craw[0]])
    DM("sp", craw[1], craw[1].ap, cc_in, writes=[craw[1]])
    for r in range(2):
        O("act", "activation", out=ctmp.ap, in_=craw[r].ap, func=AF.Tanh, scale=0.5, reads=[craw[r]], writes=[ctmp])
        O("dve", "scalar_tensor_tensor", out=ctmp.ap, in0=ctmp.ap, scalar=1.0, in1=craw[r].ap, op0=ALU.add, op1=ALU.mult,
          reads=[ctmp, craw[r]], writes=[ctmp])
        O("dve", "tensor_scalar", cst.ap[:, :, r], ctmp.ap, 0.5, None, op0=ALU.mult, reads=[ctmp], writes=[cst])
    it = 0
    for l in range(L):
        for j in range(12):
            sg = stage[it % 2]
            bm = bmt[it % 2]
            mr = mrow[it % 2]
            pb = bA[it % 2]
            DM("sp", sg, sg.ap, w_mod[l, j].rearrange("p (k n) -> p k n", k=8), writes=[sg])
            DM("sp", bm, bm.ap[0:2, :], b_mod[l, j * 512:(j + 1) * 512].partition_broadcast(2), writes=[bm])
            for k in range(8):
                O("pe", "matmul", pb.ap[0:2, :], lhsT=cst.ap[:, k, :], rhs=sg.ap[:, k, :], start=(k == 0), stop=(k == 7),
                  reads=[cst, sg], writes=[pb])
            O("dve", "tensor_tensor", mr.ap[0:2, :], pb.ap[0:2, :], bm.ap[0:2, :], op=ALU.add, reads=[pb, bm], writes=[mr])
            DM("sp", mr, mod_d[2 * l:2 * l + 2, j * 512:(j + 1) * 512], mr.ap[0:2, :], reads=[mr], writes=[modd_res])
            it += 1
    p.barrier()

    def finish():
        p.barrier(new_epoch=False)
        print("ops:", p.nops, "sems:", len(p.sems))
        p.emit()
        return nc

    if stop <= 1:
        return finish()

    for l in range(L):
        last = (l == L - 1)
        arena.reset()
        w_in_sb = arena.take([128, 8, INW], BF16, "w_in_sb")
        w_out_sb = arena.take([128, 8, D], BF16, "w_out_sb")
        wsT = arena.take([128, 8, 128], BF16, "wsT")
        bsB = arena.take([128, 8, 64], F32, "bsB")
        bsT = arena.take([128, 8], F32, "bsT")
        KT = arena.take([128, NTOK], BF16, "KT")
        Vaug = arena.take([128, NT_TILES, 2, 128], BF16, "Vaug")
        QT = arena.take([128, 8, 512], BF16, "QT")
        mixT = arena.take([128, 8, 512], BF16, "mixT")
        QT2 = arena.take([128, 8, 512], BF16, "QT2")
        mixT2 = arena.take([128, 8, 512], BF16, "mixT2")
        PT = [arena.take([128, 512], BF16, "PT%d" % i) for i in range(4)]
        rd = [arena.take([128, 512], F32, "rd%d" % i) for i in range(2)]
        gA = arena.take([128, 1024], F32, "gA")
        gB = arena.take([128, 1024], F32, "gB")
        ub = arena.take([128, 512], BF16, "ub")
        vln = arena.take([128, 512], BF16, "vln")
        mlpb = arena.take([128, 512], BF16, "mlpb")
        qf = arena.take([128, 512], F32, "qf")
        qt1 = arena.take([128, 256], F32, "qt1")
        qt2 = arena.take([128, 256], F32, "qt2")
        qb = arena.take([128, 512], BF16, "qb")
        kb = arena.take([128, 128], BF16, "kb")
        cosT = arena.take([128, 32, 32], F32, "cosT")
        sinT = arena.take([128, 32, 32], F32, "sinT")
        gqB = arena.take([128, 64], F32, "gqB")
        gkB = arena.take([128, 64], F32, "gkB")
        gsgB = arena.take([128, 512], F32, "gsgB")
        bsgB = arena.take([128, 512], F32, "bsgB")
        modc = [arena.take([128, D], F32, "modc%d" % i) for i in range(3)]

        DM("pool", w_in_sb, w_in_sb.ap, w_in[l].rearrange("p (k n) -> p k n", k=8), writes=[w_in_sb])
        DM("pool", w_out_sb, w_out_sb.ap, w_out[l].rearrange("p (k n) -> p k n", k=8), writes=[w_out_sb])
        DM("pool", wsT, wsT.ap, w_s[l].rearrange("p (h q) -> p h q", h=8), writes=[wsT])
        O("pool", "tensor_scalar", wsT.ap, wsT.ap, 0.5, None, op0=ALU.mult, reads=[wsT], writes=[wsT])
        if stop == 1.1:
            return finish()
        DM("sp", bsT, bsT.ap, b_s[l], writes=[bsT])
        O("pool", "tensor_scalar", bsB.ap, bsT.ap.unsqueeze(2).to_broadcast([128, 8, 64]), 0.5, None, op0=ALU.mult,
          reads=[bsT], writes=[bsB])
        DM("sp", cosT, cosT.ap, cos_in.rearrange("p (a b) -> p a b", a=32), writes=[cosT])
        DM("sp", sinT, sinT.ap, sin_in.rearrange("p (a b) -> p a b", a=32), writes=[sinT])
        DM("sp", gqB, gqB.ap, g_q[l, :].partition_broadcast(128), writes=[gqB])
        DM("sp", gkB, gkB.ap, g_k[l, :].partition_broadcast(128), writes=[gkB])
        DM("sp", gsgB, gsgB.ap, g_sg[l, :].partition_broadcast(128), writes=[gsgB])
        DM("sp", bsgB, bsgB.ap, b_sg[l, :].partition_broadcast(128), writes=[bsgB])
        if stop == 1.2:
            return finish()
        for QT_ in (QT, QT2):
            O("pool", "memset", QT_.ap[64:128, 0:8:2, :], 0.0, writes=[QT_])
            O("pool", "memset", QT_.ap[0:64, 1:8:2, :], 0.0, writes=[QT_])
        O("pool", "memset", Vaug.ap[:, :, 0, 64:128], 1.0, writes=[Vaug])
        O("pool", "memset", Vaug.ap[:, :, 1, 0:64], 1.0, writes=[Vaug])
        if stop == 1.3:
            return finish()
        load_mods(modc, l, 1, 0, g_pre_mix, g_post_mix)
        if stop == 1.4:
            return finish()
        load_mods(modx, l, 0, 0, g_pre_mix, g_post_mix)

        if stop <= 2:
            return finish()

        def rope(src_ap, nh, lt, dst_ap, rbufs, wbufs):
            s5 = src_ap.rearrange("p (h a b f) -> p h a b f", h=nh, a=2, b=2)
            d5 = dst_ap.rearrange("p (h a b f) -> p h a b f", h=nh, a=2, b=2)
            x1 = s5[:, :, :, 0, :]
            x2 = s5[:, :, :, 1, :]
            cb = cosT.ap[:, lt, :].rearrange("p (a f) -> p a f", a=2).unsqueeze(1).to_broadcast([128, nh, 2, 16])
            sbb = sinT.ap[:, lt, :].rearrange("p (a f) -> p a f", a=2).unsqueeze(1).to_broadcast([128, nh, 2, 16])
            n = nh * 32
            a1 = qt1.ap[:, 0:n].rearrange("p (h a f) -> p h a f", h=nh, a=2)
            a2 = qt2.ap[:, 0:n].rearrange("p (h a f) -> p h a f", h=nh, a=2)
            O("dve", "tensor_tensor", a1, x1, cb, op=ALU.mult, reads=rbufs + [cosT], writes=[qt1])
            O("dve", "tensor_tensor", a2, x2, sbb, op=ALU.mult, reads=rbufs + [sinT], writes=[qt2])
            O("dve", "tensor_tensor", d5[:, :, :, 0, :], a1, a2, op=ALU.subtract, reads=[qt1, qt2], writes=wbufs)
            O("dve", "tensor_tensor", a1, x1, sbb, op=ALU.mult, reads=rbufs + [sinT, qt1], writes=[qt1])
            O("dve", "tensor_tensor", a2, x2, cb, op=ALU.mult, reads=rbufs + [cosT, qt2], writes=[qt2])
            O("dve", "tensor_tensor", d5[:, :, :, 1, :], a1, a2, op=ALU.add, reads=[qt1, qt2], writes=wbufs)

        def head_norm(src_ap, src_bufs, nh, gB_, dstf):
            st = next_stat()
            n = nh * 64
            d = dstf.ap[:, 0:n]
            d3 = d.rearrange("p (h f) -> p h f", h=nh)
            O("act", "activation", out=d, in_=src_ap, func=AF.Square, reads=src_bufs, writes=[dstf])
            O("dve", "tensor_reduce", st.ap[:, 0:nh], d3, axis=AX.X, op=ALU.add, reads=[dstf], writes=[st])
            rsqrt_small(st, 0, nh, 1.0 / 64, EPS)
            O("dve", "tensor_tensor", d3, src_ap.rearrange("p (h f) -> p h f", h=nh),
              st.ap[:, 0:nh].unsqueeze(2).to_broadcast([128, nh, 64]), op=ALU.mult,
              reads=src_bufs + [st, dstf], writes=[dstf])
            O("dve", "tensor_tensor", d3, d3, gB_.ap.unsqueeze(1).to_broadcast([128, nh, 64]), op=ALU.mult,
              reads=[dstf, gB_], writes=[dstf])

        kvbanks = [bA[3], bA[2]]

        def front_A(t):
            isctx = t < 2
            M = modc if isctx else modx
            xb = next_xa()
            sap, sres = tile_src(l, t)
            load_tile(xb, sap, sres)
            prenorm_to_hT(xb, M[0], M[1], 0)
            bk = kvbanks[t % 2]
            for k in range(8):
                O("pe", "matmul", bk.ap[:, 0:256], lhsT=hT.ap[:, k, 0:128], rhs=w_in_sb.ap[:, k, 512:768],
                  start=(k == 0), stop=(k == 7), reads=[hT, w_in_sb], writes=[bk])

        def back_A(t):
            isctx = t < 2
            bk = kvbanks[t % 2]
            head_norm(bk.ap[:, 0:128], [bk], 2, gkB, qf)
            if isctx:
                O("dve", "tensor_copy", kb.ap, qf.ap[:, 0:128], reads=[qf], writes=[kb])
            else:
                rope(qf.ap[:, 0:128], 2, t - 2, kb.ap, [qf], [kb])
            O("act", "copy", Vaug.ap[:, t, 0, 0:64], bk.ap[:, 128:192], reads=[bk], writes=[Vaug])
            O("act", "copy", Vaug.ap[:, t, 1, 64:128], bk.ap[:, 192:256], reads=[bk], writes=[Vaug])
            O("pe", "transpose", bB[0].ap.bitcast(BF16)[:, 0:128], kb.ap, idb.ap, reads=[kb, idb], writes=[bB[0]])
            O("act", "copy", KT.ap[:, t * 128:(t + 1) * 128], bB[0].ap.bitcast(BF16)[:, 0:128], reads=[bB[0]], writes=[KT])

        front_A(0)
        for t in range(NT_TILES):
            if t + 1 < NT_TILES:
                front_A(t + 1)
            back_A(t)
        if stop <= 3:
            return finish()
        hTs = [hT, hT2]
        QTs = [QT, QT2]
        mixTs = [mixT, mixT2]
        sc_banks = [bB[0], bB[1]]
        o_banks = [bB[2], bA[3]]
        LA = 1

        def B1_tile(slot, isctx, t, i):
            M = modc if isctx else modx
            hTc, QTc, mixTc = hTs[slot], QTs[slot], mixTs[slot]
            c0 = i * 128
            xb = next_xa()
            sap, sres = tile_src(l, t)
            load_tile(xb, sap, sres)
            yield
            st = next_stat()
            h = hb[cnt["hb"] % 2]
            cnt["hb"] += 1
            O("act", "activation", out=sq_junk.ap, in_=xb.ap, func=AF.Square, accum_out=st.ap[:, 0:1],
              reads=[xb], writes=[sq_junk, st])
            yield
            yield
            rsqrt_small(st, 0, 1, 1.0 / D, EPS, mode="dve")
            O("dve", "scalar_tensor_tensor", out=xb.ap, in0=xb.ap, scalar=st.ap[:, 0:1], in1=M[0].ap,
              op0=ALU.mult, op1=ALU.mult, reads=[xb, st, M[0]], writes=[xb])
            O("dve", "tensor_tensor", h.ap, xb.ap, M[1].ap, op=ALU.add, reads=[xb, M[1]], writes=[h])
            for _ in range(5):
                yield
            for k in range(8):
                O("pe", "transpose", bT.ap[:, k * 128:(k + 1) * 128], h.ap[:, k * 128:(k + 1) * 128], idb.ap,
                  reads=[h, idb], writes=[bT])
            yield
            yield
            O("act", "copy", hTc.ap[:, :, c0:c0 + 128], bT.ap.rearrange("p (k c) -> p k c", k=8), reads=[bT], writes=[hTc])
            yield
            yield
            for (bank, col0) in ((bA[0], 0), (bA[1], 768), (bA[2], 1280)):
                for k in range(8):
                    O("pe", "matmul", bank.ap, lhsT=hTc.ap[:, k, c0:c0 + 128], rhs=w_in_sb.ap[:, k, col0:col0 + 512],
                      start=(k == 0), stop=(k == 7), reads=[hTc, w_in_sb], writes=[bank])
                yield
            yield
            z = psA[:, 512:1536]
            zb = [bA[1], bA[2]]
            stq = next_stat()
            q3 = qf.ap.rearrange("p (h f) -> p h f", h=8)
            O("act", "activation", out=qf.ap, in_=bA[0].ap, func=AF.Square, reads=[bA[0]], writes=[qf])
            O("act", "activation", out=gA.ap, in_=z, func=AF.Square, reads=zb, writes=[gA])
            yield
            yield
            yield
            O("dve", "tensor_reduce", stq.ap[:, 0:8], q3, axis=AX.X, op=ALU.add, reads=[qf], writes=[stq])
            rsqrt_small(stq, 0, 8, 1.0 / 64, EPS, mode="dve")
            O("dve", "tensor_tensor", q3, bA[0].ap.rearrange("p (h f) -> p h f", h=8),
              stq.ap[:, 0:8].unsqueeze(2).to_broadcast([128, 8, 64]), op=ALU.mult, reads=[bA[0], stq, qf], writes=[qf])
            O("dve", "tensor_tensor", q3, q3, gqB.ap.unsqueeze(1).to_broadcast([128, 8, 64]), op=ALU.mult,
              reads=[qf, gqB], writes=[qf])
            O("dve", "tensor_scalar", gA.ap, gA.ap, 0.044715, 1.0, op0=ALU.mult, op1=ALU.add, reads=[gA], writes=[gA])
            O("dve", "tensor_tensor", gA.ap, gA.ap, z, op=ALU.mult, reads=[gA] + zb, writes=[gA])
            for _ in range(6):
                yield
            O("act", "activation", out=gB.ap, in_=gA.ap, func=AF.Tanh, scale=0.7978845608028654, reads=[gA], writes=[gB])
            if isctx:
                O("dve", "tensor_copy", qb.ap, qf.ap, reads=[qf], writes=[qb])
            else:
                rope(qf.ap, 8, t - 2, qb.ap, [qf], [qb])
            for _ in range(4):
                yield
            for j in range(4):
                O("pe", "transpose", bT.ap[:, j * 128:(j + 1) * 128], qb.ap[:, j * 128:(j + 1) * 128], idb.ap,
                  reads=[qb, idb], writes=[bT])
            vp = gA.ap[:, 512:1024]
            O("dve", "scalar_tensor_tensor", out=ub.ap, in0=gB.ap[:, 0:512], scalar=1.0, in1=z[:, 0:512], op0=ALU.add, op1=ALU.mult,
              reads=[gB, bA[1]], writes=[ub])
            O("dve", "scalar_tensor_tensor", out=vp, in0=gB.ap[:, 512:1024], scalar=1.0, in1=z[:, 512:1024], op0=ALU.add, op1=ALU.mult,
              reads=[gB, bA[2], gA], writes=[gA])
            stv = next_stat()
            O("dve", "bn_stats", stv.ap[:, 0:6], vp, reads=[gA], writes=[stv])
            O("dve", "bn_aggr", stv.ap[:, 8:10], stv.ap[:, 0:6], reads=[stv], writes=[stv])
            rsqrt_small(stv, 9, 10, 1.0, 4 * EPS, mode="dve")
            O("dve", "tensor_scalar", vp, vp, stv.ap[:, 8:9], stv.ap[:, 9:10], op0=ALU.subtract, op1=ALU.mult,
              reads=[gA, stv], writes=[gA])
            O("dve", "tensor_tensor", vp, vp, gsgB.ap, op=ALU.mult, reads=[gA, gsgB], writes=[gA])
            yield
            O("act", "copy", QTc.ap[0:64, 0:8:2, c0:c0 + 128], bT.ap[0:64, 0:512].rearrange("p (j c) -> p j c", j=4),
              reads=[bT], writes=[QTc])
            O("act", "copy", QTc.ap[64:128, 1:8:2, c0:c0 + 128], bT.ap[64:128, 0:512].rearrange("p (j c) -> p j c", j=4),
              reads=[bT], writes=[QTc])
            for _ in range(4):
                yield
            O("pool", "tensor_tensor", vln.ap, vp, bsgB.ap, op=ALU.add, reads=[gA, bsgB], writes=[vln])
            yield
            yield
            for h_ in range(8):
                O("pe", "matmul", bA[0].ap[:, h_ * 64:(h_ + 1) * 64], lhsT=wsT.ap[:, h_, :], rhs=vln.ap[:, h_ * 64:(h_ + 1) * 64],
                  start=True, stop=True, reads=[wsT, vln], writes=[bA[0]])
            yield
            yield
            O("dve", "tensor_tensor", gB.ap[:, 0:512], bA[0].ap, bsB.ap.rearrange("p h f -> p (h f)"), op=ALU.add,
              reads=[bA[0], bsB], writes=[gB])
            yield
            yield
            O("pool", "tensor_tensor", mlpb.ap, gB.ap[:, 0:512], ub.ap, op=ALU.mult, reads=[gB, ub], writes=[mlpb])
            yield
            yield
            for j in range(4):
                O("pe", "transpose", bT.ap[:, 512 + j * 128:512 + (j + 1) * 128], mlpb.ap[:, j * 128:(j + 1) * 128], idb.ap,
                  reads=[mlpb, idb], writes=[bT])
            yield
            yield
            O("act", "copy", mixTc.ap[:, 4:8, c0:c0 + 128], bT.ap[:, 512:1024].rearrange("p (j c) -> p j c", j=4),
              reads=[bT], writes=[mixTc])

        def B2_head(slot, hi, NTk, nk, filler=None):
            QTc, mixTc = QTs[slot], mixTs[slot]
            j, hh = hi // 2, hi % 2
            ob = o_banks[hi % 2]
            rdb = rd[hi % 2]
            r0 = hh * 64

            def qk(kt):
                sb_ = sc_banks[kt % 2]
                O("pe", "matmul", sb_.ap[:, 0:NTk], lhsT=KT.ap[:, kt * 128:(kt + 1) * 128],
                  rhs=QTc.ap[:, 2 * j + hh, 0:NTk], start=True, stop=True, reads=[KT, QTc], writes=[sb_])

            def ex(kt):
                sb_ = sc_banks[kt % 2]
                pt = PT[kt % 4]
                O("act", "activation", out=pt.ap[:, 0:NTk], in_=sb_.ap[:, 0:NTk], func=AF.Exp, scale=0.125,
                  reads=[sb_], writes=[pt])

            def pv(kt):
                pt = PT[kt % 4]
                O("pe", "matmul", ob.ap[:, 0:NTk], lhsT=Vaug.ap[:, kt, hh, :], rhs=pt.ap[:, 0:NTk],
                  start=(kt == 0), stop=(kt == nk - 1), reads=[Vaug, pt], writes=[ob])

            for kt in range(min(LA, nk)):
                qk(kt)
            for kt in range(nk):
                ex(kt)
                if kt + LA < nk:
                    qk(kt + LA)
                pv(kt)
                if filler is not None:
                    try:
                        next(filler)
                    except StopIteration:
                        filler = None
            if filler is not None:
                for _ in filler:
                    pass
            d0 = 64 - r0
            O("dve", "reciprocal", rdb.ap[r0:r0 + 64, 0:NTk], ob.ap[d0:d0 + 64, 0:NTk], reads=[ob], writes=[rdb])
            O("dve", "tensor_tensor", mixTc.ap[r0:r0 + 64, j, 0:NTk], ob.ap[r0:r0 + 64, 0:NTk], rdb.ap[r0:r0 + 64, 0:NTk],
              op=ALU.mult, reads=[ob, rdb], writes=[mixTc])

        def B3_tile(slot, isctx, t, i):
            M = modc if isctx else modx
            mixTc = mixTs[slot]
            c0 = i * 128
            pw = [bA[1], bA[2]]
            pw_ap = psA[:, 512:1536]
            sap, sres = tile_src(l, t)
            load_tile(xr, sap, sres)
            for half in range(2):
                for c in range(8):
                    O("pe", "matmul", pw[half].ap, lhsT=mixTc.ap[:, c, c0:c0 + 128], rhs=w_out_sb.ap[:, c, half * 512:(half + 1) * 512],
                      start=(c == 0), stop=(c == 7), reads=[mixTc, w_out_sb], writes=[pw[half]])
                yield
                yield
            yield
            st = next_stat()
            O("act", "activation", out=sq_junk.ap, in_=pw_ap, func=AF.Square, accum_out=st.ap[:, 0:1],
              reads=pw, writes=[sq_junk, st])
            yield
            yield
            yield
            rsqrt_small(st, 0, 1, 1.0 / D, EPS, mode="dve")
            o = next_xa()
            O("dve", "scalar_tensor_tensor", out=o.ap, in0=pw_ap, scalar=st.ap[:, 0:1], in1=M[2].ap, op0=ALU.mult, op1=ALU.mult,
              reads=pw + [st, M[2]], writes=[o])
            for _ in range(8):
                yield
            O("pool", "tensor_tensor", o.ap, o.ap, xr.ap, op=ALU.add, reads=[o, xr], writes=[o])
            for _ in range(5):
                yield
            DM("sp", o, xs[t * 128:(t + 1) * 128, :], o.ap, reads=[o], writes=[xs_res[t]])

        def drain(gen):
            for _ in gen:
                pass

        if not last:
            for i, t in enumerate([0, 1]):
                drain(B1_tile(1, True, t, i))
            for hi in range(8):
                B2_head(1, hi, 256, 2)
            for i, t in enumerate([0, 1]):
                drain(B3_tile(1, True, t, i))
        G = [[2 + 4 * g + i for i in range(4)] for g in range(8)]
        for i, t in enumerate(G[0]):
            drain(B1_tile(0, False, t, i))
        for g in range(8):
            for hi in range(8):
                filler = None
                if hi < 4 and g >= 1:
                    filler = B3_tile((g - 1) % 2, False, G[g - 1][hi], hi)
                if hi >= 4 and g + 1 < 8:
                    filler = B1_tile((g + 1) % 2, False, G[g + 1][hi - 4], hi - 4)
                B2_head(g % 2, hi, 512, NT_TILES, filler)
        for i, t in enumerate(G[7]):
            drain(B3_tile(7 % 2, False, t, i))

        if stop <= 4:
            return finish()
        p.barrier(new_epoch=False)
        arena.reset()
        wgu = [arena.take([128, 8, 256], BF16, "wgu%d" % j) for j in range(NJ)]
        wo = [arena.take([128, D], BF16, "wo%d" % j) for j in range(NJ)]
        actT = arena.take([128, NJ, 512], BF16, "actT")
        sgt = [arena.take([128, 512], F32, "sgt%d" % i) for i in range(2)]
        for j in range(NJ):
            DM("pool", wgu[j], wgu[j].ap, w_gu[l, j].rearrange("p (k n) -> p k n", k=8), writes=[wgu[j]])
        for j in range(NJ):
            DM("pool", wo[j], wo[j].ap, w_o[l][:, j * D:(j + 1) * D], writes=[wo[j]])
        cgroups = []
        if not last:
            cgroups.append((True, [0, 1]))
        for g in range(8):
            cgroups.append((False, [2 + 4 * g + i for i in range(4)]))
        cur_stream = [None]
        hTs = [hT, hT2]

        def c_prenorm(gi):
            isctx, tiles = cgroups[gi]
            r = 1 if isctx else 0
            if cur_stream[0] != r:
                load_mods(modx, l, r, 3, g_pre_ffn, g_post_ffn)
                cur_stream[0] = r
            for i, t in enumerate(tiles):
                xb = next_xa()
                load_tile(xb, xs[t * 128:(t + 1) * 128, :], xs_res[t])
                prenorm_to_hT(xb, modx[0], modx[1], i * 128, hT=hTs[gi % 2])

        done_pre = set()
        for gi, (isctx, tiles) in enumerate(cgroups):
            if gi not in done_pre:
                c_prenorm(gi)
                done_pre.add(gi)
            hTc = hTs[gi % 2]
            NTk = 128 * len(tiles)
            gub = [(bA[0], bA[1]), (bB[0], bB[1])]
            for j in range(NJ):
                pg, pu = gub[j % 2]
                for k in range(8):
                    O("pe", "matmul", pg.ap[:, 0:NTk], lhsT=wgu[j].ap[:, k, 0:128], rhs=hTc.ap[:, k, 0:NTk],
                      start=(k == 0), stop=(k == 7), reads=[wgu[j], hTc], writes=[pg])
                for k in range(8):
                    O("pe", "matmul", pu.ap[:, 0:NTk], lhsT=wgu[j].ap[:, k, 128:256], rhs=hTc.ap[:, k, 0:NTk],
                      start=(k == 0), stop=(k == 7), reads=[wgu[j], hTc], writes=[pu])
                sg_ = sgt[j % 2]
                O("act", "activation", out=sg_.ap[:, 0:NTk], in_=pg.ap[:, 0:NTk], func=AF.Silu, reads=[pg], writes=[sg_])
                O("dve", "tensor_tensor", actT.ap[:, j, 0:NTk], sg_.ap[:, 0:NTk], pu.ap[:, 0:NTk], op=ALU.mult,
                  reads=[sg_, pu], writes=[actT])
            if gi + 1 < len(cgroups) and cgroups[gi + 1][0] == isctx:
                c_prenorm(gi + 1)
                done_pre.add(gi + 1)
            for i, t in enumerate(tiles):
                c0 = i * 128
                pw = [bA[2], bA[3]]
                for half in range(2):
                    for j in range(NJ):
                        O("pe", "matmul", pw[half].ap, lhsT=actT.ap[:, j, c0:c0 + 128], rhs=wo[j].ap[:, half * 512:(half + 1) * 512],
                          start=(j == 0), stop=(j == NJ - 1), reads=[actT, wo[j]], writes=[pw[half]])
                if last and not isctx:
                    dst_ap, dst_res = y_out[(t - 2) * 128:(t - 1) * 128, :], y_res[t - 2]
                else:
                    dst_ap, dst_res = xs[t * 128:(t + 1) * 128, :], xs_res[t]
                postnorm_store(pw, psA[:, 1024:2048], modx[2], xs[t * 128:(t + 1) * 128, :], xs_res[t], dst_ap, dst_res)
        p.barrier()

    return finish()


def _rope_tables():
    n = SEQ
    rows = n // 64
    pos_row = np.repeat(np.arange(rows, dtype=np.float32), 64)
    pos_col = np.tile(np.arange(64, dtype=np.float32), rows)
    inv = (10000.0 ** (-np.arange(0, 32, 2, dtype=np.float32) / 32)).astype(np.float32)
    ang = np.concatenate([pos_row[:, None] * inv, pos_col[:, None] * inv], axis=-1).astype(np.float32)
    cos = np.cos(ang).astype(np.float32)
    sin = np.sin(ang).astype(np.float32)
    cos = np.ascontiguousarray(cos.reshape(32, 128, 32).transpose(1, 0, 2)).reshape(128, 32 * 32)
    sin = np.ascontiguousarray(sin.reshape(32, 128, 32).transpose(1, 0, 2)).reshape(128, 32 * 32)
    return cos, sin


def prep_shared(inputs, L):
    f = lambda a: np.ascontiguousarray(np.asarray(a, dtype=np.float32))
    sh = {}
    wm = f(inputs["w_mod"])[:L]
    sh["w_mod"] = f(wm.reshape(L, 8, 128, 12, 512).transpose(0, 3, 2, 1, 4)).reshape(L, 12, 128, 8 * 512)
    sh["b_mod"] = f(inputs["b_mod"])[:L]
    for k in ("g_pre_mix", "g_post_mix", "g_pre_ffn", "g_post_ffn", "g_q", "g_k", "g_sg", "b_sg"):
        sh[k] = f(inputs[k])[:L]
    wi = f(inputs["w_in"])[:L]
    qcols = np.concatenate([np.arange(h * 64, (h + 1) * 64) for h in QPERM])
    cols = np.concatenate([qcols, np.arange(512, INW)])
    wi = wi[:, :, cols]
    sh["w_in"] = f(wi.reshape(L, 8, 128, INW).transpose(0, 2, 1, 3)).reshape(L, 128, 8 * INW)
    ws = f(inputs["w_s"])[:L]
    sh["w_s"] = f(ws.transpose(0, 3, 1, 2)).reshape(L, 128, 8 * 128)
    sh["b_s"] = f(f(inputs["b_s"])[:L].transpose(0, 2, 1))
    wo_ = f(inputs["w_out"])[:L]
    rows = np.concatenate([qcols, np.arange(512, 1024)])
    wo_ = wo_[:, rows, :]
    sh["w_out"] = f(wo_.reshape(L, 8, 128, D).transpose(0, 2, 1, 3)).reshape(L, 128, 8 * D)
    wf = f(inputs["w_ffn_in"])[:L]
    gate = wf[:, :, :HID].reshape(L, 8, 128, NJ, 128)
    up = wf[:, :, HID:].reshape(L, 8, 128, NJ, 128)
    gu = np.concatenate([gate, up], axis=-1)
    sh["w_gu"] = f(gu.transpose(0, 3, 2, 1, 4)).reshape(L, NJ, 128, 8 * 256)
    wfo = f(inputs["w_ffn_out"])[:L]
    sh["w_o"] = f(wfo.reshape(L, NJ, 128, D).transpose(0, 2, 1, 3)).reshape(L, 128, NJ * D)
    sh["ident"] = np.eye(128, dtype=np.float32)
    cos, sin = _rope_tables()
    sh["rope_cos"] = cos
    sh["rope_sin"] = sin
    sh["c_ctx"] = f(f(inputs["c_ctx"]).reshape(8, 128).T)
    return sh


def prep_core(inputs, b):
    f = lambda a: np.ascontiguousarray(np.asarray(a, dtype=np.float32))
    return {
        "x": f(inputs["x"][b]),
        "ctx": f(inputs["ctx"][b]),
        "c": f(f(inputs["c"][b]).reshape(8, 128).T),
    }


_NC_CACHE = {}


def run(inputs, n_layers=DEPTH, cores=None, stop=99):
    if cores is None:
        cores = list(range(8))
    if (n_layers, stop) not in _NC_CACHE:
        _NC_CACHE[(n_layers, stop)] = build(n_layers, stop)
    nc = _NC_CACHE[(n_layers, stop)]
    sh = prep_shared(inputs, n_layers)
    in_maps = []
    for b in cores:
        m = dict(sh)
        m.update(prep_core(inputs, b))
        in_maps.append(m)
    res = run_bass_kernel_spmd(nc, in_maps, core_ids=list(range(len(cores))))
    return np.stack([np.asarray(r["y"], dtype=np.float32) for r in res.results], axis=0)


def kernel(**inputs):
    return run(inputs, DEPTH)
```

```python
import os
import numpy as np
import concourse.bass as bass
import concourse.mybir as mybir
from concourse.bass_utils import run_bass_kernel_spmd

F32 = mybir.dt.float32
BF16 = mybir.dt.bfloat16
I32 = mybir.dt.int32
AF = mybir.ActivationFunctionType
ALU = mybir.AluOpType
AX = mybir.AxisListType

D = 1024
SEQ = 4096
CTX = 256
NTOK = SEQ + CTX
NT_TILES = NTOK // 128
DEPTH = 4
HID = 2816
NJ = HID // 128
INW = 1792
EPS = 1e-6
QPERM = [0, 4, 1, 5, 2, 6, 3, 7]


class Res:
    __slots__ = ("w", "r", "name", "sem", "excl")

    def __init__(self, name=""):
        self.w = None
        self.r = []
        self.name = name
        self.sem = None
        self.excl = False


class Buf(Res):
    __slots__ = ("ap",)

    def __init__(self, ap, name=""):
        Res.__init__(self, name)
        self.ap = ap


class Prog:
    ENGS = ("pe", "act", "dve", "pool", "sp")

    def __init__(self, nc):
        self.nc = nc
        self.streams = {e: [] for e in self.ENGS}
        self.sems = {}
        self.cnt = {}
        self.cur = {}
        self.epoch = 0
        self.dma_keys = {}
        self._new_epoch()
        self.waited = {e: {} for e in self.ENGS}
        self.ndma = 0
        self.nops = 0

    def _new_epoch(self):
        for e in self.ENGS:
            key = "%s#%d" % (e, self.epoch)
            self.sems[key] = self.nc.alloc_semaphore("s_%s_%d" % (e, self.epoch))
            self.cnt[key] = 0
            self.cur[e] = key
        self.epoch += 1

    def _need(self, eng, deps):
        best = {}
        for (k, v) in deps:
            if v > best.get(k, 0):
                best[k] = v
        out = []
        w = self.waited[eng]
        for k, v in best.items():
            if eng == "pe" and k.startswith("pe#"):
                continue
            if w.get(k, 0) < v:
                w[k] = v
                out.append((k, v))
        return out

    @staticmethod
    def _collect(reads, writes):
        deps = []
        for r in reads:
            if r.w is not None:
                deps.append(r.w)
        for wr in writes:
            if wr.w is not None:
                deps.append(wr.w)
            deps.extend(wr.r)
        return deps

    def _mark(self, me, reads, writes):
        for r in reads:
            r.r.append(me)
            if len(r.r) > 64:
                best = {}
                for (k, v) in r.r:
                    if v > best.get(k, 0):
                        best[k] = v
                r.r = list(best.items())
        for wr in writes:
            wr.w = me
            wr.r = []

    def op(self, eng, method, *args, reads=(), writes=(), **kw):
        ex = [r for r in reads if r.excl and r not in writes]
        if ex:
            writes = list(writes) + ex
        waits = self._need(eng, self._collect(reads, writes))
        key = self.cur[eng]
        self.cnt[key] += 1
        me = (key, self.cnt[key])
        self.streams[eng].append((waits, (method, args, kw), (key, 1)))
        self._mark(me, reads, writes)
        self.nops += 1
        return me

    def dma_sem_for(self, res):
        key = "dma:" + res.name
        if key not in self.sems:
            self.ndma += 1
            self.sems[key] = self.nc.alloc_semaphore("d_%d" % self.ndma)
            self.cnt[key] = 0
        return key

    def dma(self, queue, semres, out, in_, reads=(), writes=()):
        semkey = self.dma_sem_for(semres)
        waits = self._need(queue, self._collect(reads, writes))
        self.cnt[semkey] += 16
        me = (semkey, self.cnt[semkey])
        self.streams[queue].append((waits, ("dma_start", (), dict(out=out, in_=in_)), (semkey, 16)))
        self._mark(me, reads, writes)
        self.nops += 1
        return me

    def barrier(self, new_epoch=True):
        allv = [(k, v) for k, v in self.cnt.items() if v > 0]
        for e in self.ENGS:
            waits = self._need(e, allv)
            if waits:
                self.streams[e].append((waits, None, None))
        if new_epoch:
            self._new_epoch()

    def emit(self):
        nc = self.nc
        sems = self.sems
        streams = self.streams

        waited = {}
        for ename in self.ENGS:
            for (waits, fn, inc) in streams[ename]:
                for (k, v) in waits:
                    waited.setdefault(k, set()).add(v)
        remap = {}
        for k, vals in waited.items():
            if k.startswith("dma:"):
                continue
            remap[k] = {v: i + 1 for i, v in enumerate(sorted(vals))}
        seen = {}

        def run(e, ename):
            for (waits, fn, inc) in streams[ename]:
                for (k, v) in waits:
                    if k in remap:
                        e.wait_ge(sems[k], remap[k][v])
                    else:
                        e.wait_ge(sems[k], v)
                if fn is not None:
                    ins = getattr(e, fn[0])(*fn[1], **fn[2])
                    k = inc[0]
                    if k.startswith("dma:"):
                        ins.then_inc(sems[k], inc[1])
                    else:
                        seen[k] = seen.get(k, 0) + 1
                        if seen[k] in remap.get(k, ()):
                            ins.then_inc(sems[k], 1)

        with nc.Block() as block:
            @block.tensor
            def _(e):
                run(e, "pe")

            @block.scalar
            def _(e):
                run(e, "act")

            @block.vector
            def _(e):
                run(e, "dve")

            @block.gpsimd
            def _(e):
                run(e, "pool")

            @block.sync
            def _(e):
                run(e, "sp")


class Arena:
    def __init__(self, ap_bf16):
        self.ap = ap_bf16
        self.off = 0
        self.size = ap_bf16.shape[1]

    def reset(self):
        self.off = 0

    def take(self, shape, dtype, name=""):
        n = 1
        for s in shape[1:]:
            n *= s
        nel = n * (2 if dtype == F32 else 1)
        if self.off % 2:
            self.off += 1
        assert self.off + nel <= self.size, ("arena overflow", name, self.off, nel, self.size)
        v = self.ap[:, self.off:self.off + nel]
        self.off += nel
        if dtype == F32:
            v = v.bitcast(F32)
        if len(shape) == 3:
            v = v.rearrange("p (a b) -> p a b", a=shape[1])
        elif len(shape) == 4:
            v = v.rearrange("p (a b c) -> p a b c", a=shape[1], b=shape[2])
        return Buf(v, name)


def build(n_layers=DEPTH, stop=99):
    nc = bass.Bass("TRN2", target_bir_lowering=False)
    p = Prog(nc)
    O = p.op
    DM = p.dma
    L = n_layers

    def din(name, shape):
        return nc.dram_tensor(name, list(shape), F32, kind="ExternalInput").ap()

    x_in = din("x", [SEQ, D])
    ctx_in = din("ctx", [CTX, D])
    c_in = din("c", [128, 8])
    cc_in = din("c_ctx", [128, 8])
    w_mod = din("w_mod", [L, 12, 128, 8 * 512])
    b_mod = din("b_mod", [L, 6 * D])
    g_pre_mix = din("g_pre_mix", [L, D])
    g_post_mix = din("g_post_mix", [L, D])
    g_pre_ffn = din("g_pre_ffn", [L, D])
    g_post_ffn = din("g_post_ffn", [L, D])
    w_in = din("w_in", [L, 128, 8 * INW])
    g_q = din("g_q", [L, 64])
    g_k = din("g_k", [L, 64])
    g_sg = din("g_sg", [L, 512])
    b_sg = din("b_sg", [L, 512])
    w_s = din("w_s", [L, 128, 8 * 128])
    b_s = din("b_s", [L, 128, 8])
    w_out = din("w_out", [L, 128, 8 * D])
    w_gu = din("w_gu", [L, NJ, 128, 8 * 256])
    w_o = din("w_o", [L, 128, NJ * D])
    ident_in = din("ident", [128, 128])
    cos_in = din("rope_cos", [128, 32 * 32])
    sin_in = din("rope_sin", [128, 32 * 32])
    y_out = nc.dram_tensor("y", [SEQ, D], F32, kind="ExternalOutput").ap()
    xs = nc.dram_tensor("xs", [NTOK, D], F32, kind="Internal").ap()
    mod_d = nc.dram_tensor("mod_d", [L * 2, 6 * D], F32, kind="Internal").ap()

    xs_res = [Res("xs%d" % t) for t in range(NT_TILES)]
    y_res = [Res("y%d" % t) for t in range(32)]
    modd_res = Res("mod_d")

    def sb(name, shape, dtype):
        return Buf(nc.alloc_sbuf_tensor(name, list(shape), dtype).ap(), name)

    idb = sb("idb", [128, 128], BF16)
    xa = [sb("xa%d" % i, [128, D], F32) for i in range(2)]
    xr = sb("xr0", [128, D], F32)
    sq_junk = sb("sq_junk", [128, D], BF16)
    hb = [sb("hb%d" % i, [128, D], BF16) for i in range(2)]
    hT = sb("hT", [128, 8, 512], BF16)
    hT2 = sb("hT2", [128, 8, 512], BF16)
    modx = [sb("modx%d" % i, [128, D], F32) for i in range(3)]
    stat = [sb("stat%d" % i, [128, 16], F32) for i in range(4)]
    rs_y = [sb("rsy%d" % i, [128, 16], F32) for i in range(4)]
    rs_t = [sb("rst%d" % i, [128, 16], F32) for i in range(4)]
    one_i = sb("one_i", [128, 1], I32)
    magic_i = sb("magic_i", [128, 16], I32)
    arena = Arena(nc.alloc_sbuf_tensor("arena", [128, 79 * 1024], BF16).ap())

    psT = nc.alloc_psum_tensor("psT", [128, 1024], BF16).ap()
    psA = nc.alloc_psum_tensor("psA", [128, 2048], F32).ap()
    psBs = [nc.alloc_psum_tensor("psB%d" % i, [128, 512], F32).ap() for i in range(3)]
    bT = Buf(psT, "bT")
    bA = [Buf(psA[:, i * 512:(i + 1) * 512], "bA%d" % i) for i in range(4)]
    bB = [Buf(psBs[i], "bB%d" % i) for i in range(3)]
    bC = bA[3]
    for b_ in [bT] + bA + bB:
        b_.excl = True

    cnt = {"xa": 0, "hb": 0, "stat": 0}

    def next_xa():
        b = xa[cnt["xa"] % 2]
        cnt["xa"] += 1
        return b

    def next_stat():
        b = stat[cnt["stat"] % 4]
        cnt["stat"] += 1
        return b

    def load_tile(dst, src_ap, src_res):
        DM("sp", dst, dst.ap, src_ap, reads=[src_res] if src_res else [], writes=[dst])

    def rsqrt_small(st, lo, hi, scale, eps, mode="act"):
        n = hi - lo
        v = st.ap[:, lo:hi]
        O("dve", "tensor_scalar", v, v, scale, eps, op0=ALU.mult, op1=ALU.add, reads=[st], writes=[st])
        if mode == "act":
            O("act", "activation", out=v, in_=v, func=AF.Sqrt, reads=[st], writes=[st])
            O("dve", "reciprocal", v, v, reads=[st], writes=[st])
            return
        k = stat.index(st)
        yb, tb = rs_y[k], rs_t[k]
        y = yb.ap[:, 0:n]
        t1 = tb.ap[:, 0:n]
        yi = y.bitcast(I32)
        O("dve", "tensor_scalar", yi, v.bitcast(I32), one_i.ap[:, 0:1], None, op0=ALU.arith_shift_right,
          reads=[st, one_i], writes=[yb])
        O("dve", "tensor_tensor", yi, magic_i.ap[:, 0:n], yi, op=ALU.subtract, reads=[yb, magic_i], writes=[yb])
        NIT = 3
        for it_ in range(NIT):
            O("dve", "tensor_tensor", t1, v, y, op=ALU.mult, reads=[st, yb], writes=[tb])
            O("dve", "tensor_tensor", t1, t1, y, op=ALU.mult, reads=[tb, yb], writes=[tb])
            O("dve", "tensor_scalar", t1, t1, -0.5, 1.5, op0=ALU.mult, op1=ALU.add, reads=[tb], writes=[tb])
            if it_ < NIT - 1:
                O("dve", "tensor_tensor", y, y, t1, op=ALU.mult, reads=[yb, tb], writes=[yb])
            else:
                O("dve", "tensor_tensor", v, y, t1, op=ALU.mult, reads=[yb, tb], writes=[st])

    def prenorm_front(xbuf, G1, SH):
        st = next_stat()
        h = hb[cnt["hb"] % 2]
        cnt["hb"] += 1
        O("act", "activation", out=sq_junk.ap, in_=xbuf.ap, func=AF.Square, accum_out=st.ap[:, 0:1],
          reads=[xbuf], writes=[sq_junk, st])
        rsqrt_small(st, 0, 1, 1.0 / D, EPS)
        O("dve", "scalar_tensor_tensor", out=xbuf.ap, in0=xbuf.ap, scalar=st.ap[:, 0:1], in1=G1.ap,
          op0=ALU.mult, op1=ALU.mult, reads=[xbuf, st, G1], writes=[xbuf])
        O("dve", "tensor_tensor", h.ap, xbuf.ap, SH.ap, op=ALU.add, reads=[xbuf, SH], writes=[h])
        return h

    def prenorm_back(h, ncol0, hT=hT):
        for k in range(8):
            O("pe", "transpose", bT.ap[:, k * 128:(k + 1) * 128], h.ap[:, k * 128:(k + 1) * 128], idb.ap,
              reads=[h, idb], writes=[bT])
        O("act", "copy", hT.ap[:, :, ncol0:ncol0 + 128], bT.ap.rearrange("p (k c) -> p k c", k=8),
          reads=[bT], writes=[hT])

    def prenorm_to_hT(xbuf, G1, SH, ncol0, hT=hT):
        h = prenorm_front(xbuf, G1, SH)
        prenorm_back(h, ncol0, hT=hT)

    def load_mods(M, l, r, idx, gpre, gpost):
        row = l * 2 + r
        tmp = xr
        DM("sp", M[0], M[0].ap, mod_d[row, (idx + 1) * D:(idx + 2) * D].partition_broadcast(128), reads=[modd_res], writes=[M[0]])
        DM("sp", tmp, tmp.ap, gpre[l, :].partition_broadcast(128), writes=[tmp])
        O("dve", "scalar_tensor_tensor", out=M[0].ap, in0=M[0].ap, scalar=1.0, in1=tmp.ap, op0=ALU.add, op1=ALU.mult,
          reads=[M[0], tmp], writes=[M[0]])
        DM("sp", M[1], M[1].ap, mod_d[row, idx * D:(idx + 1) * D].partition_broadcast(128), reads=[modd_res], writes=[M[1]])
        DM("sp", M[2], M[2].ap, mod_d[row, (idx + 2) * D:(idx + 3) * D].partition_broadcast(128), reads=[modd_res], writes=[M[2]])
        DM("sp", tmp, tmp.ap, gpost[l, :].partition_broadcast(128), writes=[tmp])
        O("dve", "tensor_tensor", M[2].ap, M[2].ap, tmp.ap, op=ALU.mult, reads=[M[2], tmp], writes=[M[2]])

    def postnorm_store(pw_banks, pw_ap, GG, src_ap, src_res, dst_ap, dst_res):
        st = next_stat()
        xb = xr
        load_tile(xb, src_ap, src_res)
        O("act", "activation", out=sq_junk.ap, in_=pw_ap, func=AF.Square, accum_out=st.ap[:, 0:1],
          reads=pw_banks, writes=[sq_junk, st])
        rsqrt_small(st, 0, 1, 1.0 / D, EPS)
        o = next_xa()
        O("dve", "scalar_tensor_tensor", out=o.ap, in0=pw_ap, scalar=st.ap[:, 0:1], in1=GG.ap, op0=ALU.mult, op1=ALU.mult,
          reads=pw_banks + [st, GG], writes=[o])
        O("pool", "tensor_tensor", o.ap, o.ap, xb.ap, op=ALU.add, reads=[o, xb], writes=[o])
        DM("sp", o, dst_ap, o.ap, reads=[o], writes=[dst_res])

    def tile_src(l, t):
        if l == 0:
            if t < 2:
                return ctx_in[t * 128:(t + 1) * 128, :], None
            return x_in[(t - 2) * 128:(t - 1) * 128, :], None
        return xs[t * 128:(t + 1) * 128, :], xs_res[t]

    O("dve", "memset", one_i.ap, 1, writes=[one_i])
    O("dve", "memset", magic_i.ap, 0x5f3759df, writes=[magic_i])
    idf = xa[0]
    DM("sp", idf, idf.ap[:, 0:128], ident_in, writes=[idf])
    O("dve", "tensor_copy", idb.ap, idf.ap[:, 0:128], reads=[idf], writes=[idb])

    arena.reset()
    cst = arena.take([128, 8, 2], F32, "cst")
    craw = [arena.take([128, 8], F32, "craw%d" % i) for i in range(2)]
    ctmp = arena.take([128, 8], F32, "ctmp")
    NSTG = 4
    stage = [arena.take([128, 8, 512], F32, "stage%d" % i) for i in range(NSTG)]
    bmt = [arena.take([128, 512], F32, "bmt%d" % i) for i in range(2)]
    mrow = [arena.take([128, 512], F32, "mrow%d" % i) for i in range(2)]
    DM("sp", craw[0], craw[0].ap, c_in, writes=[craw[0]])
    DM("sp", craw[1], craw[1].ap, cc_in, writes=[craw[1]])
    for r in range(2):
        O("act", "activation", out=ctmp.ap, in_=craw[r].ap, func=AF.Tanh, scale=0.5, reads=[craw[r]], writes=[ctmp])
        O("dve", "scalar_tensor_tensor", out=ctmp.ap, in0=ctmp.ap, scalar=1.0, in1=craw[r].ap, op0=ALU.add, op1=ALU.mult,
          reads=[ctmp, craw[r]], writes=[ctmp])
        O("dve", "tensor_scalar", cst.ap[:, :, r], ctmp.ap, 0.5, None, op0=ALU.mult, reads=[ctmp], writes=[cst])
    it = 0
    for l in range(L):
        for j in range(12):
            sg = stage[it % NSTG]
            bm = bmt[it % 2]
            mr = mrow[it % 2]
            pb = bA[it % 2]
            DM("sp" if it % 2 == 0 else "act", sg, sg.ap, w_mod[l, j].rearrange("p (k n) -> p k n", k=8), writes=[sg])
            DM("sp", bm, bm.ap[0:2, :], b_mod[l, j * 512:(j + 1) * 512].partition_broadcast(2), writes=[bm])
            for k in range(8):
                O("pe", "matmul", pb.ap[0:2, :], lhsT=cst.ap[:, k, :], rhs=sg.ap[:, k, :], start=(k == 0), stop=(k == 7),
                  reads=[cst, sg], writes=[pb])
            O("dve", "tensor_tensor", mr.ap[0:2, :], pb.ap[0:2, :], bm.ap[0:2, :], op=ALU.add, reads=[pb, bm], writes=[mr])
            DM("sp", mr, mod_d[2 * l:2 * l + 2, j * 512:(j + 1) * 512], mr.ap[0:2, :], reads=[mr], writes=[modd_res])
            it += 1
    p.barrier()

    def finish():
        p.barrier(new_epoch=False)
        print("ops:", p.nops, "sems:", len(p.sems))
        p.emit()
        return nc

    if stop <= 1:
        return finish()

    for l in range(L):
        last = (l == L - 1)
        arena.reset()
        w_in_sb = arena.take([128, 8, INW], BF16, "w_in_sb")
        w_out_sb = arena.take([128, 8, D], BF16, "w_out_sb")
        wsT = arena.take([128, 8, 128], BF16, "wsT")
        bsB = arena.take([128, 8, 64], F32, "bsB")
        bsT = arena.take([128, 8], F32, "bsT")
        KT = arena.take([128, NTOK], BF16, "KT")
        Vaug = arena.take([128, NT_TILES, 2, 128], BF16, "Vaug")
        QT = arena.take([128, 8, 512], BF16, "QT")
        mixT = arena.take([128, 8, 512], BF16, "mixT")
        QT2 = arena.take([128, 8, 512], BF16, "QT2")
        mixT2 = arena.take([128, 8, 512], BF16, "mixT2")
        PT = [arena.take([128, 512], BF16, "PT%d" % i) for i in range(4)]
        rd = [arena.take([128, 512], F32, "rd%d" % i) for i in range(2)]
        gA = arena.take([128, 1024], F32, "gA")
        gB = arena.take([128, 1024], F32, "gB")
        ub = arena.take([128, 512], BF16, "ub")
        vln = arena.take([128, 512], BF16, "vln")
        mlpb = arena.take([128, 512], BF16, "mlpb")
        qf = arena.take([128, 512], F32, "qf")
        qt1 = arena.take([128, 256], F32, "qt1")
        qt2 = arena.take([128, 256], F32, "qt2")
        qb = arena.take([128, 512], BF16, "qb")
        kb = arena.take([128, 128], BF16, "kb")
        cosT = arena.take([128, 32, 32], F32, "cosT")
        sinT = arena.take([128, 32, 32], F32, "sinT")
        gqB = arena.take([128, 64], F32, "gqB")
        gkB = arena.take([128, 64], F32, "gkB")
        gsgB = arena.take([128, 512], F32, "gsgB")
        bsgB = arena.take([128, 512], F32, "bsgB")
        modc = [arena.take([128, D], F32, "modc%d" % i) for i in range(3)]

        DM("pool", w_in_sb, w_in_sb.ap, w_in[l].rearrange("p (k n) -> p k n", k=8), writes=[w_in_sb])
        DM("pool", w_out_sb, w_out_sb.ap, w_out[l].rearrange("p (k n) -> p k n", k=8), writes=[w_out_sb])
        DM("pool", wsT, wsT.ap, w_s[l].rearrange("p (h q) -> p h q", h=8), writes=[wsT])
        O("pool", "tensor_scalar", wsT.ap, wsT.ap, 0.5, None, op0=ALU.mult, reads=[wsT], writes=[wsT])
        if stop == 1.1:
            return finish()
        cgrp = Res("constgrp")
        DM("sp", cgrp, bsT.ap, b_s[l], writes=[bsT])
        DM("sp", cgrp, cosT.ap, cos_in.rearrange("p (a b) -> p a b", a=32), writes=[cosT])
        DM("sp", cgrp, sinT.ap, sin_in.rearrange("p (a b) -> p a b", a=32), writes=[sinT])
        DM("sp", cgrp, gqB.ap, g_q[l, :].partition_broadcast(128), writes=[gqB])
        DM("sp", cgrp, gkB.ap, g_k[l, :].partition_broadcast(128), writes=[gkB])
        DM("sp", cgrp, gsgB.ap, g_sg[l, :].partition_broadcast(128), writes=[gsgB])
        DM("sp", cgrp, bsgB.ap, b_sg[l, :].partition_broadcast(128), writes=[bsgB])
        O("dve", "memset", kb.ap, 0.0, writes=[kb, bsT, cosT, sinT, gqB, gkB, gsgB, bsgB])
        O("pool", "tensor_scalar", bsB.ap, bsT.ap.unsqueeze(2).to_broadcast([128, 8, 64]), 0.5, None, op0=ALU.mult,
          reads=[bsT], writes=[bsB])
        if stop == 1.2:
            return finish()
        for QT_ in (QT, QT2):
            O("pool", "memset", QT_.ap[64:128, 0:8:2, :], 0.0, writes=[QT_])
            O("pool", "memset", QT_.ap[0:64, 1:8:2, :], 0.0, writes=[QT_])
        O("pool", "memset", Vaug.ap[:, :, 0, 64:128], 1.0, writes=[Vaug])
        O("pool", "memset", Vaug.ap[:, :, 1, 0:64], 1.0, writes=[Vaug])
        if stop == 1.3:
            return finish()
        load_mods(modc, l, 1, 0, g_pre_mix, g_post_mix)
        if stop == 1.4:
            return finish()
        load_mods(modx, l, 0, 0, g_pre_mix, g_post_mix)

        if stop <= 2:
            return finish()

        def rope(src_ap, nh, lt, dst_ap, rbufs, wbufs):
            s5 = src_ap.rearrange("p (h a b f) -> p h a b f", h=nh, a=2, b=2)
            d5 = dst_ap.rearrange("p (h a b f) -> p h a b f", h=nh, a=2, b=2)
            x1 = s5[:, :, :, 0, :]
            x2 = s5[:, :, :, 1, :]
            cb = cosT.ap[:, lt, :].rearrange("p (a f) -> p a f", a=2).unsqueeze(1).to_broadcast([128, nh, 2, 16])
            sbb = sinT.ap[:, lt, :].rearrange("p (a f) -> p a f", a=2).unsqueeze(1).to_broadcast([128, nh, 2, 16])
            n = nh * 32
            a1 = qt1.ap[:, 0:n].rearrange("p (h a f) -> p h a f", h=nh, a=2)
            a2 = qt2.ap[:, 0:n].rearrange("p (h a f) -> p h a f", h=nh, a=2)
            O("dve", "tensor_tensor", a1, x1, cb, op=ALU.mult, reads=rbufs + [cosT], writes=[qt1])
            O("dve", "tensor_tensor", a2, x2, sbb, op=ALU.mult, reads=rbufs + [sinT], writes=[qt2])
            O("dve", "tensor_tensor", d5[:, :, :, 0, :], a1, a2, op=ALU.subtract, reads=[qt1, qt2], writes=wbufs)
            O("dve", "tensor_tensor", a1, x1, sbb, op=ALU.mult, reads=rbufs + [sinT, qt1], writes=[qt1])
            O("dve", "tensor_tensor", a2, x2, cb, op=ALU.mult, reads=rbufs + [cosT, qt2], writes=[qt2])
            O("dve", "tensor_tensor", d5[:, :, :, 1, :], a1, a2, op=ALU.add, reads=[qt1, qt2], writes=wbufs)

        def head_norm(src_ap, src_bufs, nh, gB_, dstf):
            st = next_stat()
            n = nh * 64
            d = dstf.ap[:, 0:n]
            d3 = d.rearrange("p (h f) -> p h f", h=nh)
            O("act", "activation", out=d, in_=src_ap, func=AF.Square, reads=src_bufs, writes=[dstf])
            O("dve", "tensor_reduce", st.ap[:, 0:nh], d3, axis=AX.X, op=ALU.add, reads=[dstf], writes=[st])
            rsqrt_small(st, 0, nh, 1.0 / 64, EPS)
            O("dve", "tensor_tensor", d3, src_ap.rearrange("p (h f) -> p h f", h=nh),
              st.ap[:, 0:nh].unsqueeze(2).to_broadcast([128, nh, 64]), op=ALU.mult,
              reads=src_bufs + [st, dstf], writes=[dstf])
            O("dve", "tensor_tensor", d3, d3, gB_.ap.unsqueeze(1).to_broadcast([128, nh, 64]), op=ALU.mult,
              reads=[dstf, gB_], writes=[dstf])

        kvbanks = [bA[3], bA[2]]

        def front_A(t):
            isctx = t < 2
            M = modc if isctx else modx
            xb = next_xa()
            sap, sres = tile_src(l, t)
            load_tile(xb, sap, sres)
            prenorm_to_hT(xb, M[0], M[1], 0)
            bk = kvbanks[t % 2]
            for k in range(8):
                O("pe", "matmul", bk.ap[:, 0:256], lhsT=hT.ap[:, k, 0:128], rhs=w_in_sb.ap[:, k, 512:768],
                  start=(k == 0), stop=(k == 7), reads=[hT, w_in_sb], writes=[bk])

        def back_A(t):
            isctx = t < 2
            bk = kvbanks[t % 2]
            head_norm(bk.ap[:, 0:128], [bk], 2, gkB, qf)
            if isctx:
                O("dve", "tensor_copy", kb.ap, qf.ap[:, 0:128], reads=[qf], writes=[kb])
            else:
                rope(qf.ap[:, 0:128], 2, t - 2, kb.ap, [qf], [kb])
            O("act", "copy", Vaug.ap[:, t, 0, 0:64], bk.ap[:, 128:192], reads=[bk], writes=[Vaug])
            O("act", "copy", Vaug.ap[:, t, 1, 64:128], bk.ap[:, 192:256], reads=[bk], writes=[Vaug])
            O("pe", "transpose", bB[0].ap.bitcast(BF16)[:, 0:128], kb.ap, idb.ap, reads=[kb, idb], writes=[bB[0]])
            O("act", "copy", KT.ap[:, t * 128:(t + 1) * 128], bB[0].ap.bitcast(BF16)[:, 0:128], reads=[bB[0]], writes=[KT])

        front_A(0)
        for t in range(NT_TILES):
            if t + 1 < NT_TILES:
                front_A(t + 1)
            back_A(t)
        if stop <= 3:
            return finish()
        hTs = [hT, hT2]
        QTs = [QT, QT2]
        mixTs = [mixT, mixT2]
        sc_banks = [bB[0], bB[1]]
        o_banks = [bB[2], bA[3]]
        LA = 1

        def B1_tile(slot, isctx, t, i):
            M = modc if isctx else modx
            hTc, QTc, mixTc = hTs[slot], QTs[slot], mixTs[slot]
            c0 = i * 128
            xb = next_xa()
            sap, sres = tile_src(l, t)
            load_tile(xb, sap, sres)
            yield
            st = next_stat()
            h = hb[cnt["hb"] % 2]
            cnt["hb"] += 1
            O("act", "activation", out=sq_junk.ap, in_=xb.ap, func=AF.Square, accum_out=st.ap[:, 0:1],
              reads=[xb], writes=[sq_junk, st])
            yield
            yield
            rsqrt_small(st, 0, 1, 1.0 / D, EPS, mode="dve")
            O("dve", "scalar_tensor_tensor", out=xb.ap, in0=xb.ap, scalar=st.ap[:, 0:1], in1=M[0].ap,
              op0=ALU.mult, op1=ALU.mult, reads=[xb, st, M[0]], writes=[xb])
            O("dve", "tensor_tensor", h.ap, xb.ap, M[1].ap, op=ALU.add, reads=[xb, M[1]], writes=[h])
            for _ in range(5):
                yield
            for k in range(8):
                O("pe", "transpose", bT.ap[:, k * 128:(k + 1) * 128], h.ap[:, k * 128:(k + 1) * 128], idb.ap,
                  reads=[h, idb], writes=[bT])
            yield
            yield
            O("act", "copy", hTc.ap[:, :, c0:c0 + 128], bT.ap.rearrange("p (k c) -> p k c", k=8), reads=[bT], writes=[hTc])
            yield
            yield
            for (bank, col0) in ((bA[0], 0), (bA[1], 768), (bA[2], 1280)):
                for k in range(8):
                    O("pe", "matmul", bank.ap, lhsT=hTc.ap[:, k, c0:c0 + 128], rhs=w_in_sb.ap[:, k, col0:col0 + 512],
                      start=(k == 0), stop=(k == 7), reads=[hTc, w_in_sb], writes=[bank])
                yield
            yield
            z = psA[:, 512:1536]
            zb = [bA[1], bA[2]]
            stq = next_stat()
            q3 = qf.ap.rearrange("p (h f) -> p h f", h=8)
            O("act", "activation", out=qf.ap, in_=bA[0].ap, func=AF.Square, reads=[bA[0]], writes=[qf])
            O("act", "activation", out=gA.ap, in_=z, func=AF.Square, reads=zb, writes=[gA])
            yield
            yield
            yield
            O("dve", "tensor_reduce", stq.ap[:, 0:8], q3, axis=AX.X, op=ALU.add, reads=[qf], writes=[stq])
            rsqrt_small(stq, 0, 8, 1.0 / 64, EPS, mode="dve")
            O("dve", "tensor_tensor", q3, bA[0].ap.rearrange("p (h f) -> p h f", h=8),
              stq.ap[:, 0:8].unsqueeze(2).to_broadcast([128, 8, 64]), op=ALU.mult, reads=[bA[0], stq, qf], writes=[qf])
            O("dve", "tensor_tensor", q3, q3, gqB.ap.unsqueeze(1).to_broadcast([128, 8, 64]), op=ALU.mult,
              reads=[qf, gqB], writes=[qf])
            O("dve", "tensor_scalar", gA.ap, gA.ap, 0.044715, 1.0, op0=ALU.mult, op1=ALU.add, reads=[gA], writes=[gA])
            O("dve", "tensor_tensor", gA.ap, gA.ap, z, op=ALU.mult, reads=[gA] + zb, writes=[gA])
            for _ in range(6):
                yield
            O("act", "activation", out=gB.ap, in_=gA.ap, func=AF.Tanh, scale=0.7978845608028654, reads=[gA], writes=[gB])
            if isctx:
                O("dve", "tensor_copy", qb.ap, qf.ap, reads=[qf], writes=[qb])
            else:
                rope(qf.ap, 8, t - 2, qb.ap, [qf], [qb])
            for _ in range(4):
                yield
            for j in range(4):
                O("pe", "transpose", bT.ap[:, j * 128:(j + 1) * 128], qb.ap[:, j * 128:(j + 1) * 128], idb.ap,
                  reads=[qb, idb], writes=[bT])
            vp = gA.ap[:, 512:1024]
            O("dve", "scalar_tensor_tensor", out=ub.ap, in0=gB.ap[:, 0:512], scalar=1.0, in1=z[:, 0:512], op0=ALU.add, op1=ALU.mult,
              reads=[gB, bA[1]], writes=[ub])
            O("dve", "scalar_tensor_tensor", out=vp, in0=gB.ap[:, 512:1024], scalar=1.0, in1=z[:, 512:1024], op0=ALU.add, op1=ALU.mult,
              reads=[gB, bA[2], gA], writes=[gA])
            stv = next_stat()
            O("dve", "bn_stats", stv.ap[:, 0:6], vp, reads=[gA], writes=[stv])
            O("dve", "bn_aggr", stv.ap[:, 8:10], stv.ap[:, 0:6], reads=[stv], writes=[stv])
            rsqrt_small(stv, 9, 10, 1.0, 4 * EPS, mode="dve")
            O("dve", "tensor_scalar", vp, vp, stv.ap[:, 8:9], stv.ap[:, 9:10], op0=ALU.subtract, op1=ALU.mult,
              reads=[gA, stv], writes=[gA])
            O("dve", "tensor_tensor", vp, vp, gsgB.ap, op=ALU.mult, reads=[gA, gsgB], writes=[gA])
            yield
            O("act", "copy", QTc.ap[0:64, 0:8:2, c0:c0 + 128], bT.ap[0:64, 0:512].rearrange("p (j c) -> p j c", j=4),
              reads=[bT], writes=[QTc])
            O("act", "copy", QTc.ap[64:128, 1:8:2, c0:c0 + 128], bT.ap[64:128, 0:512].rearrange("p (j c) -> p j c", j=4),
              reads=[bT], writes=[QTc])
            for _ in range(4):
                yield
            O("pool", "tensor_tensor", vln.ap, vp, bsgB.ap, op=ALU.add, reads=[gA, bsgB], writes=[vln])
            yield
            yield
            for h_ in range(8):
                O("pe", "matmul", bA[0].ap[:, h_ * 64:(h_ + 1) * 64], lhsT=wsT.ap[:, h_, :], rhs=vln.ap[:, h_ * 64:(h_ + 1) * 64],
                  start=True, stop=True, reads=[wsT, vln], writes=[bA[0]])
            yield
            yield
            O("dve", "tensor_tensor", gB.ap[:, 0:512], bA[0].ap, bsB.ap.rearrange("p h f -> p (h f)"), op=ALU.add,
              reads=[bA[0], bsB], writes=[gB])
            yield
            yield
            O("pool", "tensor_tensor", mlpb.ap, gB.ap[:, 0:512], ub.ap, op=ALU.mult, reads=[gB, ub], writes=[mlpb])
            yield
            yield
            for j in range(4):
                O("pe", "transpose", bT.ap[:, 512 + j * 128:512 + (j + 1) * 128], mlpb.ap[:, j * 128:(j + 1) * 128], idb.ap,
                  reads=[mlpb, idb], writes=[bT])
            yield
            yield
            O("act", "copy", mixTc.ap[:, 4:8, c0:c0 + 128], bT.ap[:, 512:1024].rearrange("p (j c) -> p j c", j=4),
              reads=[bT], writes=[mixTc])

        def B2_head(slot, hi, NTk, nk, filler=None):
            QTc, mixTc = QTs[slot], mixTs[slot]
            j, hh = hi // 2, hi % 2
            ob = o_banks[hi % 2]
            rdb = rd[hi % 2]
            r0 = hh * 64

            def qk(kt):
                sb_ = sc_banks[kt % 2]
                O("pe", "matmul", sb_.ap[:, 0:NTk], lhsT=KT.ap[:, kt * 128:(kt + 1) * 128],
                  rhs=QTc.ap[:, 2 * j + hh, 0:NTk], start=True, stop=True, reads=[KT, QTc], writes=[sb_])

            def ex(kt):
                sb_ = sc_banks[kt % 2]
                pt = PT[kt % 4]
                O("act", "activation", out=pt.ap[:, 0:NTk], in_=sb_.ap[:, 0:NTk], func=AF.Exp, scale=0.125,
                  reads=[sb_], writes=[pt])

            def pv(kt):
                pt = PT[kt % 4]
                O("pe", "matmul", ob.ap[:, 0:NTk], lhsT=Vaug.ap[:, kt, hh, :], rhs=pt.ap[:, 0:NTk],
                  start=(kt == 0), stop=(kt == nk - 1), reads=[Vaug, pt], writes=[ob])

            for kt in range(min(LA, nk)):
                qk(kt)
            for kt in range(nk):
                ex(kt)
                if kt + LA < nk:
                    qk(kt + LA)
                pv(kt)
                if filler is not None:
                    try:
                        next(filler)
                    except StopIteration:
                        filler = None
            if filler is not None:
                for _ in filler:
                    pass
            d0 = 64 - r0
            O("dve", "reciprocal", rdb.ap[r0:r0 + 64, 0:NTk], ob.ap[d0:d0 + 64, 0:NTk], reads=[ob], writes=[rdb])
            O("dve", "tensor_tensor", mixTc.ap[r0:r0 + 64, j, 0:NTk], ob.ap[r0:r0 + 64, 0:NTk], rdb.ap[r0:r0 + 64, 0:NTk],
              op=ALU.mult, reads=[ob, rdb], writes=[mixTc])

        def B3_tile(slot, isctx, t, i):
            M = modc if isctx else modx
            mixTc = mixTs[slot]
            c0 = i * 128
            pw = [bA[1], bA[2]]
            pw_ap = psA[:, 512:1536]
            sap, sres = tile_src(l, t)
            load_tile(xr, sap, sres)
            for half in range(2):
                for c in range(8):
                    O("pe", "matmul", pw[half].ap, lhsT=mixTc.ap[:, c, c0:c0 + 128], rhs=w_out_sb.ap[:, c, half * 512:(half + 1) * 512],
                      start=(c == 0), stop=(c == 7), reads=[mixTc, w_out_sb], writes=[pw[half]])
                yield
                yield
            yield
            st = next_stat()
            O("act", "activation", out=sq_junk.ap, in_=pw_ap, func=AF.Square, accum_out=st.ap[:, 0:1],
              reads=pw, writes=[sq_junk, st])
            yield
            yield
            yield
            rsqrt_small(st, 0, 1, 1.0 / D, EPS, mode="dve")
            o = next_xa()
            O("dve", "scalar_tensor_tensor", out=o.ap, in0=pw_ap, scalar=st.ap[:, 0:1], in1=M[2].ap, op0=ALU.mult, op1=ALU.mult,
              reads=pw + [st, M[2]], writes=[o])
            for _ in range(8):
                yield
            O("pool", "tensor_tensor", o.ap, o.ap, xr.ap, op=ALU.add, reads=[o, xr], writes=[o])
            for _ in range(5):
                yield
            DM("sp", o, xs[t * 128:(t + 1) * 128, :], o.ap, reads=[o], writes=[xs_res[t]])

        def drain(gen):
            for _ in gen:
                pass

        if not last:
            for i, t in enumerate([0, 1]):
                drain(B1_tile(1, True, t, i))
            for hi in range(8):
                B2_head(1, hi, 256, 2)
            for i, t in enumerate([0, 1]):
                drain(B3_tile(1, True, t, i))
        G = [[2 + 4 * g + i for i in range(4)] for g in range(8)]
        for i, t in enumerate(G[0]):
            drain(B1_tile(0, False, t, i))
        for g in range(8):
            for hi in range(8):
                filler = None
                if hi < 4 and g >= 1:
                    filler = B3_tile((g - 1) % 2, False, G[g - 1][hi], hi)
                if hi >= 4 and g + 1 < 8:
                    filler = B1_tile((g + 1) % 2, False, G[g + 1][hi - 4], hi - 4)
                B2_head(g % 2, hi, 512, NT_TILES, filler)
        for i, t in enumerate(G[7]):
            drain(B3_tile(7 % 2, False, t, i))

        if stop <= 4:
            return finish()
        p.barrier(new_epoch=False)
        arena.reset()
        wgu = [arena.take([128, 8, 256], BF16, "wgu%d" % j) for j in range(NJ)]
        wo = [arena.take([128, D], BF16, "wo%d" % j) for j in range(NJ)]
        actT = arena.take([128, NJ, 512], BF16, "actT")
        sgt = [arena.take([128, 512], F32, "sgt%d" % i) for i in range(2)]
        for j in range(NJ):
            DM("pool", wgu[j], wgu[j].ap, w_gu[l, j].rearrange("p (k n) -> p k n", k=8), writes=[wgu[j]])
        for j in range(NJ):
            DM("pool", wo[j], wo[j].ap, w_o[l][:, j * D:(j + 1) * D], writes=[wo[j]])
        cgroups = []
        if not last:
            cgroups.append((True, [0, 1]))
        for g in range(8):
            cgroups.append((False, [2 + 4 * g + i for i in range(4)]))
        cur_stream = [None]
        hTs = [hT, hT2]

        def c_prenorm(gi):
            isctx, tiles = cgroups[gi]
            r = 1 if isctx else 0
            if cur_stream[0] != r:
                load_mods(modx, l, r, 3, g_pre_ffn, g_post_ffn)
                cur_stream[0] = r
            hs = []
            for i, t in enumerate(tiles):
                xb = next_xa()
                load_tile(xb, xs[t * 128:(t + 1) * 128, :], xs_res[t])
                hs.append(prenorm_front(xb, modx[0], modx[1]))
                if i >= 1:
                    prenorm_back(hs[i - 1], (i - 1) * 128, hT=hTs[gi % 2])
            prenorm_back(hs[-1], (len(tiles) - 1) * 128, hT=hTs[gi % 2])

        done_pre = set()
        for gi, (isctx, tiles) in enumerate(cgroups):
            if gi not in done_pre:
                c_prenorm(gi)
                done_pre.add(gi)
            hTc = hTs[gi % 2]
            NTk = 128 * len(tiles)
            gub = [(bA[0], bA[1]), (bB[0], bB[1])]
            for j in range(NJ):
                pg, pu = gub[j % 2]
                for k in range(8):
                    O("pe", "matmul", pg.ap[:, 0:NTk], lhsT=wgu[j].ap[:, k, 0:128], rhs=hTc.ap[:, k, 0:NTk],
                      start=(k == 0), stop=(k == 7), reads=[wgu[j], hTc], writes=[pg])
                for k in range(8):
                    O("pe", "matmul", pu.ap[:, 0:NTk], lhsT=wgu[j].ap[:, k, 128:256], rhs=hTc.ap[:, k, 0:NTk],
                      start=(k == 0), stop=(k == 7), reads=[wgu[j], hTc], writes=[pu])
                sg_ = sgt[j % 2]
                O("act", "activation", out=sg_.ap[:, 0:NTk], in_=pg.ap[:, 0:NTk], func=AF.Silu, reads=[pg], writes=[sg_])
                O("dve", "tensor_tensor", actT.ap[:, j, 0:NTk], sg_.ap[:, 0:NTk], pu.ap[:, 0:NTk], op=ALU.mult,
                  reads=[sg_, pu], writes=[actT])
            if gi + 1 < len(cgroups) and cgroups[gi + 1][0] == isctx:
                c_prenorm(gi + 1)
                done_pre.add(gi + 1)
            for i, t in enumerate(tiles):
                c0 = i * 128
                pw = [bA[2], bA[3]]
                for half in range(2):
                    for j in range(NJ):
                        O("pe", "matmul", pw[half].ap, lhsT=actT.ap[:, j, c0:c0 + 128], rhs=wo[j].ap[:, half * 512:(half + 1) * 512],
                          start=(j == 0), stop=(j == NJ - 1), reads=[actT, wo[j]], writes=[pw[half]])
                if last and not isctx:
                    dst_ap, dst_res = y_out[(t - 2) * 128:(t - 1) * 128, :], y_res[t - 2]
                else:
                    dst_ap, dst_res = xs[t * 128:(t + 1) * 128, :], xs_res[t]
                postnorm_store(pw, psA[:, 1024:2048], modx[2], xs[t * 128:(t + 1) * 128, :], xs_res[t], dst_ap, dst_res)
        p.barrier()

    return finish()


def _rope_tables():
    n = SEQ
    rows = n // 64
    pos_row = np.repeat(np.arange(rows, dtype=np.float32), 64)
    pos_col = np.tile(np.arange(64, dtype=np.float32), rows)
    inv = (10000.0 ** (-np.arange(0, 32, 2, dtype=np.float32) / 32)).astype(np.float32)
    ang = np.concatenate([pos_row[:, None] * inv, pos_col[:, None] * inv], axis=-1).astype(np.float32)
    cos = np.cos(ang).astype(np.float32)
    sin = np.sin(ang).astype(np.float32)
    cos = np.ascontiguousarray(cos.reshape(32, 128, 32).transpose(1, 0, 2)).reshape(128, 32 * 32)
    sin = np.ascontiguousarray(sin.reshape(32, 128, 32).transpose(1, 0, 2)).reshape(128, 32 * 32)
    return cos, sin


def prep_shared(inputs, L):
    f = lambda a: np.ascontiguousarray(np.asarray(a, dtype=np.float32))
    sh = {}
    wm = f(inputs["w_mod"])[:L]
    sh["w_mod"] = f(wm.reshape(L, 8, 128, 12, 512).transpose(0, 3, 2, 1, 4)).reshape(L, 12, 128, 8 * 512)
    sh["b_mod"] = f(inputs["b_mod"])[:L]
    for k in ("g_pre_mix", "g_post_mix", "g_pre_ffn", "g_post_ffn", "g_q", "g_k", "g_sg", "b_sg"):
        sh[k] = f(inputs[k])[:L]
    wi = f(inputs["w_in"])[:L]
    qcols = np.concatenate([np.arange(h * 64, (h + 1) * 64) for h in QPERM])
    cols = np.concatenate([qcols, np.arange(512, INW)])
    wi = wi[:, :, cols]
    sh["w_in"] = f(wi.reshape(L, 8, 128, INW).transpose(0, 2, 1, 3)).reshape(L, 128, 8 * INW)
    ws = f(inputs["w_s"])[:L]
    sh["w_s"] = f(ws.transpose(0, 3, 1, 2)).reshape(L, 128, 8 * 128)
    sh["b_s"] = f(f(inputs["b_s"])[:L].transpose(0, 2, 1))
    wo_ = f(inputs["w_out"])[:L]
    rows = np.concatenate([qcols, np.arange(512, 1024)])
    wo_ = wo_[:, rows, :]
    sh["w_out"] = f(wo_.reshape(L, 8, 128, D).transpose(0, 2, 1, 3)).reshape(L, 128, 8 * D)
    wf = f(inputs["w_ffn_in"])[:L]
    gate = wf[:, :, :HID].reshape(L, 8, 128, NJ, 128)
    up = wf[:, :, HID:].reshape(L, 8, 128, NJ, 128)
    gu = np.concatenate([gate, up], axis=-1)
    sh["w_gu"] = f(gu.transpose(0, 3, 2, 1, 4)).reshape(L, NJ, 128, 8 * 256)
    wfo = f(inputs["w_ffn_out"])[:L]
    sh["w_o"] = f(wfo.reshape(L, NJ, 128, D).transpose(0, 2, 1, 3)).reshape(L, 128, NJ * D)
    sh["ident"] = np.eye(128, dtype=np.float32)
    cos, sin = _rope_tables()
    sh["rope_cos"] = cos
    sh["rope_sin"] = sin
    sh["c_ctx"] = f(f(inputs["c_ctx"]).reshape(8, 128).T)
    return sh


def prep_core(inputs, b):
    f = lambda a: np.ascontiguousarray(np.asarray(a, dtype=np.float32))
    return {
        "x": f(inputs["x"][b]),
        "ctx": f(inputs["ctx"][b]),
        "c": f(f(inputs["c"][b]).reshape(8, 128).T),
    }


_NC_CACHE = {}


def run(inputs, n_layers=DEPTH, cores=None, stop=99):
    if cores is None:
        cores = list(range(8))
    if (n_layers, stop) not in _NC_CACHE:
        _NC_CACHE[(n_layers, stop)] = build(n_layers, stop)
    nc = _NC_CACHE[(n_layers, stop)]
    sh = prep_shared(inputs, n_layers)
    in_maps = []
    for b in cores:
        m = dict(sh)
        m.update(prep_core(inputs, b))
        in_maps.append(m)
    res = run_bass_kernel_spmd(nc, in_maps, core_ids=list(range(len(cores))))
    return np.stack([np.asarray(r["y"], dtype=np.float32) for r in res.results], axis=0)


def kernel(**inputs):
    return run(inputs, DEPTH)
```

```python
import os
import numpy as np
import concourse.bass as bass
import concourse.mybir as mybir
from concourse.bass_utils import run_bass_kernel_spmd

F32 = mybir.dt.float32
BF16 = mybir.dt.bfloat16
I32 = mybir.dt.int32
AF = mybir.ActivationFunctionType
ALU = mybir.AluOpType
AX = mybir.AxisListType

D = 1024
SEQ = 4096
CTX = 256
NTOK = SEQ + CTX
NT_TILES = NTOK // 128
DEPTH = 4
HID = 2816
NJ = HID // 128
INW = 1792
EPS = 1e-6
QPERM = [0, 4, 1, 5, 2, 6, 3, 7]


class Res:
    __slots__ = ("w", "r", "name", "sem", "excl")

    def __init__(self, name=""):
        self.w = None
        self.r = []
        self.name = name
        self.sem = None
        self.excl = False


class Buf(Res):
    __slots__ = ("ap",)

    def __init__(self, ap, name=""):
        Res.__init__(self, name)
        self.ap = ap


class Prog:
    ENGS = ("pe", "act", "dve", "pool", "sp")

    def __init__(self, nc):
        self.nc = nc
        self.streams = {e: [] for e in self.ENGS}
        self.sems = {}
        self.cnt = {}
        self.cur = {}
        self.epoch = 0
        self.dma_keys = {}
        self._new_epoch()
        self.waited = {e: {} for e in self.ENGS}
        self.ndma = 0
        self.nops = 0

    def _new_epoch(self):
        for e in self.ENGS:
            key = "%s#%d" % (e, self.epoch)
            self.sems[key] = self.nc.alloc_semaphore("s_%s_%d" % (e, self.epoch))
            self.cnt[key] = 0
            self.cur[e] = key
        self.epoch += 1

    def _need(self, eng, deps):
        best = {}
        for (k, v) in deps:
            if v > best.get(k, 0):
                best[k] = v
        out = []
        w = self.waited[eng]
        for k, v in best.items():
            if eng == "pe" and k.startswith("pe#"):
                continue
            if w.get(k, 0) < v:
                w[k] = v
                out.append((k, v))
        return out

    @staticmethod
    def _collect(reads, writes):
        deps = []
        for r in reads:
            if r.w is not None:
                deps.append(r.w)
        for wr in writes:
            if wr.w is not None:
                deps.append(wr.w)
            deps.extend(wr.r)
        return deps

    def _mark(self, me, reads, writes):
        for r in reads:
            r.r.append(me)
            if len(r.r) > 64:
                best = {}
                for (k, v) in r.r:
                    if v > best.get(k, 0):
                        best[k] = v
                r.r = list(best.items())
        for wr in writes:
            wr.w = me
            wr.r = []

    def op(self, eng, method, *args, reads=(), writes=(), **kw):
        ex = [r for r in reads if r.excl and r not in writes]
        if ex:
            writes = list(writes) + ex
        waits = self._need(eng, self._collect(reads, writes))
        key = self.cur[eng]
        self.cnt[key] += 1
        me = (key, self.cnt[key])
        self.streams[eng].append((waits, (method, args, kw), (key, 1)))
        self._mark(me, reads, writes)
        self.nops += 1
        return me

    def dma_sem_for(self, res):
        key = "dma:" + res.name
        if key not in self.sems:
            self.ndma += 1
            self.sems[key] = self.nc.alloc_semaphore("d_%d" % self.ndma)
            self.cnt[key] = 0
        return key

    def dma(self, queue, semres, out, in_, reads=(), writes=()):
        semkey = self.dma_sem_for(semres)
        waits = self._need(queue, self._collect(reads, writes))
        self.cnt[semkey] += 16
        me = (semkey, self.cnt[semkey])
        self.streams[queue].append((waits, ("dma_start", (), dict(out=out, in_=in_)), (semkey, 16)))
        self._mark(me, reads, writes)
        self.nops += 1
        return me

    def barrier(self, new_epoch=True):
        allv = [(k, v) for k, v in self.cnt.items() if v > 0]
        for e in self.ENGS:
            waits = self._need(e, allv)
            if waits:
                self.streams[e].append((waits, None, None))
        if new_epoch:
            self._new_epoch()

    def emit(self):
        nc = self.nc
        sems = self.sems
        streams = self.streams

        waited = {}
        for ename in self.ENGS:
            for (waits, fn, inc) in streams[ename]:
                for (k, v) in waits:
                    waited.setdefault(k, set()).add(v)
        remap = {}
        for k, vals in waited.items():
            if k.startswith("dma:"):
                continue
            remap[k] = {v: i + 1 for i, v in enumerate(sorted(vals))}
        seen = {}

        def run(e, ename):
            for (waits, fn, inc) in streams[ename]:
                for (k, v) in waits:
                    if k in remap:
                        e.wait_ge(sems[k], remap[k][v])
                    else:
                        e.wait_ge(sems[k], v)
                if fn is not None:
                    ins = getattr(e, fn[0])(*fn[1], **fn[2])
                    k = inc[0]
                    if k.startswith("dma:"):
                        ins.then_inc(sems[k], inc[1])
                    else:
                        seen[k] = seen.get(k, 0) + 1
                        if seen[k] in remap.get(k, ()):
                            ins.then_inc(sems[k], 1)

        with nc.Block() as block:
            @block.tensor
            def _(e):
                run(e, "pe")

            @block.scalar
            def _(e):
                run(e, "act")

            @block.vector
            def _(e):
                run(e, "dve")

            @block.gpsimd
            def _(e):
                run(e, "pool")

            @block.sync
            def _(e):
                run(e, "sp")


class Arena:
    def __init__(self, ap_bf16):
        self.ap = ap_bf16
        self.off = 0
        self.size = ap_bf16.shape[1]

    def reset(self):
        self.off = 0

    def take(self, shape, dtype, name=""):
        n = 1
        for s in shape[1:]:
            n *= s
        nel = n * (2 if dtype == F32 else 1)
        if self.off % 2:
            self.off += 1
        assert self.off + nel <= self.size, ("arena overflow", name, self.off, nel, self.size)
        v = self.ap[:, self.off:self.off + nel]
        self.off += nel
        if dtype == F32:
            v = v.bitcast(F32)
        if len(shape) == 3:
            v = v.rearrange("p (a b) -> p a b", a=shape[1])
        elif len(shape) == 4:
            v = v.rearrange("p (a b c) -> p a b c", a=shape[1], b=shape[2])
        return Buf(v, name)


def build(n_layers=DEPTH, stop=99):
    nc = bass.Bass("TRN2", target_bir_lowering=False)
    p = Prog(nc)
    O = p.op
    DM = p.dma
    L = n_layers

    def din(name, shape):
        return nc.dram_tensor(name, list(shape), F32, kind="ExternalInput").ap()

    x_in = din("x", [SEQ, D])
    ctx_in = din("ctx", [CTX, D])
    c_in = din("c", [128, 8])
    cc_in = din("c_ctx", [128, 8])
    w_mod = din("w_mod", [L, 12, 128, 8 * 512])
    b_mod = din("b_mod", [L, 6 * D])
    g_pre_mix = din("g_pre_mix", [L, D])
    g_post_mix = din("g_post_mix", [L, D])
    g_pre_ffn = din("g_pre_ffn", [L, D])
    g_post_ffn = din("g_post_ffn", [L, D])
    w_in = din("w_in", [L, 128, 8 * INW])
    g_q = din("g_q", [L, 64])
    g_k = din("g_k", [L, 64])
    g_sg = din("g_sg", [L, 512])
    b_sg = din("b_sg", [L, 512])
    w_s = din("w_s", [L, 128, 8 * 128])
    b_s = din("b_s", [L, 128, 8])
    w_out = din("w_out", [L, 128, 8 * D])
    w_gu = din("w_gu", [L, NJ, 128, 8 * 256])
    w_o = din("w_o", [L, 128, NJ * D])
    ident_in = din("ident", [128, 128])
    cos_in = din("rope_cos", [128, 32 * 32])
    sin_in = din("rope_sin", [128, 32 * 32])
    y_out = nc.dram_tensor("y", [SEQ, D], F32, kind="ExternalOutput").ap()
    xs = nc.dram_tensor("xs", [NTOK, D], F32, kind="Internal").ap()
    mod_d = nc.dram_tensor("mod_d", [L * 2, 6 * D], F32, kind="Internal").ap()

    xs_res = [Res("xs%d" % t) for t in range(NT_TILES)]
    y_res = [Res("y%d" % t) for t in range(32)]
    modd_res = Res("mod_d")

    def sb(name, shape, dtype):
        return Buf(nc.alloc_sbuf_tensor(name, list(shape), dtype).ap(), name)

    idb = sb("idb", [128, 128], BF16)
    xa = [sb("xa%d" % i, [128, D], F32) for i in range(2)]
    xr = sb("xr0", [128, D], F32)
    sq_junk = sb("sq_junk", [128, D], BF16)
    hb = [sb("hb%d" % i, [128, D], BF16) for i in range(2)]
    hT = sb("hT", [128, 8, 512], BF16)
    hT2 = sb("hT2", [128, 8, 512], BF16)
    modx = [sb("modx%d" % i, [128, D], F32) for i in range(3)]
    stat = [sb("stat%d" % i, [128, 16], F32) for i in range(4)]
    rs_y = [sb("rsy%d" % i, [128, 16], F32) for i in range(4)]
    rs_t = [sb("rst%d" % i, [128, 16], F32) for i in range(4)]
    one_i = sb("one_i", [128, 1], I32)
    magic_i = sb("magic_i", [128, 16], I32)
    arena = Arena(nc.alloc_sbuf_tensor("arena", [128, 79 * 1024], BF16).ap())

    psT = nc.alloc_psum_tensor("psT", [128, 1024], BF16).ap()
    psA = nc.alloc_psum_tensor("psA", [128, 2048], F32).ap()
    psBs = [nc.alloc_psum_tensor("psB%d" % i, [128, 512], F32).ap() for i in range(3)]
    bT = Buf(psT, "bT")
    bA = [Buf(psA[:, i * 512:(i + 1) * 512], "bA%d" % i) for i in range(4)]
    bB = [Buf(psBs[i], "bB%d" % i) for i in range(3)]
    bC = bA[3]
    for b_ in [bT] + bA + bB:
        b_.excl = True

    cnt = {"xa": 0, "hb": 0, "stat": 0}

    def next_xa():
        b = xa[cnt["xa"] % 2]
        cnt["xa"] += 1
        return b

    def next_stat():
        b = stat[cnt["stat"] % 4]
        cnt["stat"] += 1
        return b

    def load_tile(dst, src_ap, src_res):
        DM("sp", dst, dst.ap, src_ap, reads=[src_res] if src_res else [], writes=[dst])

    def rsqrt_small(st, lo, hi, scale, eps, mode="act"):
        n = hi - lo
        v = st.ap[:, lo:hi]
        O("dve", "tensor_scalar", v, v, scale, eps, op0=ALU.mult, op1=ALU.add, reads=[st], writes=[st])
        if mode == "act":
            O("act", "activation", out=v, in_=v, func=AF.Sqrt, reads=[st], writes=[st])
            O("dve", "reciprocal", v, v, reads=[st], writes=[st])
            return
        k = stat.index(st)
        yb, tb = rs_y[k], rs_t[k]
        y = yb.ap[:, 0:n]
        t1 = tb.ap[:, 0:n]
        yi = y.bitcast(I32)
        O("dve", "tensor_scalar", yi, v.bitcast(I32), one_i.ap[:, 0:1], None, op0=ALU.arith_shift_right,
          reads=[st, one_i], writes=[yb])
        O("dve", "tensor_tensor", yi, magic_i.ap[:, 0:n], yi, op=ALU.subtract, reads=[yb, magic_i], writes=[yb])
        NIT = 3
        for it_ in range(NIT):
            O("dve", "tensor_tensor", t1, v, y, op=ALU.mult, reads=[st, yb], writes=[tb])
            O("dve", "tensor_tensor", t1, t1, y, op=ALU.mult, reads=[tb, yb], writes=[tb])
            O("dve", "tensor_scalar", t1, t1, -0.5, 1.5, op0=ALU.mult, op1=ALU.add, reads=[tb], writes=[tb])
            if it_ < NIT - 1:
                O("dve", "tensor_tensor", y, y, t1, op=ALU.mult, reads=[yb, tb], writes=[yb])
            else:
                O("dve", "tensor_tensor", v, y, t1, op=ALU.mult, reads=[yb, tb], writes=[st])

    def prenorm_front(xbuf, G1, SH):
        st = next_stat()
        h = hb[cnt["hb"] % 2]
        cnt["hb"] += 1
        O("act", "activation", out=sq_junk.ap, in_=xbuf.ap, func=AF.Square, accum_out=st.ap[:, 0:1],
          reads=[xbuf], writes=[sq_junk, st])
        rsqrt_small(st, 0, 1, 1.0 / D, EPS)
        O("dve", "scalar_tensor_tensor", out=xbuf.ap, in0=xbuf.ap, scalar=st.ap[:, 0:1], in1=G1.ap,
          op0=ALU.mult, op1=ALU.mult, reads=[xbuf, st, G1], writes=[xbuf])
        O("dve", "tensor_tensor", h.ap, xbuf.ap, SH.ap, op=ALU.add, reads=[xbuf, SH], writes=[h])
        return h

    def prenorm_back(h, ncol0, hT=hT):
        for k in range(8):
            O("pe", "transpose", bT.ap[:, k * 128:(k + 1) * 128], h.ap[:, k * 128:(k + 1) * 128], idb.ap,
              reads=[h, idb], writes=[bT])
        O("act", "copy", hT.ap[:, :, ncol0:ncol0 + 128], bT.ap.rearrange("p (k c) -> p k c", k=8),
          reads=[bT], writes=[hT])

    def prenorm_to_hT(xbuf, G1, SH, ncol0, hT=hT):
        h = prenorm_front(xbuf, G1, SH)
        prenorm_back(h, ncol0, hT=hT)

    def load_mods(M, l, r, idx, gpre, gpost):
        row = l * 2 + r
        tmp = xr
        DM("sp", M[0], M[0].ap, mod_d[row, (idx + 1) * D:(idx + 2) * D].partition_broadcast(128), reads=[modd_res], writes=[M[0]])
        DM("sp", tmp, tmp.ap, gpre[l, :].partition_broadcast(128), writes=[tmp])
        O("dve", "scalar_tensor_tensor", out=M[0].ap, in0=M[0].ap, scalar=1.0, in1=tmp.ap, op0=ALU.add, op1=ALU.mult,
          reads=[M[0], tmp], writes=[M[0]])
        DM("sp", M[1], M[1].ap, mod_d[row, idx * D:(idx + 1) * D].partition_broadcast(128), reads=[modd_res], writes=[M[1]])
        DM("sp", M[2], M[2].ap, mod_d[row, (idx + 2) * D:(idx + 3) * D].partition_broadcast(128), reads=[modd_res], writes=[M[2]])
        DM("sp", tmp, tmp.ap, gpost[l, :].partition_broadcast(128), writes=[tmp])
        O("dve", "tensor_tensor", M[2].ap, M[2].ap, tmp.ap, op=ALU.mult, reads=[M[2], tmp], writes=[M[2]])

    def postnorm_store(pw_banks, pw_ap, GG, src_ap, src_res, dst_ap, dst_res):
        st = next_stat()
        xb = xr
        load_tile(xb, src_ap, src_res)
        O("act", "activation", out=sq_junk.ap, in_=pw_ap, func=AF.Square, accum_out=st.ap[:, 0:1],
          reads=pw_banks, writes=[sq_junk, st])
        rsqrt_small(st, 0, 1, 1.0 / D, EPS)
        o = next_xa()
        O("dve", "scalar_tensor_tensor", out=o.ap, in0=pw_ap, scalar=st.ap[:, 0:1], in1=GG.ap, op0=ALU.mult, op1=ALU.mult,
          reads=pw_banks + [st, GG], writes=[o])
        O("pool", "tensor_tensor", o.ap, o.ap, xb.ap, op=ALU.add, reads=[o, xb], writes=[o])
        DM("sp", o, dst_ap, o.ap, reads=[o], writes=[dst_res])

    def tile_src(l, t):
        if l == 0:
            if t < 2:
                return ctx_in[t * 128:(t + 1) * 128, :], None
            return x_in[(t - 2) * 128:(t - 1) * 128, :], None
        return xs[t * 128:(t + 1) * 128, :], xs_res[t]

    O("dve", "memset", one_i.ap, 1, writes=[one_i])
    O("dve", "memset", magic_i.ap, 0x5f3759df, writes=[magic_i])
    idf = xa[0]
    DM("sp", idf, idf.ap[:, 0:128], ident_in, writes=[idf])
    O("dve", "tensor_copy", idb.ap, idf.ap[:, 0:128], reads=[idf], writes=[idb])

    arena.reset()
    cst = arena.take([128, 8, 2], F32, "cst")
    craw = [arena.take([128, 8], F32, "craw%d" % i) for i in range(2)]
    ctmp = arena.take([128, 8], F32, "ctmp")
    NSTG = 4
    stage = [arena.take([128, 8, 512], F32, "stage%d" % i) for i in range(NSTG)]
    bmt = [arena.take([128, 512], F32, "bmt%d" % i) for i in range(2)]
    mrow = [arena.take([128, 512], F32, "mrow%d" % i) for i in range(2)]
    DM("sp", craw[0], craw[0].ap, c_in, writes=[craw[0]])
    DM("sp", craw[1], craw[1].ap, cc_in, writes=[craw[1]])
    for r in range(2):
        O("act", "activation", out=ctmp.ap, in_=craw[r].ap, func=AF.Tanh, scale=0.5, reads=[craw[r]], writes=[ctmp])
        O("dve", "scalar_tensor_tensor", out=ctmp.ap, in0=ctmp.ap, scalar=1.0, in1=craw[r].ap, op0=ALU.add, op1=ALU.mult,
          reads=[ctmp, craw[r]], writes=[ctmp])
        O("dve", "tensor_scalar", cst.ap[:, :, r], ctmp.ap, 0.5, None, op0=ALU.mult, reads=[ctmp], writes=[cst])
    it = 0
    for l in range(L):
        for j in range(12):
            sg = stage[it % NSTG]
            bm = bmt[it % 2]
            mr = mrow[it % 2]
            pb = bA[it % 2]
            DM("sp" if it % 2 == 0 else "act", sg, sg.ap, w_mod[l, j].rearrange("p (k n) -> p k n", k=8), writes=[sg])
            DM("sp", bm, bm.ap[0:2, :], b_mod[l, j * 512:(j + 1) * 512].partition_broadcast(2), writes=[bm])
            for k in range(8):
                O("pe", "matmul", pb.ap[0:2, :], lhsT=cst.ap[:, k, :], rhs=sg.ap[:, k, :], start=(k == 0), stop=(k == 7),
                  reads=[cst, sg], writes=[pb])
            O("dve", "tensor_tensor", mr.ap[0:2, :], pb.ap[0:2, :], bm.ap[0:2, :], op=ALU.add, reads=[pb, bm], writes=[mr])
            DM("sp", mr, mod_d[2 * l:2 * l + 2, j * 512:(j + 1) * 512], mr.ap[0:2, :], reads=[mr], writes=[modd_res])
            it += 1
    p.barrier()

    def finish():
        p.barrier(new_epoch=False)
        print("ops:", p.nops, "sems:", len(p.sems))
        p.emit()
        return nc

    if stop <= 1:
        return finish()

    for l in range(L):
        last = (l == L - 1)
        arena.reset()
        w_in_sb = arena.take([128, 8, INW], BF16, "w_in_sb")
        w_out_sb = arena.take([128, 8, D], BF16, "w_out_sb")
        wsT = arena.take([128, 8, 128], BF16, "wsT")
        bsB = arena.take([128, 8, 64], F32, "bsB")
        bsT = arena.take([128, 8], F32, "bsT")
        KT = arena.take([128, NTOK], BF16, "KT")
        Vaug = arena.take([128, NT_TILES, 2, 128], BF16, "Vaug")
        QT = arena.take([128, 8, 512], BF16, "QT")
        mixT = arena.take([128, 8, 512], BF16, "mixT")
        QT2 = arena.take([128, 8, 512], BF16, "QT2")
        mixT2 = arena.take([128, 8, 512], BF16, "mixT2")
        PT = [arena.take([128, 512], BF16, "PT%d" % i) for i in range(4)]
        rd = [arena.take([128, 512], F32, "rd%d" % i) for i in range(2)]
        gA = arena.take([128, 1024], F32, "gA")
        gB = arena.take([128, 1024], F32, "gB")
        ub = arena.take([128, 512], BF16, "ub")
        vln = arena.take([128, 512], BF16, "vln")
        mlpb = arena.take([128, 512], BF16, "mlpb")
        qf = arena.take([128, 512], F32, "qf")
        qt1 = arena.take([128, 256], F32, "qt1")
        qt2 = arena.take([128, 256], F32, "qt2")
        qb = arena.take([128, 512], BF16, "qb")
        kb = arena.take([128, 128], BF16, "kb")
        cosT = arena.take([128, 32, 32], F32, "cosT")
        sinT = arena.take([128, 32, 32], F32, "sinT")
        gqB = arena.take([128, 64], F32, "gqB")
        gkB = arena.take([128, 64], F32, "gkB")
        gsgB = arena.take([128, 512], F32, "gsgB")
        bsgB = arena.take([128, 512], F32, "bsgB")
        modc = [arena.take([128, D], F32, "modc%d" % i) for i in range(3)]

        DM("pool", w_in_sb, w_in_sb.ap, w_in[l].rearrange("p (k n) -> p k n", k=8), writes=[w_in_sb])
        DM("pool", w_out_sb, w_out_sb.ap, w_out[l].rearrange("p (k n) -> p k n", k=8), writes=[w_out_sb])
        DM("pool", wsT, wsT.ap, w_s[l].rearrange("p (h q) -> p h q", h=8), writes=[wsT])
        O("pool", "tensor_scalar", wsT.ap, wsT.ap, 0.5, None, op0=ALU.mult, reads=[wsT], writes=[wsT])
        if stop == 1.1:
            return finish()
        cgrp = Res("constgrp")
        DM("sp", cgrp, bsT.ap, b_s[l], writes=[bsT])
        DM("sp", cgrp, cosT.ap, cos_in.rearrange("p (a b) -> p a b", a=32), writes=[cosT])
        DM("sp", cgrp, sinT.ap, sin_in.rearrange("p (a b) -> p a b", a=32), writes=[sinT])
        DM("sp", cgrp, gqB.ap, g_q[l, :].partition_broadcast(128), writes=[gqB])
        DM("sp", cgrp, gkB.ap, g_k[l, :].partition_broadcast(128), writes=[gkB])
        DM("sp", cgrp, gsgB.ap, g_sg[l, :].partition_broadcast(128), writes=[gsgB])
        DM("sp", cgrp, bsgB.ap, b_sg[l, :].partition_broadcast(128), writes=[bsgB])
        O("dve", "memset", kb.ap, 0.0, writes=[kb, bsT, cosT, sinT, gqB, gkB, gsgB, bsgB])
        O("pool", "tensor_scalar", bsB.ap, bsT.ap.unsqueeze(2).to_broadcast([128, 8, 64]), 0.5, None, op0=ALU.mult,
          reads=[bsT], writes=[bsB])
        if stop == 1.2:
            return finish()
        for QT_ in (QT, QT2):
            O("pool", "memset", QT_.ap[64:128, 0:8:2, :], 0.0, writes=[QT_])
            O("pool", "memset", QT_.ap[0:64, 1:8:2, :], 0.0, writes=[QT_])
        O("pool", "memset", Vaug.ap[:, :, 0, 64:128], 1.0, writes=[Vaug])
        O("pool", "memset", Vaug.ap[:, :, 1, 0:64], 1.0, writes=[Vaug])
        if stop == 1.3:
            return finish()
        load_mods(modc, l, 1, 0, g_pre_mix, g_post_mix)
        if stop == 1.4:
            return finish()
        load_mods(modx, l, 0, 0, g_pre_mix, g_post_mix)

        if stop <= 2:
            return finish()

        def rope(src_ap, nh, lt, dst_ap, rbufs, wbufs):
            s5 = src_ap.rearrange("p (h a b f) -> p h a b f", h=nh, a=2, b=2)
            d5 = dst_ap.rearrange("p (h a b f) -> p h a b f", h=nh, a=2, b=2)
            x1 = s5[:, :, :, 0, :]
            x2 = s5[:, :, :, 1, :]
            cb = cosT.ap[:, lt, :].rearrange("p (a f) -> p a f", a=2).unsqueeze(1).to_broadcast([128, nh, 2, 16])
            sbb = sinT.ap[:, lt, :].rearrange("p (a f) -> p a f", a=2).unsqueeze(1).to_broadcast([128, nh, 2, 16])
            n = nh * 32
            a1 = qt1.ap[:, 0:n].rearrange("p (h a f) -> p h a f", h=nh, a=2)
            a2 = qt2.ap[:, 0:n].rearrange("p (h a f) -> p h a f", h=nh, a=2)
            O("dve", "tensor_tensor", a1, x1, cb, op=ALU.mult, reads=rbufs + [cosT], writes=[qt1])
            O("dve", "tensor_tensor", a2, x2, sbb, op=ALU.mult, reads=rbufs + [sinT], writes=[qt2])
            O("dve", "tensor_tensor", d5[:, :, :, 0, :], a1, a2, op=ALU.subtract, reads=[qt1, qt2], writes=wbufs)
            O("dve", "tensor_tensor", a1, x1, sbb, op=ALU.mult, reads=rbufs + [sinT, qt1], writes=[qt1])
            O("dve", "tensor_tensor", a2, x2, cb, op=ALU.mult, reads=rbufs + [cosT, qt2], writes=[qt2])
            O("dve", "tensor_tensor", d5[:, :, :, 1, :], a1, a2, op=ALU.add, reads=[qt1, qt2], writes=wbufs)

        def head_norm(src_ap, src_bufs, nh, gB_, dstf):
            st = next_stat()
            n = nh * 64
            d = dstf.ap[:, 0:n]
            d3 = d.rearrange("p (h f) -> p h f", h=nh)
            O("act", "activation", out=d, in_=src_ap, func=AF.Square, reads=src_bufs, writes=[dstf])
            O("dve", "tensor_reduce", st.ap[:, 0:nh], d3, axis=AX.X, op=ALU.add, reads=[dstf], writes=[st])
            rsqrt_small(st, 0, nh, 1.0 / 64, EPS)
            O("dve", "tensor_tensor", d3, src_ap.rearrange("p (h f) -> p h f", h=nh),
              st.ap[:, 0:nh].unsqueeze(2).to_broadcast([128, nh, 64]), op=ALU.mult,
              reads=src_bufs + [st, dstf], writes=[dstf])
            O("dve", "tensor_tensor", d3, d3, gB_.ap.unsqueeze(1).to_broadcast([128, nh, 64]), op=ALU.mult,
              reads=[dstf, gB_], writes=[dstf])

        kvbanks = [bA[3], bA[2]]

        def front_A(t):
            isctx = t < 2
            M = modc if isctx else modx
            xb = next_xa()
            sap, sres = tile_src(l, t)
            load_tile(xb, sap, sres)
            prenorm_to_hT(xb, M[0], M[1], 0)
            bk = kvbanks[t % 2]
            for k in range(8):
                O("pe", "matmul", bk.ap[:, 0:256], lhsT=hT.ap[:, k, 0:128], rhs=w_in_sb.ap[:, k, 512:768],
                  start=(k == 0), stop=(k == 7), reads=[hT, w_in_sb], writes=[bk])

        def back_A(t):
            isctx = t < 2
            bk = kvbanks[t % 2]
            head_norm(bk.ap[:, 0:128], [bk], 2, gkB, qf)
            if isctx:
                O("dve", "tensor_copy", kb.ap, qf.ap[:, 0:128], reads=[qf], writes=[kb])
            else:
                rope(qf.ap[:, 0:128], 2, t - 2, kb.ap, [qf], [kb])
            O("act", "copy", Vaug.ap[:, t, 0, 0:64], bk.ap[:, 128:192], reads=[bk], writes=[Vaug])
            O("act", "copy", Vaug.ap[:, t, 1, 64:128], bk.ap[:, 192:256], reads=[bk], writes=[Vaug])
            O("pe", "transpose", bB[0].ap.bitcast(BF16)[:, 0:128], kb.ap, idb.ap, reads=[kb, idb], writes=[bB[0]])
            O("act", "copy", KT.ap[:, t * 128:(t + 1) * 128], bB[0].ap.bitcast(BF16)[:, 0:128], reads=[bB[0]], writes=[KT])

        front_A(0)
        for t in range(NT_TILES):
            if t + 1 < NT_TILES:
                front_A(t + 1)
            back_A(t)
        if stop <= 3:
            return finish()
        hTs = [hT, hT2]
        QTs = [QT, QT2]
        mixTs = [mixT, mixT2]
        sc_banks = [bB[0], bB[1]]
        o_banks = [bB[2], bA[3]]
        LA = 1

        def B1_tile(slot, isctx, t, i):
            M = modc if isctx else modx
            hTc, QTc, mixTc = hTs[slot], QTs[slot], mixTs[slot]
            c0 = i * 128
            xb = next_xa()
            sap, sres = tile_src(l, t)
            load_tile(xb, sap, sres)
            yield
            st = next_stat()
            h = hb[cnt["hb"] % 2]
            cnt["hb"] += 1
            O("act", "activation", out=sq_junk.ap, in_=xb.ap, func=AF.Square, accum_out=st.ap[:, 0:1],
              reads=[xb], writes=[sq_junk, st])
            yield
            yield
            rsqrt_small(st, 0, 1, 1.0 / D, EPS, mode="dve")
            O("dve", "scalar_tensor_tensor", out=xb.ap, in0=xb.ap, scalar=st.ap[:, 0:1], in1=M[0].ap,
              op0=ALU.mult, op1=ALU.mult, reads=[xb, st, M[0]], writes=[xb])
            O("dve", "tensor_tensor", h.ap, xb.ap, M[1].ap, op=ALU.add, reads=[xb, M[1]], writes=[h])
            for _ in range(8):
                yield
            for k in range(8):
                O("pe", "transpose", bT.ap[:, k * 128:(k + 1) * 128], h.ap[:, k * 128:(k + 1) * 128], idb.ap,
                  reads=[h, idb], writes=[bT])
            yield
            yield
            O("act", "copy", hTc.ap[:, :, c0:c0 + 128], bT.ap.rearrange("p (k c) -> p k c", k=8), reads=[bT], writes=[hTc])
            yield
            yield
            for (bank, col0) in ((bA[0], 0), (bA[1], 768), (bA[2], 1280)):
                for k in range(8):
                    O("pe", "matmul", bank.ap, lhsT=hTc.ap[:, k, c0:c0 + 128], rhs=w_in_sb.ap[:, k, col0:col0 + 512],
                      start=(k == 0), stop=(k == 7), reads=[hTc, w_in_sb], writes=[bank])
                    if k == 3:
                        yield
                yield
            yield
            z = psA[:, 512:1536]
            zb = [bA[1], bA[2]]
            stq = next_stat()
            q3 = qf.ap.rearrange("p (h f) -> p h f", h=8)
            O("act", "activation", out=qf.ap, in_=bA[0].ap, func=AF.Square, reads=[bA[0]], writes=[qf])
            O("act", "activation", out=gA.ap, in_=z, func=AF.Square, reads=zb, writes=[gA])
            yield
            yield
            yield
            O("dve", "tensor_reduce", stq.ap[:, 0:8], q3, axis=AX.X, op=ALU.add, reads=[qf], writes=[stq])
            rsqrt_small(stq, 0, 8, 1.0 / 64, EPS, mode="dve")
            O("dve", "tensor_tensor", q3, bA[0].ap.rearrange("p (h f) -> p h f", h=8),
              stq.ap[:, 0:8].unsqueeze(2).to_broadcast([128, 8, 64]), op=ALU.mult, reads=[bA[0], stq, qf], writes=[qf])
            O("dve", "tensor_tensor", q3, q3, gqB.ap.unsqueeze(1).to_broadcast([128, 8, 64]), op=ALU.mult,
              reads=[qf, gqB], writes=[qf])
            O("dve", "tensor_scalar", gA.ap, gA.ap, 0.044715, 1.0, op0=ALU.mult, op1=ALU.add, reads=[gA], writes=[gA])
            O("dve", "tensor_tensor", gA.ap, gA.ap, z, op=ALU.mult, reads=[gA] + zb, writes=[gA])
            for _ in range(6):
                yield
            O("act", "activation", out=gB.ap, in_=gA.ap, func=AF.Tanh, scale=0.7978845608028654, reads=[gA], writes=[gB])
            if isctx:
                O("dve", "tensor_copy", qb.ap, qf.ap, reads=[qf], writes=[qb])
            else:
                rope(qf.ap, 8, t - 2, qb.ap, [qf], [qb])
            for _ in range(4):
                yield
            for j in range(4):
                O("pe", "transpose", bT.ap[:, j * 128:(j + 1) * 128], qb.ap[:, j * 128:(j + 1) * 128], idb.ap,
                  reads=[qb, idb], writes=[bT])
            vp = gA.ap[:, 512:1024]
            O("dve", "scalar_tensor_tensor", out=ub.ap, in0=gB.ap[:, 0:512], scalar=1.0, in1=z[:, 0:512], op0=ALU.add, op1=ALU.mult,
              reads=[gB, bA[1]], writes=[ub])
            O("dve", "scalar_tensor_tensor", out=vp, in0=gB.ap[:, 512:1024], scalar=1.0, in1=z[:, 512:1024], op0=ALU.add, op1=ALU.mult,
              reads=[gB, bA[2], gA], writes=[gA])
            stv = next_stat()
            O("dve", "bn_stats", stv.ap[:, 0:6], vp, reads=[gA], writes=[stv])
            O("dve", "bn_aggr", stv.ap[:, 8:10], stv.ap[:, 0:6], reads=[stv], writes=[stv])
            rsqrt_small(stv, 9, 10, 1.0, 4 * EPS, mode="dve")
            O("dve", "tensor_scalar", vp, vp, stv.ap[:, 8:9], stv.ap[:, 9:10], op0=ALU.subtract, op1=ALU.mult,
              reads=[gA, stv], writes=[gA])
            O("dve", "tensor_tensor", vp, vp, gsgB.ap, op=ALU.mult, reads=[gA, gsgB], writes=[gA])
            yield
            O("act", "copy", QTc.ap[0:64, 0:8:2, c0:c0 + 128], bT.ap[0:64, 0:512].rearrange("p (j c) -> p j c", j=4),
              reads=[bT], writes=[QTc])
            O("act", "copy", QTc.ap[64:128, 1:8:2, c0:c0 + 128], bT.ap[64:128, 0:512].rearrange("p (j c) -> p j c", j=4),
              reads=[bT], writes=[QTc])
            for _ in range(4):
                yield
            O("pool", "tensor_tensor", vln.ap, vp, bsgB.ap, op=ALU.add, reads=[gA, bsgB], writes=[vln])
            yield
            yield
            for h_ in range(8):
                O("pe", "matmul", bA[0].ap[:, h_ * 64:(h_ + 1) * 64], lhsT=wsT.ap[:, h_, :], rhs=vln.ap[:, h_ * 64:(h_ + 1) * 64],
                  start=True, stop=True, reads=[wsT, vln], writes=[bA[0]])
            yield
            yield
            O("dve", "tensor_tensor", gB.ap[:, 0:512], bA[0].ap, bsB.ap.rearrange("p h f -> p (h f)"), op=ALU.add,
              reads=[bA[0], bsB], writes=[gB])
            yield
            yield
            O("pool", "tensor_tensor", mlpb.ap, gB.ap[:, 0:512], ub.ap, op=ALU.mult, reads=[gB, ub], writes=[mlpb])
            yield
            yield
            for j in range(4):
                O("pe", "transpose", bT.ap[:, 512 + j * 128:512 + (j + 1) * 128], mlpb.ap[:, j * 128:(j + 1) * 128], idb.ap,
                  reads=[mlpb, idb], writes=[bT])
            yield
            yield
            O("act", "copy", mixTc.ap[:, 4:8, c0:c0 + 128], bT.ap[:, 512:1024].rearrange("p (j c) -> p j c", j=4),
              reads=[bT], writes=[mixTc])

        def B2_head(slot, hi, NTk, nk, filler=None):
            QTc, mixTc = QTs[slot], mixTs[slot]
            j, hh = hi // 2, hi % 2
            ob = o_banks[hi % 2]
            rdb = rd[hi % 2]
            r0 = hh * 64

            def qk(kt):
                sb_ = sc_banks[kt % 2]
                O("pe", "matmul", sb_.ap[:, 0:NTk], lhsT=KT.ap[:, kt * 128:(kt + 1) * 128],
                  rhs=QTc.ap[:, 2 * j + hh, 0:NTk], start=True, stop=True, reads=[KT, QTc], writes=[sb_])

            def ex(kt):
                sb_ = sc_banks[kt % 2]
                pt = PT[kt % 4]
                O("act", "activation", out=pt.ap[:, 0:NTk], in_=sb_.ap[:, 0:NTk], func=AF.Exp, scale=0.125,
                  reads=[sb_], writes=[pt])

            def pv(kt):
                pt = PT[kt % 4]
                O("pe", "matmul", ob.ap[:, 0:NTk], lhsT=Vaug.ap[:, kt, hh, :], rhs=pt.ap[:, 0:NTk],
                  start=(kt == 0), stop=(kt == nk - 1), reads=[Vaug, pt], writes=[ob])

            for kt in range(min(LA, nk)):
                qk(kt)
            for kt in range(nk):
                ex(kt)
                if kt + LA < nk:
                    qk(kt + LA)
                pv(kt)
                if filler is not None:
                    next(filler, None)
            d0 = 64 - r0
            O("dve", "reciprocal", rdb.ap[r0:r0 + 64, 0:NTk], ob.ap[d0:d0 + 64, 0:NTk], reads=[ob], writes=[rdb])
            O("dve", "tensor_tensor", mixTc.ap[r0:r0 + 64, j, 0:NTk], ob.ap[r0:r0 + 64, 0:NTk], rdb.ap[r0:r0 + 64, 0:NTk],
              op=ALU.mult, reads=[ob, rdb], writes=[mixTc])

        def B3_tile(slot, isctx, t, i):
            M = modc if isctx else modx
            mixTc = mixTs[slot]
            c0 = i * 128
            pw = [bA[1], bA[2]]
            pw_ap = psA[:, 512:1536]
            sap, sres = tile_src(l, t)
            load_tile(xr, sap, sres)
            for half in range(2):
                for c in range(8):
                    O("pe", "matmul", pw[half].ap, lhsT=mixTc.ap[:, c, c0:c0 + 128], rhs=w_out_sb.ap[:, c, half * 512:(half + 1) * 512],
                      start=(c == 0), stop=(c == 7), reads=[mixTc, w_out_sb], writes=[pw[half]])
                yield
                yield
            yield
            st = next_stat()
            O("act", "activation", out=sq_junk.ap, in_=pw_ap, func=AF.Square, accum_out=st.ap[:, 0:1],
              reads=pw, writes=[sq_junk, st])
            yield
            yield
            yield
            rsqrt_small(st, 0, 1, 1.0 / D, EPS, mode="dve")
            o = next_xa()
            O("dve", "scalar_tensor_tensor", out=o.ap, in0=pw_ap, scalar=st.ap[:, 0:1], in1=M[2].ap, op0=ALU.mult, op1=ALU.mult,
              reads=pw + [st, M[2]], writes=[o])
            for _ in range(6):
                yield
            O("pool", "tensor_tensor", o.ap, o.ap, xr.ap, op=ALU.add, reads=[o, xr], writes=[o])
            for _ in range(4):
                yield
            DM("sp", o, xs[t * 128:(t + 1) * 128, :], o.ap, reads=[o], writes=[xs_res[t]])

        def drain(gen):
            for _ in gen:
                pass

        if not last:
            for i, t in enumerate([0, 1]):
                drain(B1_tile(1, True, t, i))
            for hi in range(8):
                B2_head(1, hi, 256, 2)
            for i, t in enumerate([0, 1]):
                drain(B3_tile(1, True, t, i))
        G = [[2 + 4 * g + i for i in range(4)] for g in range(8)]
        for i, t in enumerate(G[0]):
            drain(B1_tile(0, False, t, i))
        def group_filler(g):
            if g >= 1:
                for i_, t_ in enumerate(G[g - 1]):
                    yield from B3_tile((g - 1) % 2, False, t_, i_)
            if g + 1 < 8:
                for i_, t_ in enumerate(G[g + 1]):
                    yield from B1_tile((g + 1) % 2, False, t_, i_)

        for g in range(8):
            filler = group_filler(g)
            for hi in range(8):
                B2_head(g % 2, hi, 512, NT_TILES, filler)
            drain(filler)
        for i, t in enumerate(G[7]):
            drain(B3_tile(7 % 2, False, t, i))

        if stop <= 4:
            return finish()
        p.barrier(new_epoch=False)
        arena.reset()
        wgu = [arena.take([128, 8, 256], BF16, "wgu%d" % j) for j in range(NJ)]
        wo = [arena.take([128, D], BF16, "wo%d" % j) for j in range(NJ)]
        actT = arena.take([128, NJ, 512], BF16, "actT")
        sgt = [arena.take([128, 512], F32, "sgt%d" % i) for i in range(2)]
        for j in range(NJ):
            DM("pool", wgu[j], wgu[j].ap, w_gu[l, j].rearrange("p (k n) -> p k n", k=8), writes=[wgu[j]])
        for j in range(NJ):
            DM("pool", wo[j], wo[j].ap, w_o[l][:, j * D:(j + 1) * D], writes=[wo[j]])
        cgroups = []
        if not last:
            cgroups.append((True, [0, 1]))
        for g in range(8):
            cgroups.append((False, [2 + 4 * g + i for i in range(4)]))
        cur_stream = [None]
        hTs = [hT, hT2]

        def c_prenorm(gi):
            isctx, tiles = cgroups[gi]
            r = 1 if isctx else 0
            if cur_stream[0] != r:
                load_mods(modx, l, r, 3, g_pre_ffn, g_post_ffn)
                cur_stream[0] = r
            hs = []
            for i, t in enumerate(tiles):
                xb = next_xa()
                load_tile(xb, xs[t * 128:(t + 1) * 128, :], xs_res[t])
                hs.append(prenorm_front(xb, modx[0], modx[1]))
                if i >= 1:
                    prenorm_back(hs[i - 1], (i - 1) * 128, hT=hTs[gi % 2])
            prenorm_back(hs[-1], (len(tiles) - 1) * 128, hT=hTs[gi % 2])

        done_pre = set()
        for gi, (isctx, tiles) in enumerate(cgroups):
            if gi not in done_pre:
                c_prenorm(gi)
                done_pre.add(gi)
            hTc = hTs[gi % 2]
            NTk = 128 * len(tiles)
            gub = [(bA[0], bA[1]), (bB[0], bB[1])]
            for j in range(NJ):
                pg, pu = gub[j % 2]
                for k in range(8):
                    O("pe", "matmul", pg.ap[:, 0:NTk], lhsT=wgu[j].ap[:, k, 0:128], rhs=hTc.ap[:, k, 0:NTk],
                      start=(k == 0), stop=(k == 7), reads=[wgu[j], hTc], writes=[pg])
                for k in range(8):
                    O("pe", "matmul", pu.ap[:, 0:NTk], lhsT=wgu[j].ap[:, k, 128:256], rhs=hTc.ap[:, k, 0:NTk],
                      start=(k == 0), stop=(k == 7), reads=[wgu[j], hTc], writes=[pu])
                sg_ = sgt[j % 2]
                O("act", "activation", out=sg_.ap[:, 0:NTk], in_=pg.ap[:, 0:NTk], func=AF.Silu, reads=[pg], writes=[sg_])
                O("dve", "tensor_tensor", actT.ap[:, j, 0:NTk], sg_.ap[:, 0:NTk], pu.ap[:, 0:NTk], op=ALU.mult,
                  reads=[sg_, pu], writes=[actT])
            if gi + 1 < len(cgroups) and cgroups[gi + 1][0] == isctx:
                c_prenorm(gi + 1)
                done_pre.add(gi + 1)
            for i, t in enumerate(tiles):
                c0 = i * 128
                pw = [bA[2], bA[3]]
                for half in range(2):
                    for j in range(NJ):
                        O("pe", "matmul", pw[half].ap, lhsT=actT.ap[:, j, c0:c0 + 128], rhs=wo[j].ap[:, half * 512:(half + 1) * 512],
                          start=(j == 0), stop=(j == NJ - 1), reads=[actT, wo[j]], writes=[pw[half]])
                if last and not isctx:
                    dst_ap, dst_res = y_out[(t - 2) * 128:(t - 1) * 128, :], y_res[t - 2]
                else:
                    dst_ap, dst_res = xs[t * 128:(t + 1) * 128, :], xs_res[t]
                postnorm_store(pw, psA[:, 1024:2048], modx[2], xs[t * 128:(t + 1) * 128, :], xs_res[t], dst_ap, dst_res)
        p.barrier()

    return finish()


def _rope_tables():
    n = SEQ
    rows = n // 64
    pos_row = np.repeat(np.arange(rows, dtype=np.float32), 64)
    pos_col = np.tile(np.arange(64, dtype=np.float32), rows)
    inv = (10000.0 ** (-np.arange(0, 32, 2, dtype=np.float32) / 32)).astype(np.float32)
    ang = np.concatenate([pos_row[:, None] * inv, pos_col[:, None] * inv], axis=-1).astype(np.float32)
    cos = np.cos(ang).astype(np.float32)
    sin = np.sin(ang).astype(np.float32)
    cos = np.ascontiguousarray(cos.reshape(32, 128, 32).transpose(1, 0, 2)).reshape(128, 32 * 32)
    sin = np.ascontiguousarray(sin.reshape(32, 128, 32).transpose(1, 0, 2)).reshape(128, 32 * 32)
    return cos, sin


def prep_shared(inputs, L):
    f = lambda a: np.ascontiguousarray(np.asarray(a, dtype=np.float32))
    sh = {}
    wm = f(inputs["w_mod"])[:L]
    sh["w_mod"] = f(wm.reshape(L, 8, 128, 12, 512).transpose(0, 3, 2, 1, 4)).reshape(L, 12, 128, 8 * 512)
    sh["b_mod"] = f(inputs["b_mod"])[:L]
    for k in ("g_pre_mix", "g_post_mix", "g_pre_ffn", "g_post_ffn", "g_q", "g_k", "g_sg", "b_sg"):
        sh[k] = f(inputs[k])[:L]
    wi = f(inputs["w_in"])[:L]
    qcols = np.concatenate([np.arange(h * 64, (h + 1) * 64) for h in QPERM])
    cols = np.concatenate([qcols, np.arange(512, INW)])
    wi = wi[:, :, cols]
    sh["w_in"] = f(wi.reshape(L, 8, 128, INW).transpose(0, 2, 1, 3)).reshape(L, 128, 8 * INW)
    ws = f(inputs["w_s"])[:L]
    sh["w_s"] = f(ws.transpose(0, 3, 1, 2)).reshape(L, 128, 8 * 128)
    sh["b_s"] = f(f(inputs["b_s"])[:L].transpose(0, 2, 1))
    wo_ = f(inputs["w_out"])[:L]
    rows = np.concatenate([qcols, np.arange(512, 1024)])
    wo_ = wo_[:, rows, :]
    sh["w_out"] = f(wo_.reshape(L, 8, 128, D).transpose(0, 2, 1, 3)).reshape(L, 128, 8 * D)
    wf = f(inputs["w_ffn_in"])[:L]
    gate = wf[:, :, :HID].reshape(L, 8, 128, NJ, 128)
    up = wf[:, :, HID:].reshape(L, 8, 128, NJ, 128)
    gu = np.concatenate([gate, up], axis=-1)
    sh["w_gu"] = f(gu.transpose(0, 3, 2, 1, 4)).reshape(L, NJ, 128, 8 * 256)
    wfo = f(inputs["w_ffn_out"])[:L]
    sh["w_o"] = f(wfo.reshape(L, NJ, 128, D).transpose(0, 2, 1, 3)).reshape(L, 128, NJ * D)
    sh["ident"] = np.eye(128, dtype=np.float32)
    cos, sin = _rope_tables()
    sh["rope_cos"] = cos
    sh["rope_sin"] = sin
    sh["c_ctx"] = f(f(inputs["c_ctx"]).reshape(8, 128).T)
    return sh


def prep_core(inputs, b):
    f = lambda a: np.ascontiguousarray(np.asarray(a, dtype=np.float32))
    return {
        "x": f(inputs["x"][b]),
        "ctx": f(inputs["ctx"][b]),
        "c": f(f(inputs["c"][b]).reshape(8, 128).T),
    }


_NC_CACHE = {}


def run(inputs, n_layers=DEPTH, cores=None, stop=99):
    if cores is None:
        cores = list(range(8))
    if (n_layers, stop) not in _NC_CACHE:
        _NC_CACHE[(n_layers, stop)] = build(n_layers, stop)
    nc = _NC_CACHE[(n_layers, stop)]
    sh = prep_shared(inputs, n_layers)
    in_maps = []
    for b in cores:
        m = dict(sh)
        m.update(prep_core(inputs, b))
        in_maps.append(m)
    res = run_bass_kernel_spmd(nc, in_maps, core_ids=list(range(len(cores))))
    return np.stack([np.asarray(r["y"], dtype=np.float32) for r in res.results], axis=0)


def kernel(**inputs):
    return run(inputs, DEPTH)
```

```python
import os
import numpy as np
import concourse.bass as bass
import concourse.mybir as mybir
from concourse.bass_utils import run_bass_kernel_spmd

F32 = mybir.dt.float32
BF16 = mybir.dt.bfloat16
I32 = mybir.dt.int32
AF = mybir.ActivationFunctionType
ALU = mybir.AluOpType
AX = mybir.AxisListType

D = 1024
SEQ = 4096
CTX = 256
NTOK = SEQ + CTX
NT_TILES = NTOK // 128
DEPTH = 4
HID = 2816
NJ = HID // 128
INW = 1792
EPS = 1e-6
QPERM = [0, 4, 1, 5, 2, 6, 3, 7]


class Res:
    __slots__ = ("w", "r", "name", "sem", "excl")

    def __init__(self, name=""):
        self.w = None
        self.r = []
        self.name = name
        self.sem = None
        self.excl = False


class Buf(Res):
    __slots__ = ("ap",)

    def __init__(self, ap, name=""):
        Res.__init__(self, name)
        self.ap = ap


class Prog:
    ENGS = ("pe", "act", "dve", "pool", "sp")

    def __init__(self, nc):
        self.nc = nc
        self.streams = {e: [] for e in self.ENGS}
        self.sems = {}
        self.cnt = {}
        self.cur = {}
        self.epoch = 0
        self.dma_keys = {}
        self._new_epoch()
        self.waited = {e: {} for e in self.ENGS}
        self.ndma = 0
        self.nops = 0

    def _new_epoch(self):
        for e in self.ENGS:
            key = "%s#%d" % (e, self.epoch)
            self.sems[key] = self.nc.alloc_semaphore("s_%s_%d" % (e, self.epoch))
            self.cnt[key] = 0
            self.cur[e] = key
        self.epoch += 1

    def _need(self, eng, deps):
        best = {}
        for (k, v) in deps:
            if v > best.get(k, 0):
                best[k] = v
        out = []
        w = self.waited[eng]
        for k, v in best.items():
            if eng == "pe" and k.startswith("pe#"):
                continue
            if w.get(k, 0) < v:
                w[k] = v
                out.append((k, v))
        return out

    @staticmethod
    def _collect(reads, writes):
        deps = []
        for r in reads:
            if r.w is not None:
                deps.append(r.w)
        for wr in writes:
            if wr.w is not None:
                deps.append(wr.w)
            deps.extend(wr.r)
        return deps

    def _mark(self, me, reads, writes):
        for r in reads:
            r.r.append(me)
            if len(r.r) > 64:
                best = {}
                for (k, v) in r.r:
                    if v > best.get(k, 0):
                        best[k] = v
                r.r = list(best.items())
        for wr in writes:
            wr.w = me
            wr.r = []

    def op(self, eng, method, *args, reads=(), writes=(), **kw):
        ex = [r for r in reads if r.excl and r not in writes]
        if ex:
            writes = list(writes) + ex
        waits = self._need(eng, self._collect(reads, writes))
        key = self.cur[eng]
        self.cnt[key] += 1
        me = (key, self.cnt[key])
        self.streams[eng].append((waits, (method, args, kw), (key, 1)))
        self._mark(me, reads, writes)
        self.nops += 1
        return me

    def dma_sem_for(self, res):
        key = "dma:" + res.name
        if key not in self.sems:
            self.ndma += 1
            self.sems[key] = self.nc.alloc_semaphore("d_%d" % self.ndma)
            self.cnt[key] = 0
        return key

    def dma(self, queue, semres, out, in_, reads=(), writes=()):
        semkey = self.dma_sem_for(semres)
        waits = self._need(queue, self._collect(reads, writes))
        self.cnt[semkey] += 16
        me = (semkey, self.cnt[semkey])
        self.streams[queue].append((waits, ("dma_start", (), dict(out=out, in_=in_)), (semkey, 16)))
        self._mark(me, reads, writes)
        self.nops += 1
        return me

    def barrier(self, new_epoch=True):
        allv = [(k, v) for k, v in self.cnt.items() if v > 0]
        for e in self.ENGS:
            waits = self._need(e, allv)
            if waits:
                self.streams[e].append((waits, None, None))
        if new_epoch:
            self._new_epoch()

    def emit(self):
        nc = self.nc
        sems = self.sems
        streams = self.streams

        waited = {}
        for ename in self.ENGS:
            for (waits, fn, inc) in streams[ename]:
                for (k, v) in waits:
                    waited.setdefault(k, set()).add(v)
        remap = {}
        for k, vals in waited.items():
            if k.startswith("dma:"):
                continue
            remap[k] = {v: i + 1 for i, v in enumerate(sorted(vals))}
        seen = {}

        def run(e, ename):
            for (waits, fn, inc) in streams[ename]:
                for (k, v) in waits:
                    if k in remap:
                        e.wait_ge(sems[k], remap[k][v])
                    else:
                        e.wait_ge(sems[k], v)
                if fn is not None:
                    ins = getattr(e, fn[0])(*fn[1], **fn[2])
                    k = inc[0]
                    if k.startswith("dma:"):
                        ins.then_inc(sems[k], inc[1])
                    else:
                        seen[k] = seen.get(k, 0) + 1
                        if seen[k] in remap.get(k, ()):
                            ins.then_inc(sems[k], 1)

        with nc.Block() as block:
            @block.tensor
            def _(e):
                run(e, "pe")

            @block.scalar
            def _(e):
                run(e, "act")

            @block.vector
            def _(e):
                run(e, "dve")

            @block.gpsimd
            def _(e):
                run(e, "pool")

            @block.sync
            def _(e):
                run(e, "sp")


class Arena:
    def __init__(self, ap_bf16):
        self.ap = ap_bf16
        self.off = 0
        self.size = ap_bf16.shape[1]

    def reset(self):
        self.off = 0

    def take(self, shape, dtype, name=""):
        n = 1
        for s in shape[1:]:
            n *= s
        nel = n * (2 if dtype == F32 else 1)
        if self.off % 2:
            self.off += 1
        assert self.off + nel <= self.size, ("arena overflow", name, self.off, nel, self.size)
        v = self.ap[:, self.off:self.off + nel]
        self.off += nel
        if dtype == F32:
            v = v.bitcast(F32)
        if len(shape) == 3:
            v = v.rearrange("p (a b) -> p a b", a=shape[1])
        elif len(shape) == 4:
            v = v.rearrange("p (a b c) -> p a b c", a=shape[1], b=shape[2])
        return Buf(v, name)


def build(n_layers=DEPTH, stop=99):
    nc = bass.Bass("TRN2", target_bir_lowering=False)
    p = Prog(nc)
    O = p.op
    DM = p.dma
    L = n_layers

    def din(name, shape):
        return nc.dram_tensor(name, list(shape), F32, kind="ExternalInput").ap()

    x_in = din("x", [SEQ, D])
    ctx_in = din("ctx", [CTX, D])
    c_in = din("c", [128, 8])
    cc_in = din("c_ctx", [128, 8])
    w_mod = din("w_mod", [L, 12, 128, 8 * 512])
    b_mod = din("b_mod", [L, 6 * D])
    g_pre_mix = din("g_pre_mix", [L, D])
    g_post_mix = din("g_post_mix", [L, D])
    g_pre_ffn = din("g_pre_ffn", [L, D])
    g_post_ffn = din("g_post_ffn", [L, D])
    w_in = din("w_in", [L, 128, 8 * INW])
    g_q = din("g_q", [L, 64])
    g_k = din("g_k", [L, 64])
    g_sg = din("g_sg", [L, 512])
    b_sg = din("b_sg", [L, 512])
    w_s = din("w_s", [L, 128, 8 * 128])
    b_s = din("b_s", [L, 128, 8])
    w_out = din("w_out", [L, 128, 8 * D])
    w_gu = din("w_gu", [L, NJ, 128, 8 * 256])
    w_o = din("w_o", [L, 128, NJ * D])
    ident_in = din("ident", [128, 128])
    cos_in = din("rope_cos", [128, 32 * 32])
    sin_in = din("rope_sin", [128, 32 * 32])
    y_out = nc.dram_tensor("y", [SEQ, D], F32, kind="ExternalOutput").ap()
    xs = nc.dram_tensor("xs", [NTOK, D], F32, kind="Internal").ap()
    mod_d = nc.dram_tensor("mod_d", [L * 2, 6 * D], F32, kind="Internal").ap()

    xs_res = [Res("xs%d" % t) for t in range(NT_TILES)]
    y_res = [Res("y%d" % t) for t in range(32)]
    modd_res = Res("mod_d")

    def sb(name, shape, dtype):
        return Buf(nc.alloc_sbuf_tensor(name, list(shape), dtype).ap(), name)

    idb = sb("idb", [128, 128], BF16)
    xa = [sb("xa%d" % i, [128, D], F32) for i in range(2)]
    xr = sb("xr0", [128, D], F32)
    sq_junk = sb("sq_junk", [128, D], BF16)
    hb = [sb("hb%d" % i, [128, D], BF16) for i in range(2)]
    hT = sb("hT", [128, 8, 512], BF16)
    hT2 = sb("hT2", [128, 8, 512], BF16)
    modx = [sb("modx%d" % i, [128, D], F32) for i in range(3)]
    stat = [sb("stat%d" % i, [128, 16], F32) for i in range(4)]
    rs_y = [sb("rsy%d" % i, [128, 16], F32) for i in range(4)]
    rs_t = [sb("rst%d" % i, [128, 16], F32) for i in range(4)]
    one_i = sb("one_i", [128, 1], I32)
    magic_i = sb("magic_i", [128, 16], I32)
    arena = Arena(nc.alloc_sbuf_tensor("arena", [128, 79 * 1024], BF16).ap())

    psT = nc.alloc_psum_tensor("psT", [128, 1024], BF16).ap()
    psA = nc.alloc_psum_tensor("psA", [128, 2048], F32).ap()
    psBs = [nc.alloc_psum_tensor("psB%d" % i, [128, 512], F32).ap() for i in range(3)]
    bT = Buf(psT, "bT")
    bA = [Buf(psA[:, i * 512:(i + 1) * 512], "bA%d" % i) for i in range(4)]
    bB = [Buf(psBs[i], "bB%d" % i) for i in range(3)]
    bC = bA[3]
    for b_ in [bT] + bA + bB:
        b_.excl = True

    cnt = {"xa": 0, "hb": 0, "stat": 0}

    def next_xa():
        b = xa[cnt["xa"] % 2]
        cnt["xa"] += 1
        return b

    def next_stat():
        b = stat[cnt["stat"] % 4]
        cnt["stat"] += 1
        return b

    def load_tile(dst, src_ap, src_res):
        DM("sp", dst, dst.ap, src_ap, reads=[src_res] if src_res else [], writes=[dst])

    def rsqrt_small(st, lo, hi, scale, eps, mode="act"):
        n = hi - lo
        v = st.ap[:, lo:hi]
        O("dve", "tensor_scalar", v, v, scale, eps, op0=ALU.mult, op1=ALU.add, reads=[st], writes=[st])
        if mode == "act":
            O("act", "activation", out=v, in_=v, func=AF.Sqrt, reads=[st], writes=[st])
            O("dve", "reciprocal", v, v, reads=[st], writes=[st])
            return
        k = stat.index(st)
        yb, tb = rs_y[k], rs_t[k]
        y = yb.ap[:, 0:n]
        t1 = tb.ap[:, 0:n]
        yi = y.bitcast(I32)
        O("dve", "tensor_scalar", yi, v.bitcast(I32), one_i.ap[:, 0:1], None, op0=ALU.arith_shift_right,
          reads=[st, one_i], writes=[yb])
        O("dve", "tensor_tensor", yi, magic_i.ap[:, 0:n], yi, op=ALU.subtract, reads=[yb, magic_i], writes=[yb])
        NIT = 3
        for it_ in range(NIT):
            O("dve", "tensor_tensor", t1, v, y, op=ALU.mult, reads=[st, yb], writes=[tb])
            O("dve", "tensor_tensor", t1, t1, y, op=ALU.mult, reads=[tb, yb], writes=[tb])
            O("dve", "tensor_scalar", t1, t1, -0.5, 1.5, op0=ALU.mult, op1=ALU.add, reads=[tb], writes=[tb])
            if it_ < NIT - 1:
                O("dve", "tensor_tensor", y, y, t1, op=ALU.mult, reads=[yb, tb], writes=[yb])
            else:
                O("dve", "tensor_tensor", v, y, t1, op=ALU.mult, reads=[yb, tb], writes=[st])

    def prenorm_front(xbuf, G1, SH):
        st = next_stat()
        h = hb[cnt["hb"] % 2]
        cnt["hb"] += 1
        O("act", "activation", out=sq_junk.ap, in_=xbuf.ap, func=AF.Square, accum_out=st.ap[:, 0:1],
          reads=[xbuf], writes=[sq_junk, st])
        rsqrt_small(st, 0, 1, 1.0 / D, EPS)
        O("dve", "scalar_tensor_tensor", out=xbuf.ap, in0=xbuf.ap, scalar=st.ap[:, 0:1], in1=G1.ap,
          op0=ALU.mult, op1=ALU.mult, reads=[xbuf, st, G1], writes=[xbuf])
        O("dve", "tensor_tensor", h.ap, xbuf.ap, SH.ap, op=ALU.add, reads=[xbuf, SH], writes=[h])
        return h

    def prenorm_back(h, ncol0, hT=hT):
        for k in range(8):
            O("pe", "transpose", bT.ap[:, k * 128:(k + 1) * 128], h.ap[:, k * 128:(k + 1) * 128], idb.ap,
              reads=[h, idb], writes=[bT])
        O("act", "copy", hT.ap[:, :, ncol0:ncol0 + 128], bT.ap.rearrange("p (k c) -> p k c", k=8),
          reads=[bT], writes=[hT])

    def prenorm_to_hT(xbuf, G1, SH, ncol0, hT=hT):
        h = prenorm_front(xbuf, G1, SH)
        prenorm_back(h, ncol0, hT=hT)

    def load_mods(M, l, r, idx, gpre, gpost):
        row = l * 2 + r
        tmp = xr
        DM("sp", M[0], M[0].ap, mod_d[row, (idx + 1) * D:(idx + 2) * D].partition_broadcast(128), reads=[modd_res], writes=[M[0]])
        DM("sp", tmp, tmp.ap, gpre[l, :].partition_broadcast(128), writes=[tmp])
        O("dve", "scalar_tensor_tensor", out=M[0].ap, in0=M[0].ap, scalar=1.0, in1=tmp.ap, op0=ALU.add, op1=ALU.mult,
          reads=[M[0], tmp], writes=[M[0]])
        DM("sp", M[1], M[1].ap, mod_d[row, idx * D:(idx + 1) * D].partition_broadcast(128), reads=[modd_res], writes=[M[1]])
        DM("sp", M[2], M[2].ap, mod_d[row, (idx + 2) * D:(idx + 3) * D].partition_broadcast(128), reads=[modd_res], writes=[M[2]])
        DM("sp", tmp, tmp.ap, gpost[l, :].partition_broadcast(128), writes=[tmp])
        O("dve", "tensor_tensor", M[2].ap, M[2].ap, tmp.ap, op=ALU.mult, reads=[M[2], tmp], writes=[M[2]])

    def postnorm_store(pw_banks, pw_ap, GG, src_ap, src_res, dst_ap, dst_res):
        st = next_stat()
        xb = xr
        load_tile(xb, src_ap, src_res)
        O("act", "activation", out=sq_junk.ap, in_=pw_ap, func=AF.Square, accum_out=st.ap[:, 0:1],
          reads=pw_banks, writes=[sq_junk, st])
        rsqrt_small(st, 0, 1, 1.0 / D, EPS)
        o = next_xa()
        O("dve", "scalar_tensor_tensor", out=o.ap, in0=pw_ap, scalar=st.ap[:, 0:1], in1=GG.ap, op0=ALU.mult, op1=ALU.mult,
          reads=pw_banks + [st, GG], writes=[o])
        O("pool", "tensor_tensor", o.ap, o.ap, xb.ap, op=ALU.add, reads=[o, xb], writes=[o])
        DM("sp", o, dst_ap, o.ap, reads=[o], writes=[dst_res])

    def tile_src(l, t):
        if l == 0:
            if t < 2:
                return ctx_in[t * 128:(t + 1) * 128, :], None
            return x_in[(t - 2) * 128:(t - 1) * 128, :], None
        return xs[t * 128:(t + 1) * 128, :], xs_res[t]

    O("dve", "memset", one_i.ap, 1, writes=[one_i])
    O("dve", "memset", magic_i.ap, 0x5f3759df, writes=[magic_i])
    idf = xa[0]
    DM("sp", idf, idf.ap[:, 0:128], ident_in, writes=[idf])
    O("dve", "tensor_copy", idb.ap, idf.ap[:, 0:128], reads=[idf], writes=[idb])

    arena.reset()
    cst = arena.take([128, 8, 2], F32, "cst")
    craw = [arena.take([128, 8], F32, "craw%d" % i) for i in range(2)]
    ctmp = arena.take([128, 8], F32, "ctmp")
    NSTG = 4
    stage = [arena.take([128, 8, 512], F32, "stage%d" % i) for i in range(NSTG)]
    bmt = [arena.take([128, 512], F32, "bmt%d" % i) for i in range(2)]
    mrow = [arena.take([128, 512], F32, "mrow%d" % i) for i in range(2)]
    DM("sp", craw[0], craw[0].ap, c_in, writes=[craw[0]])
    DM("sp", craw[1], craw[1].ap, cc_in, writes=[craw[1]])
    for r in range(2):
        O("act", "activation", out=ctmp.ap, in_=craw[r].ap, func=AF.Tanh, scale=0.5, reads=[craw[r]], writes=[ctmp])
        O("dve", "scalar_tensor_tensor", out=ctmp.ap, in0=ctmp.ap, scalar=1.0, in1=craw[r].ap, op0=ALU.add, op1=ALU.mult,
          reads=[ctmp, craw[r]], writes=[ctmp])
        O("dve", "tensor_scalar", cst.ap[:, :, r], ctmp.ap, 0.5, None, op0=ALU.mult, reads=[ctmp], writes=[cst])
    it = 0
    for l in range(L):
        for j in range(12):
            sg = stage[it % NSTG]
            bm = bmt[it % 2]
            mr = mrow[it % 2]
            pb = bA[it % 2]
            DM("sp" if it % 2 == 0 else "act", sg, sg.ap, w_mod[l, j].rearrange("p (k n) -> p k n", k=8), writes=[sg])
            DM("pool", bm, bm.ap[0:2, :], b_mod[l, j * 512:(j + 1) * 512].partition_broadcast(2), writes=[bm])
            for k in range(8):
                O("pe", "matmul", pb.ap[0:2, :], lhsT=cst.ap[:, k, :], rhs=sg.ap[:, k, :], start=(k == 0), stop=(k == 7),
                  reads=[cst, sg], writes=[pb])
            O("dve", "tensor_tensor", mr.ap[0:2, :], pb.ap[0:2, :], bm.ap[0:2, :], op=ALU.add, reads=[pb, bm], writes=[mr])
            DM("pool", mr, mod_d[2 * l:2 * l + 2, j * 512:(j + 1) * 512], mr.ap[0:2, :], reads=[mr], writes=[modd_res])
            it += 1
    p.barrier()

    def finish():
        p.barrier(new_epoch=False)
        print("ops:", p.nops, "sems:", len(p.sems))
        p.emit()
        return nc

    if stop <= 1:
        return finish()

    for l in range(L):
        last = (l == L - 1)
        arena.reset()
        w_in_sb = arena.take([128, 8, INW], BF16, "w_in_sb")
        w_out_sb = arena.take([128, 8, D], BF16, "w_out_sb")
        wsT = arena.take([128, 8, 128], BF16, "wsT")
        bsB = arena.take([128, 8, 64], F32, "bsB")
        bsT = arena.take([128, 8], F32, "bsT")
        KT = arena.take([128, NTOK], BF16, "KT")
        Vaug = arena.take([128, NT_TILES, 2, 128], BF16, "Vaug")
        QT = arena.take([128, 8, 512], BF16, "QT")
        mixT = arena.take([128, 8, 512], BF16, "mixT")
        QT2 = arena.take([128, 8, 512], BF16, "QT2")
        mixT2 = arena.take([128, 8, 512], BF16, "mixT2")
        PT = [arena.take([128, 512], BF16, "PT%d" % i) for i in range(4)]
        rd = [arena.take([128, 512], F32, "rd%d" % i) for i in range(2)]
        gA = arena.take([128, 1024], F32, "gA")
        gB = arena.take([128, 1024], F32, "gB")
        ub = arena.take([128, 512], BF16, "ub")
        vln = arena.take([128, 512], BF16, "vln")
        mlpb = arena.take([128, 512], BF16, "mlpb")
        qf = arena.take([128, 512], F32, "qf")
        qt1 = arena.take([128, 256], F32, "qt1")
        qt2 = arena.take([128, 256], F32, "qt2")
        qb = arena.take([128, 512], BF16, "qb")
        kb = arena.take([128, 128], BF16, "kb")
        cosT = arena.take([128, 32, 32], F32, "cosT")
        sinT = arena.take([128, 32, 32], F32, "sinT")
        gqB = arena.take([128, 64], F32, "gqB")
        gkB = arena.take([128, 64], F32, "gkB")
        gsgB = arena.take([128, 512], F32, "gsgB")
        bsgB = arena.take([128, 512], F32, "bsgB")
        modc = [arena.take([128, D], F32, "modc%d" % i) for i in range(3)]

        DM("pool", w_in_sb, w_in_sb.ap, w_in[l].rearrange("p (k n) -> p k n", k=8), writes=[w_in_sb])
        DM("pool", w_out_sb, w_out_sb.ap, w_out[l].rearrange("p (k n) -> p k n", k=8), writes=[w_out_sb])
        DM("pool", wsT, wsT.ap, w_s[l].rearrange("p (h q) -> p h q", h=8), writes=[wsT])
        O("pool", "tensor_scalar", wsT.ap, wsT.ap, 0.5, None, op0=ALU.mult, reads=[wsT], writes=[wsT])
        if stop == 1.1:
            return finish()
        cgrp = Res("constgrp")
        DM("sp", cgrp, bsT.ap, b_s[l], writes=[bsT])
        DM("sp", cgrp, cosT.ap, cos_in.rearrange("p (a b) -> p a b", a=32), writes=[cosT])
        DM("sp", cgrp, sinT.ap, sin_in.rearrange("p (a b) -> p a b", a=32), writes=[sinT])
        DM("sp", cgrp, gqB.ap, g_q[l, :].partition_broadcast(128), writes=[gqB])
        DM("sp", cgrp, gkB.ap, g_k[l, :].partition_broadcast(128), writes=[gkB])
        DM("sp", cgrp, gsgB.ap, g_sg[l, :].partition_broadcast(128), writes=[gsgB])
        DM("sp", cgrp, bsgB.ap, b_sg[l, :].partition_broadcast(128), writes=[bsgB])
        O("dve", "memset", kb.ap, 0.0, writes=[kb, bsT, cosT, sinT, gqB, gkB, gsgB, bsgB])
        O("pool", "tensor_scalar", bsB.ap, bsT.ap.unsqueeze(2).to_broadcast([128, 8, 64]), 0.5, None, op0=ALU.mult,
          reads=[bsT], writes=[bsB])
        if stop == 1.2:
            return finish()
        for QT_ in (QT, QT2):
            O("pool", "memset", QT_.ap[64:128, 0:8:2, :], 0.0, writes=[QT_])
            O("pool", "memset", QT_.ap[0:64, 1:8:2, :], 0.0, writes=[QT_])
        O("pool", "memset", Vaug.ap[:, :, 0, 64:128], 1.0, writes=[Vaug])
        O("pool", "memset", Vaug.ap[:, :, 1, 0:64], 1.0, writes=[Vaug])
        if stop == 1.3:
            return finish()
        load_mods(modc, l, 1, 0, g_pre_mix, g_post_mix)
        if stop == 1.4:
            return finish()
        load_mods(modx, l, 0, 0, g_pre_mix, g_post_mix)

        if stop <= 2:
            return finish()

        def rope(src_ap, nh, lt, dst_ap, rbufs, wbufs):
            s5 = src_ap.rearrange("p (h a b f) -> p h a b f", h=nh, a=2, b=2)
            d5 = dst_ap.rearrange("p (h a b f) -> p h a b f", h=nh, a=2, b=2)
            x1 = s5[:, :, :, 0, :]
            x2 = s5[:, :, :, 1, :]
            cb = cosT.ap[:, lt, :].rearrange("p (a f) -> p a f", a=2).unsqueeze(1).to_broadcast([128, nh, 2, 16])
            sbb = sinT.ap[:, lt, :].rearrange("p (a f) -> p a f", a=2).unsqueeze(1).to_broadcast([128, nh, 2, 16])
            n = nh * 32
            a1 = qt1.ap[:, 0:n].rearrange("p (h a f) -> p h a f", h=nh, a=2)
            a2 = qt2.ap[:, 0:n].rearrange("p (h a f) -> p h a f", h=nh, a=2)
            O("dve", "tensor_tensor", a1, x1, cb, op=ALU.mult, reads=rbufs + [cosT], writes=[qt1])
            O("dve", "tensor_tensor", a2, x2, sbb, op=ALU.mult, reads=rbufs + [sinT], writes=[qt2])
            O("dve", "tensor_tensor", d5[:, :, :, 0, :], a1, a2, op=ALU.subtract, reads=[qt1, qt2], writes=wbufs)
            O("dve", "tensor_tensor", a1, x1, sbb, op=ALU.mult, reads=rbufs + [sinT, qt1], writes=[qt1])
            O("dve", "tensor_tensor", a2, x2, cb, op=ALU.mult, reads=rbufs + [cosT, qt2], writes=[qt2])
            O("dve", "tensor_tensor", d5[:, :, :, 1, :], a1, a2, op=ALU.add, reads=[qt1, qt2], writes=wbufs)

        def head_norm(src_ap, src_bufs, nh, gB_, dstf):
            st = next_stat()
            n = nh * 64
            d = dstf.ap[:, 0:n]
            d3 = d.rearrange("p (h f) -> p h f", h=nh)
            O("act", "activation", out=d, in_=src_ap, func=AF.Square, reads=src_bufs, writes=[dstf])
            O("dve", "tensor_reduce", st.ap[:, 0:nh], d3, axis=AX.X, op=ALU.add, reads=[dstf], writes=[st])
            rsqrt_small(st, 0, nh, 1.0 / 64, EPS)
            O("dve", "tensor_tensor", d3, src_ap.rearrange("p (h f) -> p h f", h=nh),
              st.ap[:, 0:nh].unsqueeze(2).to_broadcast([128, nh, 64]), op=ALU.mult,
              reads=src_bufs + [st, dstf], writes=[dstf])
            O("dve", "tensor_tensor", d3, d3, gB_.ap.unsqueeze(1).to_broadcast([128, nh, 64]), op=ALU.mult,
              reads=[dstf, gB_], writes=[dstf])

        kvbanks = [bA[3], bA[2]]

        def front_A(t):
            isctx = t < 2
            M = modc if isctx else modx
            xb = next_xa()
            sap, sres = tile_src(l, t)
            load_tile(xb, sap, sres)
            prenorm_to_hT(xb, M[0], M[1], 0)
            bk = kvbanks[t % 2]
            for k in range(8):
                O("pe", "matmul", bk.ap[:, 0:256], lhsT=hT.ap[:, k, 0:128], rhs=w_in_sb.ap[:, k, 512:768],
                  start=(k == 0), stop=(k == 7), reads=[hT, w_in_sb], writes=[bk])

        def back_A(t):
            isctx = t < 2
            bk = kvbanks[t % 2]
            head_norm(bk.ap[:, 0:128], [bk], 2, gkB, qf)
            if isctx:
                O("dve", "tensor_copy", kb.ap, qf.ap[:, 0:128], reads=[qf], writes=[kb])
            else:
                rope(qf.ap[:, 0:128], 2, t - 2, kb.ap, [qf], [kb])
            O("act", "copy", Vaug.ap[:, t, 0, 0:64], bk.ap[:, 128:192], reads=[bk], writes=[Vaug])
            O("act", "copy", Vaug.ap[:, t, 1, 64:128], bk.ap[:, 192:256], reads=[bk], writes=[Vaug])
            O("pe", "transpose", bB[0].ap.bitcast(BF16)[:, 0:128], kb.ap, idb.ap, reads=[kb, idb], writes=[bB[0]])
            O("act", "copy", KT.ap[:, t * 128:(t + 1) * 128], bB[0].ap.bitcast(BF16)[:, 0:128], reads=[bB[0]], writes=[KT])

        front_A(0)
        for t in range(NT_TILES):
            if t + 1 < NT_TILES:
                front_A(t + 1)
            back_A(t)
        if stop <= 3:
            return finish()
        hTs = [hT, hT2]
        QTs = [QT, QT2]
        mixTs = [mixT, mixT2]
        sc_banks = [bB[0], bB[1]]
        o_banks = [bB[2], bA[3]]
        LA = 1

        def B1_tile(slot, isctx, t, i):
            M = modc if isctx else modx
            hTc, QTc, mixTc = hTs[slot], QTs[slot], mixTs[slot]
            c0 = i * 128
            xb = next_xa()
            sap, sres = tile_src(l, t)
            load_tile(xb, sap, sres)
            yield
            st = next_stat()
            h = hb[cnt["hb"] % 2]
            cnt["hb"] += 1
            O("act", "activation", out=sq_junk.ap, in_=xb.ap, func=AF.Square, accum_out=st.ap[:, 0:1],
              reads=[xb], writes=[sq_junk, st])
            yield
            yield
            rsqrt_small(st, 0, 1, 1.0 / D, EPS, mode="dve")
            O("dve", "scalar_tensor_tensor", out=xb.ap, in0=xb.ap, scalar=st.ap[:, 0:1], in1=M[0].ap,
              op0=ALU.mult, op1=ALU.mult, reads=[xb, st, M[0]], writes=[xb])
            O("dve", "tensor_tensor", h.ap, xb.ap, M[1].ap, op=ALU.add, reads=[xb, M[1]], writes=[h])
            for _ in range(8):
                yield
            for k in range(8):
                O("pe", "transpose", bT.ap[:, k * 128:(k + 1) * 128], h.ap[:, k * 128:(k + 1) * 128], idb.ap,
                  reads=[h, idb], writes=[bT])
            yield
            yield
            O("act", "copy", hTc.ap[:, :, c0:c0 + 128], bT.ap.rearrange("p (k c) -> p k c", k=8), reads=[bT], writes=[hTc])
            yield
            yield
            for (bank, col0) in ((bA[0], 0), (bA[1], 768), (bA[2], 1280)):
                for k in range(8):
                    O("pe", "matmul", bank.ap, lhsT=hTc.ap[:, k, c0:c0 + 128], rhs=w_in_sb.ap[:, k, col0:col0 + 512],
                      start=(k == 0), stop=(k == 7), reads=[hTc, w_in_sb], writes=[bank])
                    if k == 3:
                        yield
                yield
            yield
            z = psA[:, 512:1536]
            zb = [bA[1], bA[2]]
            stq = next_stat()
            q3 = qf.ap.rearrange("p (h f) -> p h f", h=8)
            O("act", "activation", out=qf.ap, in_=bA[0].ap, func=AF.Square, reads=[bA[0]], writes=[qf])
            O("act", "activation", out=gA.ap, in_=z, func=AF.Square, reads=zb, writes=[gA])
            yield
            yield
            yield
            O("dve", "tensor_reduce", stq.ap[:, 0:8], q3, axis=AX.X, op=ALU.add, reads=[qf], writes=[stq])
            rsqrt_small(stq, 0, 8, 1.0 / 64, EPS, mode="dve")
            O("dve", "tensor_tensor", q3, bA[0].ap.rearrange("p (h f) -> p h f", h=8),
              stq.ap[:, 0:8].unsqueeze(2).to_broadcast([128, 8, 64]), op=ALU.mult, reads=[bA[0], stq, qf], writes=[qf])
            O("dve", "tensor_tensor", q3, q3, gqB.ap.unsqueeze(1).to_broadcast([128, 8, 64]), op=ALU.mult,
              reads=[qf, gqB], writes=[qf])
            O("dve", "tensor_scalar", gA.ap, gA.ap, 0.044715, 1.0, op0=ALU.mult, op1=ALU.add, reads=[gA], writes=[gA])
            O("dve", "tensor_tensor", gA.ap, gA.ap, z, op=ALU.mult, reads=[gA] + zb, writes=[gA])
            for _ in range(6):
                yield
            O("act", "activation", out=gB.ap, in_=gA.ap, func=AF.Tanh, scale=0.7978845608028654, reads=[gA], writes=[gB])
            if isctx:
                O("dve", "tensor_copy", qb.ap, qf.ap, reads=[qf], writes=[qb])
            else:
                rope(qf.ap, 8, t - 2, qb.ap, [qf], [qb])
            for _ in range(4):
                yield
            for j in range(4):
                O("pe", "transpose", bT.ap[:, j * 128:(j + 1) * 128], qb.ap[:, j * 128:(j + 1) * 128], idb.ap,
                  reads=[qb, idb], writes=[bT])
            vp = gA.ap[:, 512:1024]
            O("dve", "scalar_tensor_tensor", out=ub.ap, in0=gB.ap[:, 0:512], scalar=1.0, in1=z[:, 0:512], op0=ALU.add, op1=ALU.mult,
              reads=[gB, bA[1]], writes=[ub])
            O("dve", "scalar_tensor_tensor", out=vp, in0=gB.ap[:, 512:1024], scalar=1.0, in1=z[:, 512:1024], op0=ALU.add, op1=ALU.mult,
              reads=[gB, bA[2], gA], writes=[gA])
            stv = next_stat()
            O("dve", "bn_stats", stv.ap[:, 0:6], vp, reads=[gA], writes=[stv])
            O("dve", "bn_aggr", stv.ap[:, 8:10], stv.ap[:, 0:6], reads=[stv], writes=[stv])
            rsqrt_small(stv, 9, 10, 1.0, 4 * EPS, mode="dve")
            O("dve", "tensor_scalar", vp, vp, stv.ap[:, 8:9], stv.ap[:, 9:10], op0=ALU.subtract, op1=ALU.mult,
              reads=[gA, stv], writes=[gA])
            O("dve", "tensor_tensor", vp, vp, gsgB.ap, op=ALU.mult, reads=[gA, gsgB], writes=[gA])
            yield
            O("act", "copy", QTc.ap[0:64, 0:8:2, c0:c0 + 128], bT.ap[0:64, 0:512].rearrange("p (j c) -> p j c", j=4),
              reads=[bT], writes=[QTc])
            O("act", "copy", QTc.ap[64:128, 1:8:2, c0:c0 + 128], bT.ap[64:128, 0:512].rearrange("p (j c) -> p j c", j=4),
              reads=[bT], writes=[QTc])
            for _ in range(4):
                yield
            O("pool", "tensor_tensor", vln.ap, vp, bsgB.ap, op=ALU.add, reads=[gA, bsgB], writes=[vln])
            yield
            yield
            for h_ in range(8):
                O("pe", "matmul", bA[0].ap[:, h_ * 64:(h_ + 1) * 64], lhsT=wsT.ap[:, h_, :], rhs=vln.ap[:, h_ * 64:(h_ + 1) * 64],
                  start=True, stop=True, reads=[wsT, vln], writes=[bA[0]])
            yield
            yield
            O("dve", "tensor_tensor", gB.ap[:, 0:512], bA[0].ap, bsB.ap.rearrange("p h f -> p (h f)"), op=ALU.add,
              reads=[bA[0], bsB], writes=[gB])
            yield
            yield
            O("pool", "tensor_tensor", mlpb.ap, gB.ap[:, 0:512], ub.ap, op=ALU.mult, reads=[gB, ub], writes=[mlpb])
            yield
            yield
            for j in range(4):
                O("pe", "transpose", bT.ap[:, 512 + j * 128:512 + (j + 1) * 128], mlpb.ap[:, j * 128:(j + 1) * 128], idb.ap,
                  reads=[mlpb, idb], writes=[bT])
            yield
            yield
            O("act", "copy", mixTc.ap[:, 4:8, c0:c0 + 128], bT.ap[:, 512:1024].rearrange("p (j c) -> p j c", j=4),
              reads=[bT], writes=[mixTc])

        def B2_head(slot, hi, NTk, nk, filler=None):
            QTc, mixTc = QTs[slot], mixTs[slot]
            j, hh = hi // 2, hi % 2
            ob = o_banks[hi % 2]
            rdb = rd[hi % 2]
            r0 = hh * 64

            def qk(kt):
                sb_ = sc_banks[kt % 2]
                O("pe", "matmul", sb_.ap[:, 0:NTk], lhsT=KT.ap[:, kt * 128:(kt + 1) * 128],
                  rhs=QTc.ap[:, 2 * j + hh, 0:NTk], start=True, stop=True, reads=[KT, QTc], writes=[sb_])

            def ex(kt):
                sb_ = sc_banks[kt % 2]
                pt = PT[kt % 4]
                O("act", "activation", out=pt.ap[:, 0:NTk], in_=sb_.ap[:, 0:NTk], func=AF.Exp, scale=0.125,
                  reads=[sb_], writes=[pt])

            def pv(kt):
                pt = PT[kt % 4]
                O("pe", "matmul", ob.ap[:, 0:NTk], lhsT=Vaug.ap[:, kt, hh, :], rhs=pt.ap[:, 0:NTk],
                  start=(kt == 0), stop=(kt == nk - 1), reads=[Vaug, pt], writes=[ob])

            for kt in range(min(LA, nk)):
                qk(kt)
            for kt in range(nk):
                ex(kt)
                if kt + LA < nk:
                    qk(kt + LA)
                pv(kt)
                if filler is not None:
                    next(filler, None)
            d0 = 64 - r0
            O("dve", "reciprocal", rdb.ap[r0:r0 + 64, 0:NTk], ob.ap[d0:d0 + 64, 0:NTk], reads=[ob], writes=[rdb])
            O("dve", "tensor_tensor", mixTc.ap[r0:r0 + 64, j, 0:NTk], ob.ap[r0:r0 + 64, 0:NTk], rdb.ap[r0:r0 + 64, 0:NTk],
              op=ALU.mult, reads=[ob, rdb], writes=[mixTc])

        def B3_tile(slot, isctx, t, i):
            M = modc if isctx else modx
            mixTc = mixTs[slot]
            c0 = i * 128
            pw = [bA[1], bA[2]]
            pw_ap = psA[:, 512:1536]
            sap, sres = tile_src(l, t)
            load_tile(xr, sap, sres)
            for half in range(2):
                for c in range(8):
                    O("pe", "matmul", pw[half].ap, lhsT=mixTc.ap[:, c, c0:c0 + 128], rhs=w_out_sb.ap[:, c, half * 512:(half + 1) * 512],
                      start=(c == 0), stop=(c == 7), reads=[mixTc, w_out_sb], writes=[pw[half]])
                yield
                yield
            yield
            st = next_stat()
            O("act", "activation", out=sq_junk.ap, in_=pw_ap, func=AF.Square, accum_out=st.ap[:, 0:1],
              reads=pw, writes=[sq_junk, st])
            yield
            yield
            yield
            rsqrt_small(st, 0, 1, 1.0 / D, EPS, mode="dve")
            o = next_xa()
            O("dve", "scalar_tensor_tensor", out=o.ap, in0=pw_ap, scalar=st.ap[:, 0:1], in1=M[2].ap, op0=ALU.mult, op1=ALU.mult,
              reads=pw + [st, M[2]], writes=[o])
            for _ in range(6):
                yield
            O("pool", "tensor_tensor", o.ap, o.ap, xr.ap, op=ALU.add, reads=[o, xr], writes=[o])
            for _ in range(4):
                yield
            DM("sp", o, xs[t * 128:(t + 1) * 128, :], o.ap, reads=[o], writes=[xs_res[t]])

        def drain(gen):
            for _ in gen:
                pass

        if not last:
            for i, t in enumerate([0, 1]):
                drain(B1_tile(1, True, t, i))
            for hi in range(8):
                B2_head(1, hi, 256, 2)
            for i, t in enumerate([0, 1]):
                drain(B3_tile(1, True, t, i))
        G = [[2 + 4 * g + i for i in range(4)] for g in range(8)]
        for i, t in enumerate(G[0]):
            drain(B1_tile(0, False, t, i))
        def group_filler(g):
            if g >= 1:
                for i_, t_ in enumerate(G[g - 1]):
                    yield from B3_tile((g - 1) % 2, False, t_, i_)
            if g + 1 < 8:
                for i_, t_ in enumerate(G[g + 1]):
                    yield from B1_tile((g + 1) % 2, False, t_, i_)

        for g in range(8):
            filler = group_filler(g)
            for hi in range(8):
                B2_head(g % 2, hi, 512, NT_TILES, filler)
            drain(filler)
        for i, t in enumerate(G[7]):
            drain(B3_tile(7 % 2, False, t, i))

        if stop <= 4:
            return finish()
        p.barrier(new_epoch=False)
        arena.reset()
        wgu = [arena.take([128, 8, 256], BF16, "wgu%d" % j) for j in range(NJ)]
        wo = [arena.take([128, D], BF16, "wo%d" % j) for j in range(NJ)]
        actT = arena.take([128, NJ, 512], BF16, "actT")
        sgt = [arena.take([128, 512], F32, "sgt%d" % i) for i in range(2)]
        for j in range(NJ):
            DM("pool", wgu[j], wgu[j].ap, w_gu[l, j].rearrange("p (k n) -> p k n", k=8), writes=[wgu[j]])
        for j in range(NJ):
            DM("pool", wo[j], wo[j].ap, w_o[l][:, j * D:(j + 1) * D], writes=[wo[j]])
        cgroups = []
        if not last:
            cgroups.append((True, [0, 1]))
        for g in range(8):
            cgroups.append((False, [2 + 4 * g + i for i in range(4)]))
        cur_stream = [None]
        hTs = [hT, hT2]

        def c_prenorm(gi):
            isctx, tiles = cgroups[gi]
            r = 1 if isctx else 0
            if cur_stream[0] != r:
                load_mods(modx, l, r, 3, g_pre_ffn, g_post_ffn)
                cur_stream[0] = r
            hs = []
            for i, t in enumerate(tiles):
                xb = next_xa()
                load_tile(xb, xs[t * 128:(t + 1) * 128, :], xs_res[t])
                hs.append(prenorm_front(xb, modx[0], modx[1]))
                if i >= 1:
                    prenorm_back(hs[i - 1], (i - 1) * 128, hT=hTs[gi % 2])
            prenorm_back(hs[-1], (len(tiles) - 1) * 128, hT=hTs[gi % 2])

        done_pre = set()
        for gi, (isctx, tiles) in enumerate(cgroups):
            if gi not in done_pre:
                c_prenorm(gi)
                done_pre.add(gi)
            hTc = hTs[gi % 2]
            NTk = 128 * len(tiles)
            gub = [(bA[0], bA[1]), (bB[0], bB[1])]
            for j in range(NJ):
                pg, pu = gub[j % 2]
                for k in range(8):
                    O("pe", "matmul", pg.ap[:, 0:NTk], lhsT=wgu[j].ap[:, k, 0:128], rhs=hTc.ap[:, k, 0:NTk],
                      start=(k == 0), stop=(k == 7), reads=[wgu[j], hTc], writes=[pg])
                for k in range(8):
                    O("pe", "matmul", pu.ap[:, 0:NTk], lhsT=wgu[j].ap[:, k, 128:256], rhs=hTc.ap[:, k, 0:NTk],
                      start=(k == 0), stop=(k == 7), reads=[wgu[j], hTc], writes=[pu])
                sg_ = sgt[j % 2]
                O("act", "activation", out=sg_.ap[:, 0:NTk], in_=pg.ap[:, 0:NTk], func=AF.Silu, reads=[pg], writes=[sg_])
                O("dve", "tensor_tensor", actT.ap[:, j, 0:NTk], sg_.ap[:, 0:NTk], pu.ap[:, 0:NTk], op=ALU.mult,
                  reads=[sg_, pu], writes=[actT])
            if gi + 1 < len(cgroups) and cgroups[gi + 1][0] == isctx:
                c_prenorm(gi + 1)
                done_pre.add(gi + 1)
            for i, t in enumerate(tiles):
                c0 = i * 128
                pw = [bA[2], bA[3]]
                for half in range(2):
                    for j in range(NJ):
                        O("pe", "matmul", pw[half].ap, lhsT=actT.ap[:, j, c0:c0 + 128], rhs=wo[j].ap[:, half * 512:(half + 1) * 512],
                          start=(j == 0), stop=(j == NJ - 1), reads=[actT, wo[j]], writes=[pw[half]])
                if last and not isctx:
                    dst_ap, dst_res = y_out[(t - 2) * 128:(t - 1) * 128, :], y_res[t - 2]
                else:
                    dst_ap, dst_res = xs[t * 128:(t + 1) * 128, :], xs_res[t]
                postnorm_store(pw, psA[:, 1024:2048], modx[2], xs[t * 128:(t + 1) * 128, :], xs_res[t], dst_ap, dst_res)
        p.barrier()

    return finish()


def _rope_tables():
    n = SEQ
    rows = n // 64
    pos_row = np.repeat(np.arange(rows, dtype=np.float32), 64)
    pos_col = np.tile(np.arange(64, dtype=np.float32), rows)
    inv = (10000.0 ** (-np.arange(0, 32, 2, dtype=np.float32) / 32)).astype(np.float32)
    ang = np.concatenate([pos_row[:, None] * inv, pos_col[:, None] * inv], axis=-1).astype(np.float32)
    cos = np.cos(ang).astype(np.float32)
    sin = np.sin(ang).astype(np.float32)
    cos = np.ascontiguousarray(cos.reshape(32, 128, 32).transpose(1, 0, 2)).reshape(128, 32 * 32)
    sin = np.ascontiguousarray(sin.reshape(32, 128, 32).transpose(1, 0, 2)).reshape(128, 32 * 32)
    return cos, sin


def prep_shared(inputs, L):
    f = lambda a: np.ascontiguousarray(np.asarray(a, dtype=np.float32))
    sh = {}
    wm = f(inputs["w_mod"])[:L]
    sh["w_mod"] = f(wm.reshape(L, 8, 128, 12, 512).transpose(0, 3, 2, 1, 4)).reshape(L, 12, 128, 8 * 512)
    sh["b_mod"] = f(inputs["b_mod"])[:L]
    for k in ("g_pre_mix", "g_post_mix", "g_pre_ffn", "g_post_ffn", "g_q", "g_k", "g_sg", "b_sg"):
        sh[k] = f(inputs[k])[:L]
    wi = f(inputs["w_in"])[:L]
    qcols = np.concatenate([np.arange(h * 64, (h + 1) * 64) for h in QPERM])
    cols = np.concatenate([qcols, np.arange(512, INW)])
    wi = wi[:, :, cols]
    sh["w_in"] = f(wi.reshape(L, 8, 128, INW).transpose(0, 2, 1, 3)).reshape(L, 128, 8 * INW)
    ws = f(inputs["w_s"])[:L]
    sh["w_s"] = f(ws.transpose(0, 3, 1, 2)).reshape(L, 128, 8 * 128)
    sh["b_s"] = f(f(inputs["b_s"])[:L].transpose(0, 2, 1))
    wo_ = f(inputs["w_out"])[:L]
    rows = np.concatenate([qcols, np.arange(512, 1024)])
    wo_ = wo_[:, rows, :]
    sh["w_out"] = f(wo_.reshape(L, 8, 128, D).transpose(0, 2, 1, 3)).reshape(L, 128, 8 * D)
    wf = f(inputs["w_ffn_in"])[:L]
    gate = wf[:, :, :HID].reshape(L, 8, 128, NJ, 128)
    up = wf[:, :, HID:].reshape(L, 8, 128, NJ, 128)
    gu = np.concatenate([gate, up], axis=-1)
    sh["w_gu"] = f(gu.transpose(0, 3, 2, 1, 4)).reshape(L, NJ, 128, 8 * 256)
    wfo = f(inputs["w_ffn_out"])[:L]
    sh["w_o"] = f(wfo.reshape(L, NJ, 128, D).transpose(0, 2, 1, 3)).reshape(L, 128, NJ * D)
    sh["ident"] = np.eye(128, dtype=np.float32)
    cos, sin = _rope_tables()
    sh["rope_cos"] = cos
    sh["rope_sin"] = sin
    sh["c_ctx"] = f(f(inputs["c_ctx"]).reshape(8, 128).T)
    return sh


def prep_core(inputs, b):
    f = lambda a: np.ascontiguousarray(np.asarray(a, dtype=np.float32))
    return {
        "x": f(inputs["x"][b]),
        "ctx": f(inputs["ctx"][b]),
        "c": f(f(inputs["c"][b]).reshape(8, 128).T),
    }


_NC_CACHE = {}


def run(inputs, n_layers=DEPTH, cores=None, stop=99):
    if cores is None:
        cores = list(range(8))
    if (n_layers, stop) not in _NC_CACHE:
        _NC_CACHE[(n_layers, stop)] = build(n_layers, stop)
    nc = _NC_CACHE[(n_layers, stop)]
    sh = prep_shared(inputs, n_layers)
    in_maps = []
    for b in cores:
        m = dict(sh)
        m.update(prep_core(inputs, b))
        in_maps.append(m)
    res = run_bass_kernel_spmd(nc, in_maps, core_ids=list(range(len(cores))))
    return np.stack([np.asarray(r["y"], dtype=np.float32) for r in res.results], axis=0)


def kernel(**inputs):
    return run(inputs, DEPTH)
```
